# Optimizing a Trainium2 kernel written in Bass

```python
import math
import jax
import jax.numpy as jnp
from jax import lax
import numpy as np

D_MODEL = 1024
BATCH = 8
SEQ = 4096
DEPTH = 4

N_MIXERS = 4
PLE_DIM = 256
D_FF = 2816
NORM_EPS = 1e-6
CHUNK = 64
MAX_POS_OFFSET = 1024

GLA_HEADS = 4
GLA_DK = D_MODEL // (2 * GLA_HEADS)
GLA_DV = D_MODEL // GLA_HEADS
GLA_GATE_RANK = 16
GLA_TAU = 16.0
GLA_IN = 2 * GLA_HEADS * GLA_DK + 2 * GLA_HEADS * GLA_DV + GLA_GATE_RANK

RET_HEADS = 4
RET_DK = D_MODEL // RET_HEADS
RET_DV = 2 * D_MODEL // RET_HEADS
RET_IN = 2 * RET_HEADS * RET_DK + 2 * RET_HEADS * RET_DV
ROPE_BASE = 10000.0
RET_LN_EPS = 1e-5

S5_GROUP = 16
S5_GROUPS = D_MODEL // S5_GROUP
S5_STATE = 64
S5_DT_MIN = 1e-3
S5_DT_MAX = 1e-1

RWKV_HEAD = 64
RWKV_HEADS = D_MODEL // RWKV_HEAD
RWKV_DECAY_RANK = 64
RWKV_A_RANK = 64
RWKV_GATE_RANK = 160
RWKV_GN_EPS = 64e-5

N_GLA = (DEPTH + N_MIXERS - 1) // N_MIXERS
N_RET = (DEPTH + N_MIXERS - 2) // N_MIXERS
N_S5 = (DEPTH + N_MIXERS - 3) // N_MIXERS
N_RWKV = DEPTH // N_MIXERS

kernel_name = "hybrid_gla_retnet_s5_rwkv7_macaron"


def rmsnorm(x, g):
    xf = x.astype(jnp.float32)
    y = xf * lax.rsqrt(jnp.mean(xf * xf, axis=-1, keepdims=True) + NORM_EPS)
    return (y * g.astype(jnp.float32)).astype(x.dtype)


def head_layernorm(o, eps):
    of = o.astype(jnp.float32)
    mean = jnp.mean(of, axis=-1, keepdims=True)
    var = jnp.mean(jnp.square(of - mean), axis=-1, keepdims=True)
    return (of - mean) * lax.rsqrt(var + eps)


def swiglu(x, w_gu, w_d):
    gate, up = jnp.split(x @ w_gu, 2, axis=-1)
    return (jax.nn.silu(gate) * up) @ w_d


def rotary_tables(positions):
    inv = ROPE_BASE ** (-jnp.linspace(0.0, 1.0, RET_DK // 2, dtype=jnp.float32))
    ang = positions.astype(jnp.float32)[..., None] * inv
    return jnp.cos(ang), jnp.sin(ang)


def apply_rotary(t, cos, sin):
    t2 = t.reshape(t.shape[:-1] + (t.shape[-1] // 2, 2))
    a, b = t2[..., 0], t2[..., 1]
    c, s = cos[:, :, None, :], sin[:, :, None, :]
    return jnp.stack([a * c - b * s, b * c + a * s], axis=-1).reshape(t.shape)


def gla_mixer(x, w_in, w_gate_up, b_gate, norm_g, w_out):
    bsz, t, _ = x.shape
    n = t // CHUNK
    hk, hv = GLA_HEADS * GLA_DK, GLA_HEADS * GLA_DV
    proj = (x @ w_in).astype(jnp.float32)
    q, k, v, g, z = jnp.split(proj, [hk, 2 * hk, 2 * hk + hv, 2 * hk + 2 * hv], axis=-1)
    log_a = jax.nn.log_sigmoid(z @ w_gate_up + b_gate) / GLA_TAU
    ch = lambda u, d: u.reshape(bsz, n, CHUNK, GLA_HEADS, d)
    qc = ch(q, GLA_DK) * GLA_DK ** -0.5
    kc = ch(k, GLA_DK)
    vc = ch(v, GLA_DV)
    bc = jnp.cumsum(ch(log_a, GLA_DK), axis=2)
    causal = jnp.tril(jnp.ones((CHUNK, CHUNK), dtype=bool))
    scores = jnp.einsum('bnchd,bnshd->bnhcs', qc * jnp.exp(bc), kc * jnp.exp(-bc))
    intra = jnp.einsum('bnhcs,bnshe->bnche', jnp.where(causal, scores, 0.0), vc)

    def step(state, inp):
        q_n, k_n, v_n, b_n = inp
        b_last = b_n[:, -1]
        inter = jnp.einsum('bchd,bhde->bche', q_n * jnp.exp(b_n), state)
        state = jnp.exp(b_last)[..., None] * state + jnp.einsum(
            'bchd,bche->bhde', k_n * jnp.exp(b_last[:, None] - b_n), v_n)
        return state, inter

    s0 = jnp.zeros((bsz, GLA_HEADS, GLA_DK, GLA_DV), jnp.float32)
    _, inter = lax.scan(step, s0, (qc.swapaxes(0, 1), kc.swapaxes(0, 1),
                                   vc.swapaxes(0, 1), bc.swapaxes(0, 1)))
    o = (intra + inter.swapaxes(0, 1)).reshape(bsz, t, GLA_HEADS, GLA_DV)
    o = rmsnorm(o, norm_g).reshape(bsz, t, hv) * jax.nn.silu(g)
    return (o @ w_out).astype(x.dtype)


def retention_mixer(x, cos, sin, w_in, w_out):
    bsz, t, _ = x.shape
    n = t // CHUNK
    hk, hv = RET_HEADS * RET_DK, RET_HEADS * RET_DV
    proj = (x @ w_in).astype(jnp.float32)
    q, k, v, g = jnp.split(proj, [hk, 2 * hk, 2 * hk + hv], axis=-1)
    q = apply_rotary(q.reshape(bsz, t, RET_HEADS, RET_DK), cos, sin)
    k = apply_rotary(k.reshape(bsz, t, RET_HEADS, RET_DK), cos, sin) * RET_DK ** -0.5
    lg = jnp.log(1.0 - 2.0 ** (-5.0 - jnp.arange(RET_HEADS, dtype=jnp.float32)))
    idx = jnp.arange(CHUNK, dtype=jnp.float32)
    rel = idx[:, None] - idx[None, :]
    intra_decay = jnp.where(rel >= 0, jnp.exp(jnp.maximum(rel, 0.0)[None] * lg[:, None, None]), 0.0)
    q_decay = jnp.exp((idx + 1.0)[:, None] * lg[None])
    k_decay = jnp.exp((CHUNK - 1.0 - idx)[:, None] * lg[None])
    chunk_decay = jnp.exp(CHUNK * lg)
    qc = q.reshape(bsz, n, CHUNK, RET_HEADS, RET_DK)
    kc = k.reshape(bsz, n, CHUNK, RET_HEADS, RET_DK)
    vc = v.reshape(bsz, n, CHUNK, RET_HEADS, RET_DV)
    scores = jnp.einsum('bnchd,bnshd->bnhcs', qc, kc) * intra_decay
    intra = jnp.einsum('bnhcs,bnshe->bnche', scores, vc)

    def step(state, inp):
        q_n, k_n, v_n = inp
        inter = jnp.einsum('bchd,bhde->bche', q_n, state) * q_decay[None, :, :, None]
        state = state * chunk_decay[None, :, None, None] + jnp.einsum(
            'bchd,bche->bhde', k_n * k_decay[None, :, :, None], v_n)
        return state, inter

    s0 = jnp.zeros((bsz, RET_HEADS, RET_DK, RET_DV), jnp.float32)
    _, inter = lax.scan(step, s0, (qc.swapaxes(0, 1), kc.swapaxes(0, 1), vc.swapaxes(0, 1)))
    o = (intra + inter.swapaxes(0, 1)).reshape(bsz, t, RET_HEADS, RET_DV)
    o = head_layernorm(o, RET_LN_EPS).reshape(bsz, t, hv) * jax.nn.silu(g)
    return (o @ w_out).astype(x.dtype)


def _complex_affine_combine(e1, e2):
    a1r, a1i, b1r, b1i = e1
    a2r, a2i, b2r, b2i = e2
    return (a2r * a1r - a2i * a1i,
            a2r * a1i + a2i * a1r,
            a2r * b1r - a2i * b1i + b2r,
            a2r * b1i + a2i * b1r + b2i)


def s5_mixer(u, lam_re, lam_im, log_dt, b_re, b_im, c_re, c_im, d_skip, w_glu, b_glu):
    bsz, t, d = u.shape
    f32 = jnp.float32
    lam_re = lam_re.astype(f32)
    lam_im = lam_im.astype(f32)
    dt = jnp.exp(log_dt.astype(f32))[:, None]
    mag = jnp.exp(lam_re * dt)
    lb_re = mag * jnp.cos(lam_im * dt)
    lb_im = mag * jnp.sin(lam_im * dt)
    den = lam_re * lam_re + lam_im * lam_im
    f_re = ((lb_re - 1.0) * lam_re + lb_im * lam_im) / den
    f_im = (lb_im * lam_re - (lb_re - 1.0) * lam_im) / den
    bb_re = f_re[..., None] * b_re - f_im[..., None] * b_im
    bb_im = f_re[..., None] * b_im + f_im[..., None] * b_re
    ug = u.astype(f32).reshape(bsz, t, S5_GROUPS, S5_GROUP)
    bu_re = jnp.einsum('btgc,gpc->btgp', ug, bb_re)
    bu_im = jnp.einsum('btgc,gpc->btgp', ug, bb_im)
    shape_a = (1, t, S5_GROUPS, S5_STATE)
    a_re = jnp.broadcast_to(lb_re, shape_a)
    a_im = jnp.broadcast_to(lb_im, shape_a)
    _, _, x_re, x_im = lax.associative_scan(_complex_affine_combine, (a_re, a_im, bu_re, bu_im), axis=1)
    y = jnp.einsum('btgp,gcp->btgc', x_re, c_re) - jnp.einsum('btgp,gcp->btgc', x_im, c_im)
    y = y.reshape(bsz, t, d) + d_skip * u
    z = jax.nn.gelu(y)
    return (z * jax.nn.sigmoid(z @ w_glu + b_glu)).astype(u.dtype)


def rwkv7_mixer(x, mu, w_rkv, w0, w1, w2, a0, a1, a2, g1, g2, k_k, k_a, r_k, ln_w, ln_b, w_out):
    bsz, t, d = x.shape
    f32 = jnp.float32
    xf = x.astype(f32)
    xx = jnp.pad(xf, ((0, 0), (1, 0), (0, 0)))[:, :-1] - xf
    xs = xf[None] + xx[None] * mu[:, None, None, :]
    r, k, v = jnp.einsum('nbtd,nde->nbte', xs[:3], w_rkv)
    w_raw = -jax.nn.softplus(-(w0 + jnp.tanh(xs[3] @ w1) @ w2)) - 0.5
    decay = jnp.exp(-jnp.exp(w_raw))
    a = jax.nn.sigmoid(a0 + (xs[4] @ a1) @ a2)
    g = jax.nn.sigmoid(xs[5] @ g1) @ g2
    hd = lambda u: u.reshape(bsz, t, RWKV_HEADS, RWKV_HEAD)
    kk = hd(k * k_k)
    kk = kk / jnp.maximum(jnp.sqrt(jnp.sum(kk * kk, axis=-1, keepdims=True)), 1e-12)
    k = hd(k * (1.0 + (a - 1.0) * k_a))
    r, v, decay, a = hd(r), hd(v), hd(decay), hd(a)

    def step(state, inp):
        r_t, w_t, k_t, v_t, a_t, b_t = inp
        sa = jnp.einsum('bhij,bhj->bhi', state, a_t)
        state = (state * w_t[:, :, None, :] + sa[..., None] * b_t[:, :, None, :]
                 + v_t[..., None] * k_t[:, :, None, :])
        return state, jnp.einsum('bhij,bhj->bhi', state, r_t)

    tm = lambda u: u.swapaxes(0, 1)
    s0 = jnp.zeros((bsz, RWKV_HEADS, RWKV_HEAD, RWKV_HEAD), f32)
    _, y = lax.scan(step, s0, (tm(r), tm(decay), tm(k), tm(v), tm(-kk), tm(kk * a)))
    y = y.swapaxes(0, 1)
    y = head_layernorm(y, RWKV_GN_EPS).reshape(bsz, t, d) * ln_w + ln_b
    y = y + (jnp.sum(r * k * r_k, axis=-1, keepdims=True) * v).reshape(bsz, t, d)
    return ((y * g) @ w_out).astype(x.dtype)


def setup_inputs(seed: int = 0) -> dict:
    key = jax.random.key(seed)
    ks = iter(jax.random.split(key, 64))
    f32 = jnp.float32
    D = D_MODEL

    def nrm(shape, scale):
        return scale * jax.random.normal(next(ks), shape, f32)

    def uni(shape, lo, hi):
        return jax.random.uniform(next(ks), shape, f32, lo, hi)

    x = nrm((BATCH, SEQ, D), 1.0)
    p = nrm((DEPTH, BATCH, SEQ, PLE_DIM), 1.0)
    positions = (jax.random.randint(next(ks), (BATCH, 1), 0, MAX_POS_OFFSET, dtype=jnp.int32)
                 + jnp.arange(SEQ, dtype=jnp.int32)[None, :])
    norm_g = 1.0 + nrm((DEPTH, 4, D), 0.05)
    final_g = 1.0 + nrm((D,), 0.05)
    ffn_w_gu = nrm((DEPTH, 2, D, 2 * D_FF), D ** -0.5)
    ffn_w_d = nrm((DEPTH, 2, D_FF, D), D_FF ** -0.5)
    ple_w_proj = nrm((DEPTH, PLE_DIM, D), PLE_DIM ** -0.5)
    ple_w_gate = nrm((DEPTH, D, D), D ** -0.5)

    gla_w_in = nrm((N_GLA, D, GLA_IN), D ** -0.5)
    gla_w_gate_up = nrm((N_GLA, GLA_GATE_RANK, GLA_HEADS * GLA_DK), GLA_GATE_RANK ** -0.5)
    gla_b_gate = nrm((N_GLA, GLA_HEADS * GLA_DK), 0.5)
    gla_norm_g = 1.0 + nrm((N_GLA, GLA_DV), 0.05)
    gla_w_out = nrm((N_GLA, GLA_HEADS * GLA_DV, D), (GLA_HEADS * GLA_DV) ** -0.5)

    ret_w_in = nrm((N_RET, D, RET_IN), D ** -0.5)
    ret_w_out = nrm((N_RET, RET_HEADS * RET_DV, D), (RET_HEADS * RET_DV) ** -0.5)

    s5_lam_re = -0.5 + nrm((N_S5, S5_GROUPS, S5_STATE), 0.01)
    s5_lam_im = math.pi * jnp.arange(S5_STATE, dtype=f32) + nrm((N_S5, S5_GROUPS, S5_STATE), 0.01)
    s5_log_dt = uni((N_S5, S5_GROUPS), math.log(S5_DT_MIN), math.log(S5_DT_MAX))
    b_scale = (2.0 * S5_GROUP) ** -0.5
    s5_b_re = nrm((N_S5, S5_GROUPS, S5_STATE, S5_GROUP), b_scale)
    s5_b_im = nrm((N_S5, S5_GROUPS, S5_STATE, S5_GROUP), b_scale)
    c_scale = (2.0 * S5_STATE) ** -0.5
    s5_c_re = nrm((N_S5, S5_GROUPS, S5_GROUP, S5_STATE), c_scale)
    s5_c_im = nrm((N_S5, S5_GROUPS, S5_GROUP, S5_STATE), c_scale)
    s5_d = nrm((N_S5, D), 1.0)
    s5_w_glu = nrm((N_S5, D, D), D ** -0.5)
    s5_b_glu = nrm((N_S5, D), 0.02)

    rw_mu = uni((N_RWKV, 6, D), 0.0, 1.0)
    rw_w_rkv = nrm((N_RWKV, 3, D, D), D ** -0.5)
    rw_w0 = uni((N_RWKV, D), -6.0, -1.0)
    rw_w1 = nrm((N_RWKV, D, RWKV_DECAY_RANK), D ** -0.5)
    rw_w2 = nrm((N_RWKV, RWKV_DECAY_RANK, D), RWKV_DECAY_RANK ** -0.5)
    rw_a0 = nrm((N_RWKV, D), 0.1)
    rw_a1 = nrm((N_RWKV, D, RWKV_A_RANK), D ** -0.5)
    rw_a2 = nrm((N_RWKV, RWKV_A_RANK, D), RWKV_A_RANK ** -0.5)
    rw_g1 = nrm((N_RWKV, D, RWKV_GATE_RANK), D ** -0.5)
    rw_g2 = nrm((N_RWKV, RWKV_GATE_RANK, D), RWKV_GATE_RANK ** -0.5)
    rw_k_k = 0.85 + nrm((N_RWKV, D), 0.05)
    rw_k_a = 1.0 + nrm((N_RWKV, D), 0.05)
    rw_r_k = nrm((N_RWKV, RWKV_HEADS, RWKV_HEAD), 0.1)
    rw_ln_w = 1.0 + nrm((N_RWKV, D), 0.05)
    rw_ln_b = nrm((N_RWKV, D), 0.02)
    rw_w_out = nrm((N_RWKV, D, D), D ** -0.5)

    return {"x": x, "p": p, "positions": positions, "norm_g": norm_g, "final_g": final_g,
            "ffn_w_gu": ffn_w_gu, "ffn_w_d": ffn_w_d, "ple_w_proj": ple_w_proj, "ple_w_gate": ple_w_gate,
            "gla_w_in": gla_w_in, "gla_w_gate_up": gla_w_gate_up, "gla_b_gate": gla_b_gate,
            "gla_norm_g": gla_norm_g, "gla_w_out": gla_w_out,
            "ret_w_in": ret_w_in, "ret_w_out": ret_w_out,
            "s5_lam_re": s5_lam_re, "s5_lam_im": s5_lam_im, "s5_log_dt": s5_log_dt,
            "s5_b_re": s5_b_re, "s5_b_im": s5_b_im, "s5_c_re": s5_c_re, "s5_c_im": s5_c_im,
            "s5_d": s5_d, "s5_w_glu": s5_w_glu, "s5_b_glu": s5_b_glu,
            "rw_mu": rw_mu, "rw_w_rkv": rw_w_rkv, "rw_w0": rw_w0, "rw_w1": rw_w1, "rw_w2": rw_w2,
            "rw_a0": rw_a0, "rw_a1": rw_a1, "rw_a2": rw_a2, "rw_g1": rw_g1, "rw_g2": rw_g2,
            "rw_k_k": rw_k_k, "rw_k_a": rw_k_a, "rw_r_k": rw_r_k, "rw_ln_w": rw_ln_w,
            "rw_ln_b": rw_ln_b, "rw_w_out": rw_w_out}


def reference(x, p, positions, norm_g, final_g, ffn_w_gu, ffn_w_d, ple_w_proj, ple_w_gate,
              gla_w_in, gla_w_gate_up, gla_b_gate, gla_norm_g, gla_w_out,
              ret_w_in, ret_w_out,
              s5_lam_re, s5_lam_im, s5_log_dt, s5_b_re, s5_b_im, s5_c_re, s5_c_im,
              s5_d, s5_w_glu, s5_b_glu,
              rw_mu, rw_w_rkv, rw_w0, rw_w1, rw_w2, rw_a0, rw_a1, rw_a2, rw_g1, rw_g2,
              rw_k_k, rw_k_a, rw_r_k, rw_ln_w, rw_ln_b, rw_w_out):
    cos, sin = rotary_tables(positions)
    h = x
    for i in range(DEPTH):
        kind = i % N_MIXERS
        j = i // N_MIXERS
        g = norm_g[i]
        h = h + 0.5 * swiglu(rmsnorm(h, g[0]), ffn_w_gu[i, 0], ffn_w_d[i, 0])
        hn = rmsnorm(h, g[1])
        if kind == 0:
            mix = gla_mixer(hn, gla_w_in[j], gla_w_gate_up[j], gla_b_gate[j], gla_norm_g[j], gla_w_out[j])
        elif kind == 1:
            mix = retention_mixer(hn, cos, sin, ret_w_in[j], ret_w_out[j])
        elif kind == 2:
            mix = s5_mixer(hn, s5_lam_re[j], s5_lam_im[j], s5_log_dt[j], s5_b_re[j], s5_b_im[j],
                           s5_c_re[j], s5_c_im[j], s5_d[j], s5_w_glu[j], s5_b_glu[j])
        else:
            mix = rwkv7_mixer(hn, rw_mu[j], rw_w_rkv[j], rw_w0[j], rw_w1[j], rw_w2[j],
                              rw_a0[j], rw_a1[j], rw_a2[j], rw_g1[j], rw_g2[j],
                              rw_k_k[j], rw_k_a[j], rw_r_k[j], rw_ln_w[j], rw_ln_b[j], rw_w_out[j])
        h = h + mix.astype(h.dtype)
        h = h + 0.5 * swiglu(rmsnorm(h, g[2]), ffn_w_gu[i, 1], ffn_w_d[i, 1])
        gate = jax.nn.sigmoid(rmsnorm(h, g[3]) @ ple_w_gate[i])
        h = h + gate * (p[i] @ ple_w_proj[i])
    return rmsnorm(h, final_g)
```

```python
import numpy as np
from contextlib import ExitStack
import concourse.bass as bass
import concourse.mybir as mybir
from concourse.bass_utils import run_bass_kernel_spmd

F32 = mybir.dt.float32
BF16 = mybir.dt.bfloat16
I32 = mybir.dt.int32
AF = mybir.ActivationFunctionType
ALU = mybir.AluOpType
AX = mybir.AxisListType

T = 4096
D = 1024
FF = 2816
NB = T // 128
EPS = 1e-6
ENGS = ("sync", "scalar", "gpsimd", "tensor", "vector")


class Res:
    __slots__ = ("name", "w", "r", "dsem", "dcnt")

    def __init__(self, name=""):
        self.name = name
        self.w = None
        self.r = []
        self.dsem = None
        self.dcnt = 0


class Prog:
    def __init__(self, nc, same_engine_sync=True):
        self.nc = nc
        self.ops = {e: [] for e in ENGS}
        self.seen = {e: {} for e in ENGS}
        self.dres = []
        self.free_dsems = []
        self.ndsem = 0
        self.same_engine_sync = same_engine_sync
        self.last_ev = {e: None for e in ENGS}

    def _need(self, eng, ev, waits):
        if ev is None:
            return
        if ev[0] == 'E':
            _, peng, idx = ev
            if peng == eng and (eng == "tensor" or not self.same_engine_sync):
                return
            key = ('E', peng)
            val = idx
        else:
            _, sid, cnt = ev
            key = ('D', sid)
            val = cnt
        if self.seen[eng].get(key, -1) >= val:
            return
        self.seen[eng][key] = val
        waits.append(ev)

    def op(self, eng, fn, reads=(), writes=(), dma=False):
        waits = []
        for r in reads:
            self._need(eng, r.w, waits)
        for w in writes:
            self._need(eng, w.w, waits)
            for ev in w.r:
                self._need(eng, ev, waits)
        if dma:
            res = writes[0]
            if res.dsem is None:
                if self.free_dsems:
                    res.dsem, res.dcnt = self.free_dsems.pop()
                else:
                    res.dsem = self.ndsem
                    self.ndsem += 1
                    res.dcnt = 0
                self.dres.append(res)
            res.dcnt += 16
            ev = ('D', res.dsem, res.dcnt)
        else:
            ev = ('E', eng, len(self.ops[eng]))
            self.last_ev[eng] = ev
        self.ops[eng].append([fn, waits, ev, dma])
        for r in reads:
            r.r.append(ev)
        for w in writes:
            w.w = ev
            w.r = []
        return ev

    def barrier(self, keep=()):
        evs = [self.last_ev[e] for e in ENGS if self.last_ev[e] is not None]
        evs += [('D', r.dsem, r.dcnt) for r in self.dres]
        for e in ENGS:
            waits = []
            for ev in evs:
                self._need(e, ev, waits)
            if waits:
                self.ops[e].append([None, waits, None, False])
        newd = []
        for r in self.dres:
            if r in keep:
                newd.append(r)
            else:
                self.free_dsems.append((r.dsem, r.dcnt))
                r.dsem = None
                r.w = None
                r.r = []
        self.dres = newd

    def emit(self, final_waits=()):
        nc = self.nc
        fw = []
        for r in final_waits:
            self._need("sync", r.w, fw)
        self.ops["sync"].append([None, fw, None, False])
        needed = {e: set() for e in ENGS}
        for e in ENGS:
            for fn, waits, ev, dma in self.ops[e]:
                for w in waits:
                    if w[0] == 'E':
                        needed[w[1]].add(w[2])
        semval = {e: {} for e in ENGS}
        for e in ENGS:
            c = 0
            for i in sorted(needed[e]):
                c += 1
                semval[e][i] = c
            print(f"[prog] {e}: {len(self.ops[e])} ops, {c} signals", flush=True)
        print(f"[prog] dma sems: {self.ndsem}", flush=True)
        with ExitStack() as es:
            esem = {e: es.enter_context(nc.semaphore(f"es_{e}")) for e in ENGS}
            dsem = [es.enter_context(nc.semaphore(f"ds_{i}")) for i in range(self.ndsem)]
            block = es.enter_context(nc.Block())

            def mk(e):
                def body(engobj):
                    for i, (fn, waits, ev, dma) in enumerate(self.ops[e]):
                        for w in waits:
                            if w[0] == 'E':
                                engobj.wait_ge(esem[w[1]], semval[w[1]][w[2]])
                            else:
                                engobj.wait_ge(dsem[w[1]], w[2])
                        if fn is None:
                            continue
                        ins = fn(engobj)
                        if dma:
                            ins.then_inc(dsem[ev[1]], 16)
                        elif i in semval[e]:
                            ins.then_inc(esem[e], 1)
                return body

            for e in ENGS:
                getattr(block, e)(mk(e))


class Arena:
    def __init__(self, tensor, n32):
        self.t = tensor
        self.n32 = n32
        self.off = 0

    def reset(self, to=0):
        self.off = to

    def alloc_at(self, off32, shape, dt):
        save = self.off
        self.off = off32
        ap = self.alloc(shape, dt)
        end = self.off
        self.off = save
        return ap, end

    def alloc(self, shape, dt):
        nel = 1
        for s in shape[1:]:
            nel *= s
        n32 = nel if dt in (F32, I32) else (nel + 1) // 2
        n32 = (n32 + 7) // 8 * 8
        assert self.off + n32 <= self.n32, f"arena overflow {self.off}+{n32}>{self.n32}"
        ap = self.t[:, self.off:self.off + n32]
        self.off += n32
        if dt != F32:
            ap = ap.bitcast(dt)
        ap = ap[:, 0:nel]
        np_ = shape[0]
        if np_ < 128:
            ap = ap[0:np_, :]
        if len(shape) == 3:
            ap = ap.rearrange("p (a b) -> p a b", b=shape[2])
        elif len(shape) == 4:
            ap = ap.rearrange("p (a b c) -> p a b c", b=shape[2], c=shape[3])
        return ap


class Ctx:
    pass


ARENA_N32 = 206 * 256
FFN_TOP_N32 = 8 * 5632 // 2 + 22 * 1024 // 2 + 1024
FFN_TOP_OFF = ARENA_N32 - FFN_TOP_N32


def make_ctx(nc, es):
    C = Ctx()
    C.nc = nc
    C.P = Prog(nc)
    arena_t = es.enter_context(nc.sbuf_tensor("arena", [128, ARENA_N32], F32))
    C.A = Arena(arena_t, ARENA_N32)
    C.ps = es.enter_context(nc.psum_tensor("psum", [128, 4096], F32))
    C.ident_f = C.A.alloc([128, 128], F32)
    C.ident_b = C.A.alloc([128, 128], BF16)
    C.base = C.A.off
    C.Rconst = Res("const")
    return C


def bank(C, b, dt=F32):
    ap = C.ps[:, b * 512:(b + 1) * 512]
    if dt != F32:
        ap = ap.bitcast(dt)
    return ap


def setup_consts(C, ident_dram):
    P = C.P
    P.op("sync", lambda e: e.dma_start(out=C.ident_f, in_=ident_dram), writes=[C.Rconst], dma=True)
    P.op("vector", lambda e: e.tensor_copy(out=C.ident_b, in_=C.ident_f), reads=[C.Rconst], writes=[C.Rconst])


def bcast_row(dram_vec_ap, n):
    return dram_vec_ap.rearrange("(o n) -> o n", o=1).to_broadcast([128, n])


def norm_rows(C, ht, Rht, gb, xn, Rxn, ss, Rss, junk, Rjunk, eps=EPS, dfeat=D):
    P = C.P
    P.op("scalar", lambda e: e.activation(out=junk, in_=ht, func=AF.Square, accum_out=ss[:, 0:1]),
         reads=[Rht], writes=[Rjunk, Rss])
    P.op("vector", lambda e: e.tensor_scalar(out=ss[:, 1:2], in0=ss[:, 0:1], scalar1=1.0 / dfeat, scalar2=eps,
                                             op0=ALU.mult, op1=ALU.add), reads=[Rss], writes=[Rss])
    P.op("scalar", lambda e: e.activation(out=ss[:, 1:2], in_=ss[:, 1:2], func=AF.Sqrt), reads=[Rss], writes=[Rss])
    P.op("vector", lambda e: e.reciprocal(out=ss[:, 1:2], in_=ss[:, 1:2]), reads=[Rss], writes=[Rss])
    if gb is not None:
        P.op("vector", lambda e: e.scalar_tensor_tensor(out=xn, in0=ht, scalar=ss[:, 1:2], in1=gb,
                                                        op0=ALU.mult, op1=ALU.mult),
             reads=[Rht, Rss], writes=[Rxn])
    else:
        P.op("vector", lambda e: e.tensor_scalar(out=xn, in0=ht, scalar1=ss[:, 1:2], scalar2=None, op0=ALU.mult),
             reads=[Rht, Rss], writes=[Rxn])


def transpose_rows(C, xn, Rxn, nchunk, pbank, Rpb, dst, Rdst, evac_eng="scalar"):
    P = C.P
    pb = bank(C, pbank, BF16)
    for c in range(nchunk):
        P.op("tensor", lambda e, c=c: e.transpose(out=pb[:, c * 128:(c + 1) * 128], in_=xn[:, c * 128:(c + 1) * 128],
                                                  identity=C.ident_b),
             reads=[Rxn, C.Rconst], writes=[Rpb])
    src = pb[:, 0:nchunk * 128].rearrange("p (a b) -> p a b", b=128)
    if evac_eng == "scalar":
        P.op("scalar", lambda e: e.copy(out=dst, in_=src), reads=[Rpb], writes=[Rdst])
    else:
        P.op("vector", lambda e: e.tensor_copy(out=dst, in_=src), reads=[Rpb], writes=[Rdst])


def load_cast_weight(C, w_dram, rows, cols, dst, Rdst, stages, Rstages, col_chunk, cnt=[0]):
    P = C.P
    nk = rows // 128
    for kc in range(nk):
        for c0 in range(0, cols, col_chunk):
            cw = min(col_chunk, cols - c0)
            i = cnt[0] % len(stages)
            cnt[0] += 1
            st, Rst = stages[i], Rstages[i]
            q = "sync" if (cnt[0] % 2 == 0) else "gpsimd"
            P.op(q, lambda e, st=st, kc=kc, c0=c0, cw=cw: e.dma_start(out=st[:, 0:cw], in_=w_dram[kc * 128:(kc + 1) * 128, c0:c0 + cw]),
                 writes=[Rst], dma=True)
            ce = ("vector", "scalar")[cnt[0] % 2]
            if ce == "vector":
                P.op("vector", lambda e, st=st, kc=kc, c0=c0, cw=cw: e.tensor_copy(out=dst[:, kc, c0:c0 + cw], in_=st[:, 0:cw]),
                     reads=[Rst], writes=[Res()])
            else:
                P.op("scalar", lambda e, st=st, kc=kc, c0=c0, cw=cw: e.copy(out=dst[:, kc, c0:c0 + cw], in_=st[:, 0:cw]),
                     reads=[Rst], writes=[Res()])


def ffn_top(C):
    A = C.A
    wgu, e1 = A.alloc_at(FFN_TOP_OFF, [128, 8, 2 * FF], BF16)
    wd, e2 = A.alloc_at(e1, [128, FF // 128, D], BF16)
    gb, e3 = A.alloc_at(e2, [128, D], F32)
    assert e3 <= ARENA_N32
    return wgu, wd, gb


def ffn_prefetch_pieces(C, wgu_dram, wd_dram, g_dram, stages, Rstages):
    P = C.P
    wgu, wd, gb = ffn_top(C)
    pieces = []
    cnt = [0]

    def mk(src, dst, cw):
        def f():
            i = cnt[0] % len(stages)
            cnt[0] += 1
            st, Rst = stages[i], Rstages[i]
            DMA(P, "sync", st[:, 0:cw], src, [], [Rst])
            CP(P, ("vector", "scalar")[cnt[0] % 2], dst, st[:, 0:cw], [Rst], [Res()])
        return f

    pieces.append(lambda: DMA(P, "sync", gb, bcast_row(g_dram, D), [], [Res()]))
    CW = 704
    for kc in range(8):
        for c0 in range(0, 2 * FF, CW):
            pieces.append(mk(wgu_dram[kc * 128:(kc + 1) * 128, c0:c0 + CW], wgu[:, kc, c0:c0 + CW], CW))
    for fc in range(FF // 128):
        pieces.append(mk(wd_dram[fc * 128:(fc + 1) * 128, :], wd[:, fc, :], D))
    return pieces


def emit_pieces(pieces, i, n):
    if not pieces:
        return
    lo = (len(pieces) * i) // n
    hi = (len(pieces) * (i + 1)) // n
    for f in pieces[lo:hi]:
        f()


def ffn_phase(C, h_in, h_dram, Rh, wgu_dram, wd_dram, g_dram, preloaded=False):
    P, A = C.P, C.A
    P.barrier()
    A.reset(C.base)
    TT_ = 256
    TT = TT_
    wgu, wd, gb = ffn_top(C)
    if not preloaded:
        stages = [A.alloc([128, FF], F32) for _ in range(2)]
        Rst = [Res("st") for _ in stages]
        Rgb = Res("gb")
        P.op("sync", lambda e: e.dma_start(out=gb, in_=bcast_row(g_dram, D)), writes=[Rgb], dma=True)
        load_cast_weight(C, wgu_dram, D, 2 * FF, wgu, None, stages, Rst, FF)
        P.barrier()
        A.reset(C.base)
    NWS = 6
    wdst = [A.alloc([128, D], F32) for _ in range(NWS)]
    Rwdst = [Res("wdst") for _ in range(NWS)]
    Rwd = [Res("wd") for _ in range(FF // 128)]
    ht = [[A.alloc([128, D], F32) for _ in range(2)] for _ in range(2)]
    Rht = [[Res("ht") for _ in range(2)] for _ in range(2)]
    xn = [A.alloc([128, D], BF16) for _ in range(2)]
    Rxn = [Res("xn") for _ in range(2)]
    junk = A.alloc([128, D], BF16)
    Rjunk = Res("junk")
    ss = [A.alloc([128, 2], F32) for _ in range(2)]
    Rss = [Res("ss") for _ in range(2)]
    xnT = [A.alloc([128, 8, TT], BF16) for _ in range(2)]
    RxnT = [[Res("xnT") for _ in range(2)] for _ in range(2)]
    hT = A.alloc([128, FF // 128, TT], BF16)
    RhT = [Res("hT") for _ in range(FF // 128)]
    sg = [A.alloc([128, TT], F32) for _ in range(2)]
    Rsg = [Res("sg") for _ in range(2)]
    Rpb = [Res("pb") for _ in range(8)]
    assert A.off <= FFN_TOP_OFF, A.off
    nfc = FF // 128
    k = 0
    kk_ = [0]

    def head_norm(tt):
        s = tt % 2
        for tb in range(2):
            blk = tt * 2 + tb
            k = (tt * 2 + tb) % 2
            P.op("sync", lambda e, s=s, tb=tb, blk=blk: e.dma_start(out=ht[s][tb], in_=h_in[blk * 128:(blk + 1) * 128, :]),
                 reads=[Rh[blk]], writes=[Rht[s][tb]], dma=True)
            norm_rows(C, ht[s][tb], Rht[s][tb], gb, xn[k], Rxn[k], ss[k], Rss[k], junk, Rjunk)

    def head_T(tt):
        s = tt % 2
        for tb in range(2):
            k = (tt * 2 + tb) % 2
            transpose_rows(C, xn[k], Rxn[k], 8, k, Rpb[k], xnT[s][:, :, tb * 128:(tb + 1) * 128], RxnT[s][tb], evac_eng="vector")

    head_norm(0)
    head_T(0)
    for tt in range(T // TT):
        s = tt % 2
        if tt == 0 and not preloaded:
            for fc0 in range(NWS):
                DMA(P, "sync", wdst[fc0], wd_dram[fc0 * 128:(fc0 + 1) * 128, :], [], [Rwdst[fc0]])
        for fc in range(nfc):
            if fc == 14 and tt + 1 < T // TT:
                head_norm(tt + 1)
            j = fc % 2
            pg = bank(C, 2 + 2 * j)[:, 0:TT]
            pu = bank(C, 3 + 2 * j)[:, 0:TT]
            for kc in range(8):
                P.op("tensor", lambda e, pg=pg, kc=kc, fc=fc, s=s: e.matmul(pg, lhsT=wgu[:, kc, fc * 128:(fc + 1) * 128], rhs=xnT[s][:, kc, :],
                                                                          start=(kc == 0), stop=(kc == 7)),
                     reads=[RxnT[s][0], RxnT[s][1]], writes=[Rpb[2 + 2 * j]])
            for kc in range(8):
                P.op("tensor", lambda e, pu=pu, kc=kc, fc=fc, s=s: e.matmul(pu, lhsT=wgu[:, kc, FF + fc * 128:FF + (fc + 1) * 128], rhs=xnT[s][:, kc, :],
                                                                          start=(kc == 0), stop=(kc == 7)),
                     reads=[RxnT[s][0], RxnT[s][1]], writes=[Rpb[3 + 2 * j]])
            P.op("scalar", lambda e, pg=pg, j=j: e.activation(out=sg[j], in_=pg, func=AF.Silu), reads=[Rpb[2 + 2 * j]], writes=[Rsg[j]])
            P.op("vector", lambda e, pu=pu, j=j, fc=fc: e.tensor_tensor(out=hT[:, fc, :], in0=pu, in1=sg[j], op=ALU.mult),
                 reads=[Rpb[3 + 2 * j], Rsg[j]], writes=[RhT[fc]])
            if tt == 0 and not preloaded:
                CP(P, ("gpsimd", "scalar", "vector")[fc % 3], wd[:, fc, :], wdst[fc % NWS], [Rwdst[fc % NWS]], [Rwd[fc]])
                if fc + NWS < nfc:
                    DMA(P, "sync", wdst[(fc + NWS) % NWS], wd_dram[(fc + NWS) * 128:(fc + NWS + 1) * 128, :], [], [Rwdst[(fc + NWS) % NWS]])
        if tt + 1 < T // TT:
            head_T(tt + 1)
        for tb in range(2):
            blk = tt * 2 + tb
            for dh in range(2):
                b = 6 + dh
                po = bank(C, b)
                for fc in range(nfc):
                    P.op("tensor", lambda e, po=po, fc=fc, tb=tb, dh=dh: e.matmul(po, lhsT=hT[:, fc, tb * 128:(tb + 1) * 128], rhs=wd[:, fc, dh * 512:(dh + 1) * 512],
                                                                                   start=(fc == 0), stop=(fc == nfc - 1)),
                         reads=[RhT[fc], Rwd[fc]], writes=[Rpb[b]])
                P.op("vector", lambda e, po=po, s=s, tb=tb, dh=dh: e.scalar_tensor_tensor(out=ht[s][tb][:, dh * 512:(dh + 1) * 512], in0=po, scalar=0.5,
                                                                                         in1=ht[s][tb][:, dh * 512:(dh + 1) * 512], op0=ALU.mult, op1=ALU.add),
                     reads=[Rpb[b], Rht[s][tb]], writes=[Rht[s][tb]])
            P.op("gpsimd", lambda e, s=s, tb=tb, blk=blk: e.dma_start(out=h_dram[blk * 128:(blk + 1) * 128, :], in_=ht[s][tb]),
                 reads=[Rht[s][tb]], writes=[Rh[blk]], dma=True)


def ple_phase(C, h_dram, Rh, wg_dram, wp_dram, g_dram, p_dram, prefetch=None, final=None):
    P, A = C.P, C.A
    P.barrier()
    A.reset(C.base)
    wg = A.alloc([128, 8, D], BF16)
    wp = A.alloc([128, 2, D], BF16)
    gb = A.alloc([128, D], F32)
    stages = [A.alloc([128, D], F32) for _ in range(2)]
    Rst = [Res("st") for _ in stages]
    Rgb = Res("gb")
    P.op("sync", lambda e: e.dma_start(out=gb, in_=bcast_row(g_dram, D)), writes=[Rgb], dma=True)
    load_cast_weight(C, wg_dram, D, D, wg, None, stages, Rst, D)
    load_cast_weight(C, wp_dram, 256, D, wp, None, stages, Rst, D)
    P.barrier()
    pieces = prefetch(stages, [Res(), Res()]) if prefetch else []
    if final is not None:
        gf_dram, out_dram, Rout = final
        gfb = A.alloc([128, D], F32)
        DMA(P, "sync", gfb, bcast_row(gf_dram, D), [], [Res()])
        xo = [A.alloc([128, D], F32) for _ in range(2)]
        Rxo = [Res("xo") for _ in range(2)]
        ssf = [A.alloc([128, 2], F32) for _ in range(2)]
        Rssf = [Res("ssf") for _ in range(2)]
    ht = [A.alloc([128, D], F32) for _ in range(2)]
    Rht = [Res("ht") for _ in range(2)]
    pt = [A.alloc([128, 256], F32) for _ in range(2)]
    Rpt = [Res("pt") for _ in range(2)]
    pbf = [A.alloc([128, 256], BF16) for _ in range(2)]
    Rpbf = [Res("pbf") for _ in range(2)]
    xn = [A.alloc([128, D], BF16) for _ in range(2)]
    Rxn = [Res("xn") for _ in range(2)]
    junk = A.alloc([128, D], BF16)
    Rjunk = Res("junk")
    ss = [A.alloc([128, 2], F32) for _ in range(2)]
    Rss = [Res("ss") for _ in range(2)]
    xnT = [A.alloc([128, 8, 128], BF16) for _ in range(2)]
    RxnT = [Res("xnT") for _ in range(2)]
    pT = [A.alloc([128, 2, 128], BF16) for _ in range(2)]
    RpT = [Res("pT") for _ in range(2)]
    sg = [A.alloc([128, 512], F32) for _ in range(2)]
    Rsg = [Res("sg") for _ in range(2)]
    Rpb = [Res("pb") for _ in range(8)]
    for blk in range(NB):
        s = blk % 2
        P.op("sync", lambda e, s=s, blk=blk: e.dma_start(out=ht[s], in_=h_dram[blk * 128:(blk + 1) * 128, :]),
             reads=[Rh[blk]], writes=[Rht[s]], dma=True)
        P.op("sync", lambda e, s=s, blk=blk: e.dma_start(out=pt[s], in_=p_dram[blk * 128:(blk + 1) * 128, :]),
             writes=[Rpt[s]], dma=True)
        norm_rows(C, ht[s], Rht[s], gb, xn[s], Rxn[s], ss[s], Rss[s], junk, Rjunk)
        transpose_rows(C, xn[s], Rxn[s], 8, s, Rpb[s], xnT[s], RxnT[s], evac_eng="vector")
        P.op("gpsimd", lambda e, s=s: e.tensor_copy(out=pbf[s], in_=pt[s]), reads=[Rpt[s]], writes=[Rpbf[s]])
        transpose_rows(C, pbf[s], Rpbf[s], 2, 2 + s, Rpb[2 + s], pT[s], RpT[s], evac_eng="scalar")
        for dh in range(2):
            pg = bank(C, 4 + dh)
            pp = bank(C, 6 + dh)
            for kc in range(8):
                P.op("tensor", lambda e, pg=pg, kc=kc, s=s, dh=dh: e.matmul(pg, lhsT=xnT[s][:, kc, :], rhs=wg[:, kc, dh * 512:(dh + 1) * 512],
                                                                          start=(kc == 0), stop=(kc == 7)),
                     reads=[RxnT[s]], writes=[Rpb[4 + dh]])
            for kc in range(2):
                P.op("tensor", lambda e, pp=pp, kc=kc, s=s, dh=dh: e.matmul(pp, lhsT=pT[s][:, kc, :], rhs=wp[:, kc, dh * 512:(dh + 1) * 512],
                                                                          start=(kc == 0), stop=(kc == 1)),
                     reads=[RpT[s]], writes=[Rpb[6 + dh]])
            P.op("scalar", lambda e, pg=pg, dh=dh: e.activation(out=sg[dh], in_=pg, func=AF.Sigmoid), reads=[Rpb[4 + dh]], writes=[Rsg[dh]])
            P.op("vector", lambda e, pp=pp, dh=dh: e.tensor_tensor(out=sg[dh], in0=pp, in1=sg[dh], op=ALU.mult),
                 reads=[Rpb[6 + dh], Rsg[dh]], writes=[Rsg[dh]])
            P.op("gpsimd", lambda e, s=s, dh=dh: e.tensor_tensor(out=ht[s][:, dh * 512:(dh + 1) * 512], in0=ht[s][:, dh * 512:(dh + 1) * 512],
                                                                  in1=sg[dh], op=ALU.add),
                 reads=[Rsg[dh], Rht[s]], writes=[Rht[s]])
        if final is None:
            P.op("gpsimd", lambda e, s=s, blk=blk: e.dma_start(out=h_dram[blk * 128:(blk + 1) * 128, :], in_=ht[s]),
                 reads=[Rht[s]], writes=[Rh[blk]], dma=True)
        else:
            norm_rows(C, ht[s], Rht[s], gfb, xo[s], Rxo[s], ssf[s], Rssf[s], junk, Rjunk)
            DMA(P, "gpsimd", out_dram[blk * 128:(blk + 1) * 128, :], xo[s], [Rxo[s]], [Rout[blk]])
        emit_pieces(pieces, blk, NB)
    assert (not prefetch) or A.off <= FFN_TOP_OFF, A.off


def final_phase(C, h_dram, Rh, g_dram, out_dram, Rout):
    P, A = C.P, C.A
    P.barrier()
    A.reset(C.base)
    gb = A.alloc([128, D], F32)
    Rgb = Res("gb")
    P.op("sync", lambda e: e.dma_start(out=gb, in_=bcast_row(g_dram, D)), writes=[Rgb], dma=True)
    P.barrier()
    ht = [A.alloc([128, D], F32) for _ in range(2)]
    Rht = [Res("ht") for _ in range(2)]
    xo = [A.alloc([128, D], F32) for _ in range(2)]
    Rxo = [Res("xo") for _ in range(2)]
    junk = A.alloc([128, D], BF16)
    Rjunk = Res("junk")
    ss = [A.alloc([128, 2], F32) for _ in range(2)]
    Rss = [Res("ss") for _ in range(2)]
    for blk in range(NB):
        s = blk % 2
        P.op("sync", lambda e, s=s, blk=blk: e.dma_start(out=ht[s], in_=h_dram[blk * 128:(blk + 1) * 128, :]),
             reads=[Rh[blk]], writes=[Rht[s]], dma=True)
        norm_rows(C, ht[s], Rht[s], gb, xo[s], Rxo[s], ss[s], Rss[s], junk, Rjunk)
        P.op("gpsimd", lambda e, s=s, blk=blk: e.dma_start(out=out_dram[blk * 128:(blk + 1) * 128, :], in_=xo[s]),
             reads=[Rxo[s]], writes=[Rout[blk]], dma=True)


def ACT(P, out, in_, func, R=(), Wr=(), bias=None, scale=None, accum_out=None):
    kw = {}
    if bias is not None:
        kw["bias"] = bias
    if scale is not None:
        kw["scale"] = scale
    if accum_out is not None:
        kw["accum_out"] = accum_out
    return P.op("scalar", lambda e: e.activation(out=out, in_=in_, func=func, **kw), reads=R, writes=Wr)


def TT(P, eng, out, in0, in1, op, R=(), Wr=()):
    return P.op(eng, lambda e: e.tensor_tensor(out=out, in0=in0, in1=in1, op=op), reads=R, writes=Wr)


def TS(P, eng, out, in0, s1, s2, op0, op1=None, R=(), Wr=()):
    if op1 is None:
        return P.op(eng, lambda e: e.tensor_scalar(out=out, in0=in0, scalar1=s1, scalar2=None, op0=op0), reads=R, writes=Wr)
    return P.op(eng, lambda e: e.tensor_scalar(out=out, in0=in0, scalar1=s1, scalar2=s2, op0=op0, op1=op1), reads=R, writes=Wr)


def STT(P, out, in0, scalar, in1, op0, op1, R=(), Wr=()):
    return P.op("vector", lambda e: e.scalar_tensor_tensor(out=out, in0=in0, scalar=scalar, in1=in1, op0=op0, op1=op1), reads=R, writes=Wr)


def MM(P, out, lhsT, rhs, start=True, stop=True, R=(), Wr=()):
    return P.op("tensor", lambda e: e.matmul(out, lhsT=lhsT, rhs=rhs, start=start, stop=stop), reads=R, writes=Wr)


def TR(P, out, in_, ident, R=(), Wr=()):
    return P.op("tensor", lambda e: e.transpose(out=out, in_=in_, identity=ident), reads=R, writes=Wr)


def CP(P, eng, out, in_, R=(), Wr=()):
    if eng == "scalar":
        return P.op("scalar", lambda e: e.copy(out=out, in_=in_), reads=R, writes=Wr)
    return P.op(eng, lambda e: e.tensor_copy(out=out, in_=in_), reads=R, writes=Wr)


def DMA(P, q, out, in_, R=(), Wr=(), **kw):
    return P.op(q, lambda e: e.dma_start(out=out, in_=in_, **kw), reads=R, writes=Wr, dma=True)


def SCAN(P, out, d0, d1, init, R=(), Wr=()):
    return P.op("vector", lambda e: e.tensor_tensor_scan(out=out, data0=d0, data1=d1, initial=init, op0=ALU.mult, op1=ALU.add), reads=R, writes=Wr)


def MEMSET(P, eng, ap, val, Wr=()):
    return P.op(eng, lambda e: e.memset(ap, val), writes=Wr)


def xnT_all_phase(C, h_dram, Rh, g_dram):
    P, A = C.P, C.A
    P.barrier()
    A.reset(C.base)
    xnT = A.alloc([128, 8, T], BF16)
    C.base2 = A.off
    gb = A.alloc([128, D], F32)
    Rgb = Res("gb")
    P.op("sync", lambda e: e.dma_start(out=gb, in_=bcast_row(g_dram, D)), writes=[Rgb], dma=True)
    ht = [A.alloc([128, D], F32) for _ in range(2)]
    Rht = [Res("ht") for _ in range(2)]
    xn = [A.alloc([128, D], BF16) for _ in range(2)]
    Rxn = [Res("xn") for _ in range(2)]
    junk = A.alloc([128, D], BF16)
    Rjunk = Res("junk")
    ss = [A.alloc([128, 2], F32) for _ in range(2)]
    Rss = [Res("ss") for _ in range(2)]
    Rpb = [Res("pb") for _ in range(2)]
    for blk in range(NB):
        s = blk % 2
        P.op("sync", lambda e, s=s, blk=blk: e.dma_start(out=ht[s], in_=h_dram[blk * 128:(blk + 1) * 128, :]),
             reads=[Rh[blk]], writes=[Rht[s]], dma=True)
        norm_rows(C, ht[s], Rht[s], gb, xn[s], Rxn[s], ss[s], Rss[s], junk, Rjunk)
        transpose_rows(C, xn[s], Rxn[s], 8, s, Rpb[s], xnT[:, :, blk * 128:(blk + 1) * 128], Res(), evac_eng="vector")
    return xnT


def load_w_cols(C, w_dram, c0, ncols, dst, stage, Rstage, eng="vector", deint=False):
    P = C.P
    P.op("sync", lambda e: e.dma_start(out=stage, in_=w_dram[:, c0:c0 + ncols].rearrange("(kc p) j -> p kc j", p=128)),
         writes=[Rstage], dma=True)
    if deint:
        src = stage.rearrange("p k (m two) -> p k two m", two=2)
        dstv = dst.rearrange("p k (two m) -> p k two m", two=2)
    else:
        src, dstv = stage, dst
    if eng == "vector":
        P.op("vector", lambda e: e.tensor_copy(out=dstv, in_=src), reads=[Rstage], writes=[Res()])
    else:
        P.op("scalar", lambda e: e.copy(out=dstv, in_=src), reads=[Rstage], writes=[Res()])


def proj_tok_phase(C, xnT, w_dram, c0, ncols, out_dram, func):
    P, A = C.P, C.A
    P.barrier()
    A.reset(C.base2)
    wb = A.alloc([128, 8, ncols], BF16)
    m0 = A.off
    stage = A.alloc([128, 8, 512], F32)
    Rstage = Res("st")
    for j in range(0, ncols, 512):
        load_w_cols(C, w_dram, c0 + j, 512, wb[:, :, j:j + 512], stage, Rstage, eng=("vector", "gpsimd")[(j // 512) % 2])
    P.barrier()
    A.reset(m0)
    ot = [A.alloc([128, ncols], BF16) for _ in range(2)]
    Rot = [Res("ot") for _ in range(2)]
    Rod = [Res("od") for _ in range(2)]
    Rpb = [Res("pb") for _ in range(8)]
    k = 0
    for blk in range(NB):
        s = blk % 2
        for j in range(0, ncols, 512):
            b = k % 4
            k += 1
            pb = bank(C, b)
            for kc in range(8):
                P.op("tensor", lambda e, pb=pb, kc=kc, blk=blk, j=j: e.matmul(pb, lhsT=xnT[:, kc, blk * 128:(blk + 1) * 128], rhs=wb[:, kc, j:j + 512],
                                                                          start=(kc == 0), stop=(kc == 7)), writes=[Rpb[b]])
            P.op("scalar", lambda e, pb=pb, s=s, j=j: e.activation(out=ot[s][:, j:j + 512], in_=pb, func=func), reads=[Rpb[b]], writes=[Rot[s]])
        P.op("gpsimd", lambda e, s=s, blk=blk: e.dma_start(out=out_dram[blk * 128:(blk + 1) * 128, :], in_=ot[s]), reads=[Rot[s]], writes=[Rod[s]], dma=True)


def outproj_phase(C, og_dram, nfeat, wo_dram, h_dram, Rh, prefetch=None):
    P, A = C.P, C.A
    P.barrier()
    A.reset(C.base)
    nkc = nfeat // 128
    wo = A.alloc([128, nkc, D], BF16)
    stages = [A.alloc([128, D], F32) for _ in range(2)]
    Rst = [Res("st") for _ in stages]
    load_cast_weight(C, wo_dram, nfeat, D, wo, None, stages, Rst, D)
    P.barrier()
    pieces = prefetch(stages, [Res(), Res()]) if prefetch else []
    og = [A.alloc([128, nfeat], BF16) for _ in range(2)]
    Rog = [Res("og") for _ in range(2)]
    ogT = [A.alloc([128, nkc, 128], BF16) for _ in range(2)]
    RogT = [Res("ogT") for _ in range(2)]
    ht = [A.alloc([128, D], F32) for _ in range(2)]
    Rht = [Res("ht") for _ in range(2)]
    Rpb = [Res("pb") for _ in range(8)]
    for blk in range(NB):
        s = blk % 2
        P.op("sync", lambda e, s=s, blk=blk: e.dma_start(out=og[s], in_=og_dram[blk * 128:(blk + 1) * 128, :]), writes=[Rog[s]], dma=True)
        P.op("sync", lambda e, s=s, blk=blk: e.dma_start(out=ht[s], in_=h_dram[blk * 128:(blk + 1) * 128, :]), reads=[Rh[blk]], writes=[Rht[s]], dma=True)
        for c8 in range(0, nkc, 8):
            n8 = min(8, nkc - c8)
            b = (blk * 2 + c8 // 8) % 2
            transpose_rows(C, og[s][:, c8 * 128:(c8 + n8) * 128], Rog[s], n8, b, Rpb[b], ogT[s][:, c8:c8 + n8, :], RogT[s],
                           evac_eng=("vector", "scalar")[(c8 // 8) % 2])
        for dh in range(2):
            b = 2 + (blk * 2 + dh) % 4
            pb = bank(C, b)
            for kc in range(nkc):
                P.op("tensor", lambda e, pb=pb, kc=kc, s=s, dh=dh: e.matmul(pb, lhsT=ogT[s][:, kc, :], rhs=wo[:, kc, dh * 512:(dh + 1) * 512],
                                                                          start=(kc == 0), stop=(kc == nkc - 1)), reads=[RogT[s]], writes=[Rpb[b]])
            P.op("vector", lambda e, pb=pb, s=s, dh=dh: e.tensor_tensor(out=ht[s][:, dh * 512:(dh + 1) * 512], in0=pb, in1=ht[s][:, dh * 512:(dh + 1) * 512], op=ALU.add),
                 reads=[Rpb[b], Rht[s]], writes=[Rht[s]])
        P.op("gpsimd", lambda e, s=s, blk=blk: e.dma_start(out=h_dram[blk * 128:(blk + 1) * 128, :], in_=ht[s]), reads=[Rht[s]], writes=[Rh[blk]], dma=True)
        emit_pieces(pieces, blk, NB)
    assert (not prefetch) or A.off <= FFN_TOP_OFF, A.off


def la_core(C, cfg, h, qe, ke, kdT, gl, v_dram, sg_dram, og_dram, maskT, gng, eps):
    P, A = C.P, C.A
    ndc, dv = cfg["ndc"], cfg["dv"]
    m0 = A.off
    S = [A.alloc([128, dv], F32) for _ in range(ndc)]
    Sb = [A.alloc([128, dv], BF16) for _ in range(ndc)]
    RS = [Res("S") for _ in range(ndc)]
    RSb = [Res("Sb") for _ in range(ndc)]
    vb = [A.alloc([128, dv], BF16) for _ in range(3)]
    Rvb = [Res("vb") for _ in range(3)]
    sgb = [A.alloc([128, dv], BF16) for _ in range(3)]
    Rsgb = [Res("sgb") for _ in range(3)]
    scm = [A.alloc([128, 128], BF16) for _ in range(2)]
    Rscm = [Res("scm") for _ in range(2)]
    st4 = [A.alloc([128, 8], F32) for _ in range(2)]
    Rst4 = [Res("st4") for _ in range(2)]
    junk = A.alloc([128, dv], BF16)
    Rjunk = Res("junk")
    tmp = [A.alloc([128, dv], F32) for _ in range(2)]
    Rtmp = [Res("tmp") for _ in range(2)]
    ogt = [A.alloc([128, dv], BF16) for _ in range(2)]
    Rogt = [Res("ogt") for _ in range(2)]
    Rogd = [Res("ogd") for _ in range(2)]
    Rpb = [Res("pb") for _ in range(8)]
    for dc in range(ndc):
        P.op("vector", lambda e, dc=dc: e.memset(S[dc], 0.0), writes=[RS[dc]])
        P.op("vector", lambda e, dc=dc: e.memset(Sb[dc], 0.0), writes=[RSb[dc]])
    post = _la_post_factory(C, cfg, h, og_dram, gng, eps, junk, Rjunk, st4, Rst4, tmp, Rtmp, ogt, Rogt, Rogd, sgb, Rsgb, Rpb)
    deferred = None
    for blk in range(NB):
        s2, s3 = blk % 2, blk % 3
        cs = slice(blk * 128, (blk + 1) * 128)
        for lb in ([0, 1] if blk == 0 else [blk + 1]):
            if lb < NB:
                DMA(P, "sync", vb[lb % 3], v_dram[lb * 128:(lb + 1) * 128, h * dv:(h + 1) * dv], [], [Rvb[lb % 3]])
                DMA(P, "sync", sgb[lb % 3], sg_dram[lb * 128:(lb + 1) * 128, h * dv:(h + 1) * dv], [], [Rsgb[lb % 3]])
        psc = bank(C, s2)[:, 0:128]
        for dc in range(ndc):
            P.op("tensor", lambda e, psc=psc, dc=dc, cs=cs: e.matmul(psc, lhsT=ke[dc][:, cs], rhs=qe[dc][:, cs], start=(dc == 0), stop=(dc == ndc - 1)),
                 writes=[Rpb[s2]])
        P.op("vector", lambda e, psc=psc, s2=s2: e.tensor_tensor(out=scm[s2], in0=psc, in1=maskT, op=ALU.mult), reads=[Rpb[s2]], writes=[Rscm[s2]])
        po = bank(C, 2 + s2)[:, 0:dv]
        P.op("tensor", lambda e, po=po, s2=s2, s3=s3: e.matmul(po, lhsT=scm[s2], rhs=vb[s3], start=True, stop=False),
             reads=[Rscm[s2], Rvb[s3]], writes=[Rpb[2 + s2]])
        for dc in range(ndc):
            P.op("tensor", lambda e, po=po, dc=dc, cs=cs: e.matmul(po, lhsT=qe[dc][:, cs], rhs=Sb[dc], start=False, stop=(dc == ndc - 1)),
                 reads=[RSb[dc]], writes=[Rpb[2 + s2]])
        for dc in range(ndc):
            b = 4 + (blk * ndc + dc) % 4
            pst = bank(C, b)[:, 0:dv]
            P.op("tensor", lambda e, pst=pst, dc=dc, blk=blk, s3=s3: e.matmul(pst, lhsT=kdT[:, blk, dc * 128:(dc + 1) * 128], rhs=vb[s3], start=True, stop=True),
                 reads=[Rvb[s3]], writes=[Rpb[b]])
            glv = gl if isinstance(gl, float) else gl[:, dc, blk:blk + 1]
            P.op("vector", lambda e, pst=pst, dc=dc, glv=glv: e.scalar_tensor_tensor(out=Sb[dc], in0=S[dc], scalar=glv, in1=pst, op0=ALU.mult, op1=ALU.add),
                 reads=[Rpb[b], RS[dc]], writes=[RSb[dc]])
            P.op("vector", lambda e, pst=pst, dc=dc, glv=glv: e.scalar_tensor_tensor(out=S[dc], in0=S[dc], scalar=glv, in1=pst, op0=ALU.mult, op1=ALU.add),
                 reads=[Rpb[b], RS[dc]], writes=[RS[dc]])
        if deferred is not None:
            deferred()
        deferred = (lambda blk=blk, s2=s2, s3=s3, po=po: post(blk, s2, s3, po))
    deferred()
    A.reset(m0)


def _la_post_factory(C, cfg, h, og_dram, gng, eps, junk, Rjunk, st4, Rst4, tmp, Rtmp, ogt, Rogt, Rogd, sgb, Rsgb, Rpb):
    P = C.P
    dv = cfg["dv"]

    def post(blk, s2, s3, po):
        if cfg["kind"] == "gla":
            P.op("scalar", lambda e, po=po, s2=s2: e.activation(out=junk, in_=po, func=AF.Square, accum_out=st4[s2][:, 0:1]),
                 reads=[Rpb[2 + s2]], writes=[Rjunk, Rst4[s2]])
            P.op("vector", lambda e, s2=s2: e.tensor_scalar(out=st4[s2][:, 1:2], in0=st4[s2][:, 0:1], scalar1=1.0 / dv, scalar2=eps, op0=ALU.mult, op1=ALU.add),
                 reads=[Rst4[s2]], writes=[Rst4[s2]])
            P.op("scalar", lambda e, s2=s2: e.activation(out=st4[s2][:, 1:2], in_=st4[s2][:, 1:2], func=AF.Sqrt), reads=[Rst4[s2]], writes=[Rst4[s2]])
            P.op("vector", lambda e, s2=s2: e.reciprocal(out=st4[s2][:, 1:2], in_=st4[s2][:, 1:2]), reads=[Rst4[s2]], writes=[Rst4[s2]])
            P.op("vector", lambda e, po=po, s2=s2: e.scalar_tensor_tensor(out=tmp[s2], in0=po, scalar=st4[s2][:, 1:2], in1=gng, op0=ALU.mult, op1=ALU.mult),
                 reads=[Rpb[2 + s2], Rst4[s2]], writes=[Rtmp[s2]])
        else:
            P.op("scalar", lambda e, po=po, s2=s2: e.activation(out=junk, in_=po, func=AF.Square, accum_out=st4[s2][:, 0:1]),
                 reads=[Rpb[2 + s2]], writes=[Rjunk, Rst4[s2]])
            P.op("scalar", lambda e, po=po, s2=s2: e.activation(out=junk, in_=po, func=AF.Identity, accum_out=st4[s2][:, 2:3]),
                 reads=[Rpb[2 + s2]], writes=[Rjunk, Rst4[s2]])
            P.op("vector", lambda e, s2=s2: e.tensor_scalar(out=st4[s2][:, 3:4], in0=st4[s2][:, 2:3], scalar1=1.0 / dv, scalar2=None, op0=ALU.mult),
                 reads=[Rst4[s2]], writes=[Rst4[s2]])
            P.op("vector", lambda e, s2=s2: e.tensor_tensor(out=st4[s2][:, 4:5], in0=st4[s2][:, 3:4], in1=st4[s2][:, 3:4], op=ALU.mult),
                 reads=[Rst4[s2]], writes=[Rst4[s2]])
            P.op("vector", lambda e, s2=s2: e.scalar_tensor_tensor(out=st4[s2][:, 1:2], in0=st4[s2][:, 0:1], scalar=1.0 / dv, in1=st4[s2][:, 4:5], op0=ALU.mult, op1=ALU.subtract),
                 reads=[Rst4[s2]], writes=[Rst4[s2]])
            P.op("vector", lambda e, s2=s2: e.tensor_scalar(out=st4[s2][:, 1:2], in0=st4[s2][:, 1:2], scalar1=eps, scalar2=None, op0=ALU.add),
                 reads=[Rst4[s2]], writes=[Rst4[s2]])
            P.op("scalar", lambda e, s2=s2: e.activation(out=st4[s2][:, 1:2], in_=st4[s2][:, 1:2], func=AF.Sqrt), reads=[Rst4[s2]], writes=[Rst4[s2]])
            P.op("vector", lambda e, s2=s2: e.reciprocal(out=st4[s2][:, 1:2], in_=st4[s2][:, 1:2]), reads=[Rst4[s2]], writes=[Rst4[s2]])
            P.op("vector", lambda e, po=po, s2=s2: e.tensor_scalar(out=tmp[s2], in0=po, scalar1=st4[s2][:, 3:4], scalar2=st4[s2][:, 1:2], op0=ALU.subtract, op1=ALU.mult),
                 reads=[Rpb[2 + s2], Rst4[s2]], writes=[Rtmp[s2]])
        P.op("gpsimd", lambda e, s2=s2, s3=s3: e.tensor_tensor(out=ogt[s2], in0=tmp[s2], in1=sgb[s3], op=ALU.mult),
             reads=[Rtmp[s2], Rsgb[s3]], writes=[Rogt[s2]])
        P.op("sync", lambda e, s2=s2, blk=blk: e.dma_start(out=og_dram[blk * 128:(blk + 1) * 128, h * dv:(h + 1) * dv], in_=ogt[s2]),
             reads=[Rogt[s2]], writes=[Rogd[s2]], dma=True)
    return post


def gla_mixer(C, h_dram, Rh, g_dram, W, SC, prefetch=None):
    P, A = C.P, C.A
    H, DK, DV = 4, 128, 256
    cfg = dict(ndc=1, dv=DV, kind="gla")
    xnT = xnT_all_phase(C, h_dram, Rh, g_dram)
    proj_tok_phase(C, xnT, W["w_in"], 2 * H * DK, H * DV, SC["v"], AF.Copy)
    proj_tok_phase(C, xnT, W["w_in"], 2 * H * DK + H * DV, H * DV, SC["sg"], AF.Silu)
    P.barrier()
    A.reset(C.base2)
    maskT = A.alloc([128, 128], F32)
    gng = A.alloc([128, DV], F32)
    m01 = A.alloc([128, 1024], F32)
    zw = A.alloc([128, 8, 16], BF16)
    zwst = A.alloc([128, 8, 16], F32)
    wgu = A.alloc([16, 512], BF16)
    wgust = A.alloc([16, 512], F32)
    zT = A.alloc([16, T], BF16)
    Rc = [Res("c") for _ in range(8)]
    P.op("sync", lambda e: e.dma_start(out=maskT, in_=W["maskT"]), writes=[Rc[0]], dma=True)
    P.op("sync", lambda e: e.dma_start(out=gng, in_=bcast_row(W["norm_g"], DV)), writes=[Rc[1]], dma=True)
    P.op("vector", lambda e: e.memset(m01, 1.0), writes=[Rc[2]])
    P.op("vector", lambda e: e.memset(m01.rearrange("p (a b) -> p a b", b=128)[:, :, 0:1], 0.0), writes=[Rc[2]])
    load_w_cols(C, W["w_in"], 2 * H * DK + 2 * H * DV, 16, zw, zwst, Rc[3])
    P.op("sync", lambda e: e.dma_start(out=wgust, in_=W["w_gate_up"]), writes=[Rc[4]], dma=True)
    P.op("vector", lambda e: e.tensor_copy(out=wgu, in_=wgust), reads=[Rc[4]], writes=[Rc[4]])
    P.barrier()
    Rpb = [Res("pb") for _ in range(8)]
    for j in range(T // 512):
        b = j % 2
        pz = bank(C, b)[0:16, :]
        for kc in range(8):
            P.op("tensor", lambda e, pz=pz, kc=kc, j=j: e.matmul(pz, lhsT=zw[:, kc, :], rhs=xnT[:, kc, j * 512:(j + 1) * 512], start=(kc == 0), stop=(kc == 7)),
                 writes=[Rpb[b]])
        P.op("vector", lambda e, pz=pz, j=j: e.tensor_copy(out=zT[:, j * 512:(j + 1) * 512], in_=pz), reads=[Rpb[b]], writes=[Res()])
    mh = A.off
    for h in range(H):
        P.barrier()
        A.reset(mh)
        wqk = A.alloc([128, 8, 256], BF16)
        wst = A.alloc([128, 8, 128], F32)
        Rwst = Res("wst")
        nb = A.alloc([128, 1], F32)
        Rnb = Res("nb")
        load_w_cols(C, W["w_in"], h * DK, 128, wqk[:, :, 0:128], wst, Rwst)
        load_w_cols(C, W["w_in"], H * DK + h * DK, 128, wqk[:, :, 128:256], wst, Rwst)
        P.op("sync", lambda e, h=h: e.dma_start(out=nb, in_=W["b_gate"][h * 128:(h + 1) * 128].rearrange("(p o) -> p o", o=1)), writes=[Rnb], dma=True)
        P.op("vector", lambda e: e.tensor_scalar(out=nb, in0=nb, scalar1=-1.0, scalar2=None, op0=ALU.mult), reads=[Rnb], writes=[Rnb])
        qe = [A.alloc([128, T], BF16)]
        ke = [A.alloc([128, T], BF16)]
        kdT = A.alloc([128, NB, 128], BF16)
        gl = A.alloc([128, 1, NB], F32)
        P.barrier()
        e1 = [A.alloc([128, 1024], F32) for _ in range(2)]
        Re1 = [Res("e1") for _ in range(2)]
        cum = [A.alloc([128, 1024], F32) for _ in range(2)]
        Rcum = [Res("cum") for _ in range(2)]
        ed = A.alloc([128, 1024], F32)
        Red = Res("ed")
        eu = A.alloc([128, 1024], F32)
        Reu = Res("eu")
        kdw = A.alloc([128, 1024], BF16)
        Rkdw = Res("kdw")
        Rgl = Res("gl")
        for tp in range(T // 1024):
            s = tp % 2
            pqs, pks = [], []
            for half in range(2):
                cols = slice(tp * 1024 + half * 512, tp * 1024 + (half + 1) * 512)
                bq, bk, bl = half, 2 + half, 4 + half
                pq, pk, pl = bank(C, bq), bank(C, bk), bank(C, bl)
                pqs.append((pq, bq)); pks.append((pk, bk))
                for kc in range(8):
                    P.op("tensor", lambda e, pq=pq, kc=kc, cols=cols: e.matmul(pq, lhsT=wqk[:, kc, 0:128], rhs=xnT[:, kc, cols], start=(kc == 0), stop=(kc == 7)),
                         writes=[Rpb[bq]])
                for kc in range(8):
                    P.op("tensor", lambda e, pk=pk, kc=kc, cols=cols: e.matmul(pk, lhsT=wqk[:, kc, 128:256], rhs=xnT[:, kc, cols], start=(kc == 0), stop=(kc == 7)),
                         writes=[Rpb[bk]])
                P.op("tensor", lambda e, pl=pl, cols=cols, h=h: e.matmul(pl, lhsT=wgu[:, h * 128:(h + 1) * 128], rhs=zT[:, cols], start=True, stop=True),
                     writes=[Rpb[bl]])
                P.op("scalar", lambda e, pl=pl, s=s, half=half: e.activation(out=e1[s][:, half * 512:(half + 1) * 512], in_=pl, func=AF.Exp, bias=nb, scale=-1.0),
                     reads=[Rpb[bl], Rnb], writes=[Re1[s]])
            P.op("scalar", lambda e, s=s: e.activation(out=e1[s], in_=e1[s], func=AF.Ln, bias=1.0, scale=1.0), reads=[Re1[s]], writes=[Re1[s]])
            P.op("vector", lambda e, s=s: e.tensor_tensor_scan(out=cum[s], data0=m01, data1=e1[s], initial=0.0, op0=ALU.mult, op1=ALU.add),
                 reads=[Re1[s]], writes=[Rcum[s]])
            P.op("scalar", lambda e, s=s: e.activation(out=ed, in_=cum[s], func=AF.Exp, scale=-1.0 / 16), reads=[Rcum[s]], writes=[Red])
            for half in range(2):
                pq, bq = pqs[half]
                P.op("vector", lambda e, pq=pq, half=half, tp=tp: e.scalar_tensor_tensor(out=qe[0][:, tp * 1024 + half * 512:tp * 1024 + (half + 1) * 512], in0=pq,
                                                                                       scalar=float(DK) ** -0.5, in1=ed[:, half * 512:(half + 1) * 512], op0=ALU.mult, op1=ALU.mult),
                     reads=[Rpb[bq], Red], writes=[Res()])
            P.op("scalar", lambda e, s=s: e.activation(out=eu, in_=cum[s], func=AF.Exp, scale=1.0 / 16), reads=[Rcum[s]], writes=[Reu])
            for half in range(2):
                pk, bk = pks[half]
                P.op("vector", lambda e, pk=pk, half=half: e.tensor_tensor(out=eu[:, half * 512:(half + 1) * 512], in0=pk, in1=eu[:, half * 512:(half + 1) * 512], op=ALU.mult),
                     reads=[Rpb[bk], Reu], writes=[Reu])
            P.op("gpsimd", lambda e, tp=tp: e.tensor_copy(out=ke[0][:, tp * 1024:(tp + 1) * 1024], in_=eu), reads=[Reu], writes=[Res()])
            P.op("scalar", lambda e, s=s, tp=tp: e.activation(out=gl[:, 0, tp * 8:(tp + 1) * 8], in_=cum[s].rearrange("p (a b) -> p a b", b=128)[:, :, 127],
                                                              func=AF.Exp, scale=-1.0 / 16), reads=[Rcum[s]], writes=[Rgl])
            P.op("vector", lambda e, tp=tp: e.tensor_tensor(out=kdw.rearrange("p (a b) -> p a b", b=128), in0=eu.rearrange("p (a b) -> p a b", b=128),
                                                            in1=gl[:, 0, tp * 8:(tp + 1) * 8].unsqueeze(2).to_broadcast([128, 8, 128]), op=ALU.mult),
                 reads=[Reu, Rgl], writes=[Rkdw])
            bt = 6 + tp % 2
            pt = bank(C, bt, BF16)
            for j in range(8):
                P.op("tensor", lambda e, pt=pt, j=j: e.transpose(out=pt[:, j * 128:(j + 1) * 128], in_=kdw[:, j * 128:(j + 1) * 128], identity=C.ident_b),
                     reads=[Rkdw], writes=[Rpb[bt]])
            P.op("vector", lambda e, pt=pt, tp=tp: e.tensor_copy(out=kdT[:, tp * 8:(tp + 1) * 8, :], in_=pt.rearrange("p (a b) -> p a b", b=128)),
                 reads=[Rpb[bt]], writes=[Res()])
        P.barrier()
        la_core(C, cfg, h, qe, ke, kdT, gl, SC["v"], SC["sg"], SC["og"], maskT, gng, EPS)
    outproj_phase(C, SC["og"], H * DV, W["w_out"], h_dram, Rh, prefetch=prefetch)


def sincos_tables(C, pos_dram, invt, cosT, sinT, work_mark):
    P, A = C.P, C.A
    A.reset(work_mark)
    PI2 = 6.28318
    PI1 = 3.14159
    pi_ = [A.alloc([128, 1024], I32) for _ in range(2)]
    a = [A.alloc([128, 1024], F32) for _ in range(2)]
    ki = [A.alloc([128, 1024], I32) for _ in range(2)]
    R1 = [Res() for _ in range(2)]; R2 = [Res() for _ in range(2)]; R3 = [Res() for _ in range(2)]
    for tp in range(T // 1024):
        s = tp % 2
        cs = slice(tp * 1024, (tp + 1) * 1024)
        P.op("sync", lambda e, s=s, cs=cs: e.dma_start(out=pi_[s], in_=pos_dram[cs].rearrange("(o n) -> o n", o=1).to_broadcast([128, 1024])),
             writes=[R1[s]], dma=True)
        P.op("vector", lambda e, s=s: e.tensor_scalar(out=a[s], in0=pi_[s], scalar1=invt, scalar2=None, op0=ALU.mult), reads=[R1[s]], writes=[R2[s]])
        P.op("vector", lambda e, s=s: e.tensor_copy(out=ki[s], in_=a[s]), reads=[R2[s]], writes=[R3[s]])
        P.op("vector", lambda e, s=s: e.tensor_tensor(out=a[s], in0=a[s], in1=ki[s], op=ALU.subtract), reads=[R2[s], R3[s]], writes=[R2[s]])
        P.op("vector", lambda e, s=s: e.scalar_tensor_tensor(out=a[s], in0=a[s], scalar=0.5, in1=a[s], op0=ALU.is_gt, op1=ALU.subtract),
             reads=[R2[s]], writes=[R2[s]])
        P.op("scalar", lambda e, s=s, cs=cs: e.activation(out=sinT[:, cs], in_=a[s], func=AF.Sin, scale=-PI2), reads=[R2[s]], writes=[Res()])
        P.op("scalar", lambda e, s=s, cs=cs: e.activation(out=cosT[:, cs], in_=a[s], func=AF.Sin, scale=-PI1), reads=[R2[s]], writes=[R3[s]])
        P.op("scalar", lambda e, s=s, cs=cs: e.activation(out=cosT[:, cs], in_=cosT[:, cs], func=AF.Square), reads=[R3[s]], writes=[R3[s]])
        P.op("vector", lambda e, s=s, cs=cs: e.tensor_scalar(out=cosT[:, cs], in0=cosT[:, cs], scalar1=-2.0, scalar2=1.0, op0=ALU.mult, op1=ALU.add),
             reads=[R3[s]], writes=[R3[s]])
    P.barrier()
    A.reset(work_mark)


def ret_mixer(C, h_dram, Rh, g_dram, W, SC, prefetch=None):
    P, A = C.P, C.A
    H, DV = 4, 512
    cfg = dict(ndc=2, dv=DV, kind="ret")
    xnT = xnT_all_phase(C, h_dram, Rh, g_dram)
    proj_tok_phase(C, xnT, W["w_in"], 2048, H * DV, SC["v"], AF.Copy)
    proj_tok_phase(C, xnT, W["w_in"], 4096, H * DV, SC["sg"], AF.Silu)
    P.barrier()
    A.reset(C.base2)
    maskT = A.alloc([128, 128], F32)
    invt = A.alloc([128, 1], F32)
    cosT = A.alloc([128, T], F32)
    sinT = A.alloc([128, T], F32)
    Rc = [Res("c") for _ in range(4)]
    P.op("sync", lambda e: e.dma_start(out=maskT, in_=W["maskT"]), writes=[Rc[0]], dma=True)
    P.op("sync", lambda e: e.dma_start(out=invt, in_=W["invt"]), writes=[Rc[1]], dma=True)
    P.barrier()
    mh = A.off
    sincos_tables(C, W["positions"], invt, cosT, sinT, mh)
    Rpb = [Res("pb") for _ in range(8)]
    for h in range(H):
        P.barrier()
        A.reset(mh)
        qe = [A.alloc([128, T], BF16) for _ in range(2)]
        ke = [A.alloc([128, T], BF16) for _ in range(2)]
        kdT = A.alloc([128, NB, 256], BF16)
        tab = A.alloc([128, 3, 128], F32)
        mcore = A.off
        wq = A.alloc([128, 8, 256], BF16)
        wk = A.alloc([128, 8, 256], BF16)
        wst = A.alloc([128, 8, 256], F32)
        Rwst = Res("wst")
        Rtab = Res("tab")
        load_w_cols(C, W["w_in"], h * 256, 256, wq, wst, Rwst, deint=True)
        load_w_cols(C, W["w_in"], 1024 + h * 256, 256, wk, wst, Rwst, deint=True)
        P.op("sync", lambda e, h=h: e.dma_start(out=tab.rearrange("p a b -> p (a b)"), in_=bcast_row(W["tab"][h], 384)), writes=[Rtab], dma=True)
        P.barrier()
        tw = [[A.alloc([128, 512], F32) for _ in range(6)] for _ in range(2)]
        Rtw = [[Res() for _ in range(6)] for _ in range(2)]
        kdw = [A.alloc([128, 512], BF16) for _ in range(2)]
        Rkdw = [Res() for _ in range(2)]
        for j in range(T // 512):
            cols = slice(j * 512, (j + 1) * 512)
            b0 = (j % 2) * 4
            for qk, wmat in ((0, wq), (1, wk)):
                for eo in range(2):
                    b = b0 + qk * 2 + eo
                    pb = bank(C, b)
                    for kc in range(8):
                        P.op("tensor", lambda e, pb=pb, kc=kc, wmat=wmat, eo=eo, cols=cols: e.matmul(pb, lhsT=wmat[:, kc, eo * 128:(eo + 1) * 128], rhs=xnT[:, kc, cols],
                                                                                                 start=(kc == 0), stop=(kc == 7)), writes=[Rpb[b]])
            for qk in range(2):
                pe_, po_ = bank(C, b0 + qk * 2), bank(C, b0 + qk * 2 + 1)
                Re_, Ro_ = Rpb[b0 + qk * 2], Rpb[b0 + qk * 2 + 1]
                t = tw[qk]; Rt = Rtw[qk]
                P.op("vector", lambda e, pe_=pe_, t=t, cols=cols: e.tensor_tensor(out=t[0], in0=pe_, in1=cosT[:, cols], op=ALU.mult), reads=[Re_], writes=[Rt[0]])
                P.op("vector", lambda e, po_=po_, t=t, cols=cols: e.tensor_tensor(out=t[1], in0=po_, in1=sinT[:, cols], op=ALU.mult), reads=[Ro_], writes=[Rt[1]])
                P.op("vector", lambda e, po_=po_, t=t, cols=cols: e.tensor_tensor(out=t[2], in0=po_, in1=cosT[:, cols], op=ALU.mult), reads=[Ro_], writes=[Rt[2]])
                P.op("vector", lambda e, pe_=pe_, t=t, cols=cols: e.tensor_tensor(out=t[3], in0=pe_, in1=sinT[:, cols], op=ALU.mult), reads=[Re_], writes=[Rt[3]])
                P.op("vector", lambda e, t=t: e.tensor_tensor(out=t[4], in0=t[0], in1=t[1], op=ALU.subtract), reads=[Rt[0], Rt[1]], writes=[Rt[4]])
                P.op("vector", lambda e, t=t: e.tensor_tensor(out=t[5], in0=t[2], in1=t[3], op=ALU.add), reads=[Rt[2], Rt[3]], writes=[Rt[5]])
                v3 = lambda ap: ap.rearrange("p (a b) -> p a b", b=128)
                tb_ = lambda i: tab[:, i, :].unsqueeze(1).to_broadcast([128, 4, 128])
                if qk == 0:
                    for dc in range(2):
                        P.op("gpsimd", lambda e, t=t, dc=dc, cols=cols: e.tensor_tensor(out=v3(qe[dc][:, cols]), in0=v3(t[4 + dc]), in1=tb_(0), op=ALU.mult),
                             reads=[Rt[4 + dc], Rtab], writes=[Res()])
                else:
                    for dc in range(2):
                        P.op("gpsimd", lambda e, t=t, dc=dc, cols=cols: e.tensor_tensor(out=v3(ke[dc][:, cols]), in0=v3(t[4 + dc]), in1=tb_(1), op=ALU.mult),
                             reads=[Rt[4 + dc], Rtab], writes=[Res()])
                        P.op("gpsimd", lambda e, t=t, dc=dc: e.tensor_tensor(out=v3(kdw[dc]), in0=v3(t[4 + dc]), in1=tb_(2), op=ALU.mult),
                             reads=[Rt[4 + dc], Rtab], writes=[Rkdw[dc]])
            pt = bank(C, b0, BF16)
            for blk in range(4):
                for dc in range(2):
                    P.op("tensor", lambda e, pt=pt, blk=blk, dc=dc: e.transpose(out=pt[:, blk * 256 + dc * 128: blk * 256 + (dc + 1) * 128],
                                                                            in_=kdw[dc][:, blk * 128:(blk + 1) * 128], identity=C.ident_b),
                         reads=[Rkdw[dc]], writes=[Rpb[b0]])
            P.op("scalar", lambda e, pt=pt, j=j: e.copy(out=kdT[:, j * 4:(j + 1) * 4, :], in_=pt.rearrange("p (a b) -> p a b", b=256)),
                 reads=[Rpb[b0]], writes=[Res()])
        P.barrier()
        A.reset(mcore)
        la_core(C, cfg, h, qe, ke, kdT, W["gl"][h], SC["v"], SC["sg"], SC["og"], maskT, None, 1e-5)
    outproj_phase(C, SC["og"], H * DV, W["w_out"], h_dram, Rh, prefetch=prefetch)


PI2 = 6.28318
PI1 = 3.14159
LP = 512


def range_reduce_sincos(P, a, ki, R, sin_out, cos_out):
    P.op("vector", lambda e: e.tensor_copy(out=ki, in_=a), reads=[R], writes=[R])
    P.op("vector", lambda e: e.tensor_tensor(out=a, in0=a, in1=ki, op=ALU.subtract), reads=[R], writes=[R])
    P.op("vector", lambda e: e.scalar_tensor_tensor(out=a, in0=a, scalar=0.5, in1=a, op0=ALU.is_gt, op1=ALU.subtract), reads=[R], writes=[R])
    P.op("scalar", lambda e: e.activation(out=sin_out, in_=a, func=AF.Sin, scale=-PI2), reads=[R], writes=[R])
    P.op("scalar", lambda e: e.activation(out=cos_out, in_=a, func=AF.Sin, scale=-PI1), reads=[R], writes=[R])
    P.op("scalar", lambda e: e.activation(out=cos_out, in_=cos_out, func=AF.Square), reads=[R], writes=[R])
    P.op("vector", lambda e: e.tensor_scalar(out=cos_out, in0=cos_out, scalar1=-2.0, scalar2=1.0, op0=ALU.mult, op1=ALU.add), reads=[R], writes=[R])


def s5_mixer(C, h_dram, Rh, g_dram, W, SC):
    P, A = C.P, C.A
    xnT = xnT_all_phase(C, h_dram, Rh, g_dram)
    P.barrier()
    A.reset(C.base2)
    iota = A.alloc([128, LP + 1], F32)
    msk2 = A.alloc([128, 2], F32)
    BdT = [A.alloc([128, 32, 128], BF16) for _ in range(2)]
    CdT = [A.alloc([128, 8, 128], BF16) for _ in range(3)]
    mag = A.alloc([128, 32], F32)
    trn = A.alloc([128, 32], F32)
    mperm = A.off
    X32 = A.alloc([32, 3, 128], F32)
    ldt = A.alloc([32, 2], F32)
    par = A.alloc([128, 3, 32], F32)
    Rp = Res("par")
    Rpb = [Res("pb") for _ in range(8)]
    P.op("sync", lambda e: e.dma_start(out=iota, in_=W["iota"]), writes=[Res()], dma=True)
    P.op("sync", lambda e: e.dma_start(out=msk2, in_=W["msk2"]), writes=[Res()], dma=True)
    P.op("sync", lambda e: e.dma_start(out=X32[:, 0, :], in_=W["lam_re"].rearrange("(k two) p -> k (two p)", two=2)), writes=[Res()], dma=True)
    P.op("sync", lambda e: e.dma_start(out=X32[:, 1, :], in_=W["lam_im"].rearrange("(k two) p -> k (two p)", two=2)), writes=[Res()], dma=True)
    P.op("sync", lambda e: e.dma_start(out=ldt, in_=W["log_dt"].rearrange("(k two) -> k two", two=2)), writes=[Res()], dma=True)
    P.barrier()
    P.op("vector", lambda e: e.tensor_copy(out=X32[:, 2, :].rearrange("k (two p) -> k two p", two=2), in_=ldt.unsqueeze(2).to_broadcast([32, 2, 64])),
         writes=[Rp])
    pp = bank(C, 0)
    for j in range(3):
        P.op("tensor", lambda e, j=j: e.transpose(out=pp[:, j * 32:(j + 1) * 32], in_=X32[:, j, :], identity=C.ident_f[0:32, 0:32]), reads=[Rp], writes=[Rpb[0]])
    P.op("vector", lambda e: e.tensor_copy(out=par.rearrange("p a b -> p (a b)"), in_=pp[:, 0:96]), reads=[Rpb[0]], writes=[Rp])
    sm = [A.alloc([128, 32], F32) for _ in range(12)]
    smi = A.alloc([128, 32], I32)
    lre, lim = par[:, 0, :], par[:, 1, :]
    dt_, xr_, th_, sn, cs, lbr, lbi, den, um, fre, fim, tmp_ = sm
    V = lambda fn: P.op("vector", fn, reads=[Rp], writes=[Rp])
    S_ = lambda fn: P.op("scalar", fn, reads=[Rp], writes=[Rp])
    S_(lambda e: e.activation(out=dt_, in_=par[:, 2, :], func=AF.Exp))
    V(lambda e: e.tensor_tensor(out=xr_, in0=lre, in1=dt_, op=ALU.mult))
    V(lambda e: e.tensor_tensor(out=th_, in0=lim, in1=dt_, op=ALU.mult))
    S_(lambda e: e.activation(out=mag, in_=xr_, func=AF.Exp))
    V(lambda e: e.tensor_scalar(out=trn, in0=th_, scalar1=1.0 / (2 * np.pi), scalar2=None, op0=ALU.mult))
    V(lambda e: e.tensor_copy(out=th_, in_=trn))
    range_reduce_sincos(P, th_, smi, Rp, sn, cs)
    V(lambda e: e.tensor_tensor(out=lbr, in0=mag, in1=cs, op=ALU.mult))
    V(lambda e: e.tensor_tensor(out=lbi, in0=mag, in1=sn, op=ALU.mult))
    V(lambda e: e.tensor_tensor(out=den, in0=lre, in1=lre, op=ALU.mult))
    V(lambda e: e.tensor_tensor(out=tmp_, in0=lim, in1=lim, op=ALU.mult))
    V(lambda e: e.tensor_tensor(out=den, in0=den, in1=tmp_, op=ALU.add))
    V(lambda e: e.reciprocal(out=den, in_=den))
    V(lambda e: e.tensor_scalar(out=um, in0=lbr, scalar1=-1.0, scalar2=None, op0=ALU.add))
    V(lambda e: e.tensor_tensor(out=fre, in0=um, in1=lre, op=ALU.mult))
    V(lambda e: e.tensor_tensor(out=tmp_, in0=lbi, in1=lim, op=ALU.mult))
    V(lambda e: e.tensor_tensor(out=fre, in0=fre, in1=tmp_, op=ALU.add))
    V(lambda e: e.tensor_tensor(out=fre, in0=fre, in1=den, op=ALU.mult))
    V(lambda e: e.tensor_tensor(out=fim, in0=lbi, in1=lre, op=ALU.mult))
    V(lambda e: e.tensor_tensor(out=tmp_, in0=um, in1=lim, op=ALU.mult))
    V(lambda e: e.tensor_tensor(out=fim, in0=fim, in1=tmp_, op=ALU.subtract))
    V(lambda e: e.tensor_tensor(out=fim, in0=fim, in1=den, op=ALU.mult))
    bre = A.alloc([128, 32, 16], F32)
    bim = A.alloc([128, 32, 16], F32)
    bbr = A.alloc([128, 32, 16], F32)
    bbi = A.alloc([128, 32, 16], F32)
    t16 = A.alloc([128, 32, 16], F32)
    Rb = Res("b")
    P.op("sync", lambda e: e.dma_start(out=bre, in_=W["b_re"].rearrange("(k two) p c -> (two p) k c", two=2)), writes=[Res()], dma=True)
    P.op("sync", lambda e: e.dma_start(out=bim, in_=W["b_im"].rearrange("(k two) p c -> (two p) k c", two=2)), writes=[Res()], dma=True)
    P.barrier()
    fb = lambda f: f.unsqueeze(2).to_broadcast([128, 32, 16])
    V(lambda e: e.tensor_tensor(out=bbr, in0=bre, in1=fb(fre), op=ALU.mult))
    V(lambda e: e.tensor_tensor(out=t16, in0=bim, in1=fb(fim), op=ALU.mult))
    V(lambda e: e.tensor_tensor(out=bbr, in0=bbr, in1=t16, op=ALU.subtract))
    V(lambda e: e.tensor_tensor(out=bbi, in0=bim, in1=fb(fre), op=ALU.mult))
    V(lambda e: e.tensor_tensor(out=t16, in0=bre, in1=fb(fim), op=ALU.mult))
    V(lambda e: e.tensor_tensor(out=bbi, in0=bbi, in1=t16, op=ALU.add))
    Mall = [A.alloc([128, 8, 128], F32) for _ in range(2)]
    for ri, bb in enumerate((bbr, bbi)):
        V(lambda e, ri=ri: e.memset(Mall[ri], 0.0))
        for two in range(2):
            ps_ = slice(two * 64, (two + 1) * 64)
            dstv = Mall[ri][ps_].rearrange("p kq (kr tw c) -> p kq kr tw c", kr=4, tw=2)[:, :, :, two, :]
            srcv = bb[ps_].rearrange("p (kq kr) c -> p kq kr c", kr=4)
            V(lambda e, dstv=dstv, srcv=srcv: e.tensor_copy(out=dstv, in_=srcv))
    Ct = [A.alloc([128, 8, 64], F32) for _ in range(2)]
    Cx = [A.alloc([128, 8, 128], F32) for _ in range(2)]
    P.op("sync", lambda e: e.dma_start(out=Ct[0], in_=W["c_re"].rearrange("(kq kr two) i p -> (kr two i) kq p", kr=4, two=2)), writes=[Res()], dma=True)
    P.op("sync", lambda e: e.dma_start(out=Ct[1], in_=W["c_im"].rearrange("(kq kr two) i p -> (kr two i) kq p", kr=4, two=2)), writes=[Res()], dma=True)
    P.barrier()
    for ri in range(2):
        for two in range(2):
            dstv = Cx[ri].rearrange("p kq (two q) -> p kq two q", two=2)[:, :, two, :]
            if ri == 0:
                V(lambda e, dstv=dstv, two=two: e.tensor_scalar(out=dstv, in0=Ct[0], scalar1=msk2[:, two:two + 1], scalar2=None, op0=ALU.mult))
            else:
                V(lambda e, dstv=dstv, two=two: e.tensor_scalar(out=dstv, in0=Ct[1], scalar1=msk2[:, two:two + 1], scalar2=-1.0, op0=ALU.mult, op1=ALU.mult))
    Mz = [A.alloc([128, 32, 128], F32) for _ in range(2)]
    for ri in range(2):
        V(lambda e, ri=ri: e.memset(Mz[ri], 0.0))
        for kr in range(4):
            dstv = Mz[ri].rearrange("p (kq kr) x -> p kq kr x", kr=4)[:, :, kr, kr * 32:(kr + 1) * 32]
            srcv = Mall[ri][:, :, kr * 32:(kr + 1) * 32]
            V(lambda e, dstv=dstv, srcv=srcv: e.tensor_copy(out=dstv, in_=srcv))
    n = 0
    for src, dst, ng in ((Mz[0], BdT[0], 8), (Mz[1], BdT[1], 8), (Cx[0], CdT[0], 2), (Cx[1], CdT[1], 2)):
        for half in range(ng):
            b = 1 + n % 4
            n += 1
            pb = bank(C, b)
            for q in range(4):
                kq = half * 4 + q
                P.op("tensor", lambda e, pb=pb, q=q, kq=kq, src=src: e.transpose(out=pb[:, q * 128:(q + 1) * 128], in_=src[:, kq, :], identity=C.ident_f),
                     reads=[Rp], writes=[Rpb[b]])
            P.op("vector", lambda e, pb=pb, dst=dst, half=half: e.tensor_copy(out=dst[:, half * 4:(half + 1) * 4, :], in_=pb.rearrange("p (a b) -> p a b", b=128)),
                 reads=[Rpb[b]], writes=[Res()])
    P.barrier()
    TS(P, "vector", CdT[2], CdT[0], -1.0, None, ALU.mult, None, [], [Res()])
    P.barrier()
    A.reset(mperm)
    cT = [A.alloc([128, LP + 1], F32) for _ in range(2)]
    sT = [A.alloc([128, LP + 1], F32) for _ in range(2)]
    rfull = [A.alloc([128, LP], F32) for _ in range(2)]
    aT = A.alloc([128, LP + 1], F32)
    kiT = A.alloc([128, LP + 1], I32)
    Rtab = [Res("tab") for _ in range(2)]
    Rsc = Res("tabscratch")
    NS = 3
    tq = [[A.alloc([128, LP], F32) for _ in range(4)] for _ in range(NS)]
    Rtq = [[Res() for _ in range(4)] for _ in range(NS)]
    zq = [[A.alloc([128, LP], F32) for _ in range(2)] for _ in range(NS)]
    Rzq = [[Res() for _ in range(2)] for _ in range(NS)]
    wq = [[A.alloc([128, LP], F32) for _ in range(2)] for _ in range(NS)]
    Rwq = [[Res() for _ in range(2)] for _ in range(NS)]
    uq = [[A.alloc([128, LP], BF16) for _ in range(4)] for _ in range(2)]
    Ruq = [[Res() for _ in range(4)] for _ in range(2)]
    xb = [[A.alloc([128, LP], BF16) for _ in range(2)] for _ in range(2)]
    Rxb = [[Res() for _ in range(2)] for _ in range(2)]
    ini = [A.alloc([128, 4], F32) for _ in range(NS)]
    Rini = [Res() for _ in range(NS)]
    ysb = [A.alloc([128, LP], F32) for _ in range(2)]
    Rysb = [Res() for _ in range(2)]
    Ryd = [Res() for _ in range(2)]
    npc = T // LP
    NIT = 32 * npc

    def tables(k):
        tb = k % 2
        TS(P, "vector", aT, iota, trn[:, k:k + 1], None, ALU.mult, None, [Rsc], [Rsc])
        CP(P, "vector", kiT, aT, [Rsc], [Rsc])
        TT(P, "vector", aT, aT, kiT, ALU.subtract, [Rsc], [Rsc])
        STT(P, aT, aT, 0.5, aT, ALU.is_gt, ALU.subtract, [Rsc], [Rsc])
        ACT(P, sT[tb], aT, AF.Sin, [Rsc], [Rtab[tb]], scale=-PI2)
        ACT(P, cT[tb], aT, AF.Sin, [Rsc], [Rtab[tb]], scale=-PI1)
        ACT(P, cT[tb], cT[tb], AF.Square, [Rtab[tb]], [Rtab[tb]])
        TS(P, "vector", cT[tb], cT[tb], -2.0, 1.0, ALU.mult, ALU.add, [Rtab[tb]], [Rtab[tb]])
        TS(P, "vector", rfull[tb], iota[:, 0:LP], 0.0, mag[:, k:k + 1], ALU.mult, ALU.add, [], [Rtab[tb]])

    def stT(it):
        k, j = it // npc, it % npc
        if j == 0:
            tables(k)
        tb, s3, s2 = k % 2, it % NS, it % 2
        kq = k // 4
        cols = slice(j * LP, (j + 1) * LP)
        br_, bi_ = 2 * s2, 2 * s2 + 1
        pbre, pbim = bank(C, br_), bank(C, bi_)
        MM(P, pbre, BdT[0][:, k, :], xnT[:, kq, cols], True, True, [], [Rpb[br_]])
        MM(P, pbim, BdT[1][:, k, :], xnT[:, kq, cols], True, True, [], [Rpb[bi_]])
        c_, s_ = cT[tb][:, 0:LP], sT[tb][:, 0:LP]
        t = tq[s3]; Rt = Rtq[s3]
        pend = ini_ops(it - 1) if it >= 1 else []
        pend = pend + [None] * (4 - len(pend))
        TT(P, "vector", t[0], pbre, c_, ALU.mult, [Rpb[br_], Rtab[tb]], [Rt[0]])
        if pend[0]: pend[0]()
        TT(P, "vector", t[1], pbim, s_, ALU.mult, [Rpb[bi_], Rtab[tb]], [Rt[1]])
        if pend[1]: pend[1]()
        TT(P, "vector", t[2], pbim, c_, ALU.mult, [Rpb[bi_], Rtab[tb]], [Rt[2]])
        if pend[2]: pend[2]()
        TT(P, "vector", t[3], pbre, s_, ALU.mult, [Rpb[br_], Rtab[tb]], [Rt[3]])
        if pend[3]: pend[3]()

    def ini_ops(it):
        if it < 0 or it >= NIT:
            return []
        k, j = it // npc, it % npc
        if j == 0:
            return []
        tb, s3 = k % 2, it % NS
        p3 = (it - 1) % NS
        wrl, wil = wq[p3][0][:, LP - 1:LP], wq[p3][1][:, LP - 1:LP]
        cL, sL = cT[tb][:, LP:LP + 1], sT[tb][:, LP:LP + 1]
        iv = ini[s3]
        return [
            lambda: TS(P, "vector", iv[:, 0:1], wil, sL, None, ALU.mult, None, [Rwq[p3][1], Rtab[tb]], [Rini[s3]]),
            lambda: TS(P, "vector", iv[:, 2:3], wil, cL, None, ALU.mult, None, [Rwq[p3][1], Rtab[tb]], [Rini[s3]]),
            lambda: STT(P, iv[:, 1:2], wrl, cL, iv[:, 0:1], ALU.mult, ALU.subtract, [Rwq[p3][0], Rini[s3]], [Rini[s3]]),
            lambda: STT(P, iv[:, 3:4], wrl, sL, iv[:, 2:3], ALU.mult, ALU.add, [Rwq[p3][0], Rini[s3]], [Rini[s3]]),
        ]

    def stZ(it):
        s3 = it % NS
        t = tq[s3]; Rt = Rtq[s3]
        TT(P, "gpsimd", zq[s3][0], t[0], t[1], ALU.add, [Rt[0], Rt[1]], [Rzq[s3][0]])
        TT(P, "gpsimd", zq[s3][1], t[2], t[3], ALU.subtract, [Rt[2], Rt[3]], [Rzq[s3][1]])

    def stS(it):
        k, j = it // npc, it % npc
        tb, s3 = k % 2, it % NS
        if j == 0:
            init_r, init_i, rd = 0.0, 0.0, []
        else:
            iv = ini[s3]
            init_r, init_i, rd = iv[:, 1:2], iv[:, 3:4], [Rini[s3]]
        SCAN(P, wq[s3][0], rfull[tb], zq[s3][0], init_r, [Rzq[s3][0], Rtab[tb]] + rd, [Rwq[s3][0]])
        SCAN(P, wq[s3][1], rfull[tb], zq[s3][1], init_i, [Rzq[s3][1], Rtab[tb]] + rd, [Rwq[s3][1]])

    def stU(it):
        k = it // npc
        tb, s3, s2 = k % 2, it % NS, it % 2
        c_, s_ = cT[tb][:, 0:LP], sT[tb][:, 0:LP]
        wr, wi = wq[s3]
        u = uq[s2]; Ru = Ruq[s2]
        TT(P, "gpsimd", u[0], wr, c_, ALU.mult, [Rwq[s3][0], Rtab[tb]], [Ru[0]])
        TT(P, "gpsimd", u[1], wi, s_, ALU.mult, [Rwq[s3][1], Rtab[tb]], [Ru[1]])
        TT(P, "gpsimd", u[2], wr, s_, ALU.mult, [Rwq[s3][0], Rtab[tb]], [Ru[2]])
        TT(P, "vector", u[3], wi, c_, ALU.mult, [Rwq[s3][1], Rtab[tb]], [Ru[3]])

    def stX(it):
        k, j = it // npc, it % npc
        s2 = it % 2
        kq, kr = k // 4, k % 4
        Rs = slice(32 * kr, 32 * kr + 32)
        cols = slice(j * LP, (j + 1) * LP)
        u = uq[s2]; Ru = Ruq[s2]
        by = 4 + s2
        py = bank(C, by)
        MM(P, py, CdT[0][:, kq, :], u[0], True, False, [Ru[0]], [Rpb[by]])
        MM(P, py, CdT[2][:, kq, :], u[1], False, False, [Ru[1]], [Rpb[by]])
        MM(P, py, CdT[1][:, kq, :], u[2], False, False, [Ru[2]], [Rpb[by]])
        MM(P, py, CdT[1][:, kq, :], u[3], False, True, [Ru[3]], [Rpb[by]])
        CP(P, "scalar", ysb[s2], py, [Rpb[by]], [Rysb[s2]])
        DMA(P, "sync", SC["y"][32 * k:32 * k + 32, cols], ysb[s2][Rs, :], [Rysb[s2]], [Ryd[s2]])

    for tick in range(NIT + 2):
        if tick < NIT:
            stT(tick)
            stZ(tick)
        if tick == NIT:
            for f in ini_ops(tick - 1):
                f()
        if 0 <= tick - 1 < NIT:
            stS(tick - 1)
            stU(tick - 1)
        if 0 <= tick - 2 < NIT:
            stX(tick - 2)
    P.barrier()
    A.reset(C.base2)
    wg = A.alloc([128, 8, D], BF16)
    stage = A.alloc([128, 8, 512], F32)
    Rstage = Res()
    for j in range(0, D, 512):
        load_w_cols(C, W["w_glu"], j, 512, wg[:, :, j:j + 512], stage, Rstage)
    dcol = A.alloc([128, 8], F32)
    bcol = A.alloc([128, 8], F32)
    P.op("sync", lambda e: e.dma_start(out=dcol, in_=W["d"].rearrange("(kq p) -> p kq", p=128), allow_slow_non_contiguous=True), writes=[Res()], dma=True)
    P.op("sync", lambda e: e.dma_start(out=bcol, in_=W["b_glu"].rearrange("(kq p) -> p kq", p=128), allow_slow_non_contiguous=True), writes=[Res()], dma=True)
    P.barrier()
    yt = [A.alloc([128, 8, LP], F32) for _ in range(2)]
    Ryt = [[Res() for _ in range(8)] for _ in range(2)]
    zf = A.alloc([128, 8, LP], F32)
    Rzf = [Res() for _ in range(8)]
    zb = A.alloc([128, 8, LP], BF16)
    Rzb = [Res() for _ in range(8)]
    q1 = [A.alloc([128, LP], F32) for _ in range(2)]
    Rq1 = [Res() for _ in range(2)]
    mixT = [A.alloc([128, LP], F32) for _ in range(2)]
    RmixT = [Res() for _ in range(2)]
    ht = [A.alloc([128, D], F32) for _ in range(4)]
    Rht = [Res() for _ in range(4)]
    for tt in range(T // LP):
        s = tt % 2
        cols = slice(tt * LP, (tt + 1) * LP)
        for kq in range(8):
            P.op("sync", lambda e, s=s, kq=kq, cols=cols: e.dma_start(out=yt[s][:, kq, :], in_=SC["y"][kq * 128:(kq + 1) * 128, cols]), writes=[Ryt[s][kq]], dma=True)
        for tb in range(4):
            blk = tt * 4 + tb
            P.op("sync", lambda e, tb=tb, blk=blk: e.dma_start(out=ht[tb], in_=h_dram[blk * 128:(blk + 1) * 128, :]), reads=[Rh[blk]], writes=[Rht[tb]], dma=True)
        for kq in range(8):
            y_ = yt[s][:, kq, :]
            Ry = Ryt[s][kq]
            q = q1[kq % 2]; Rq = Rq1[kq % 2]
            P.op("vector", lambda e, y_=y_, kq=kq, cols=cols: e.scalar_tensor_tensor(out=y_, in0=xnT[:, kq, cols], scalar=dcol[:, kq:kq + 1], in1=y_, op0=ALU.mult, op1=ALU.add),
                 reads=[Ry], writes=[Ry])
            P.op("scalar", lambda e, q=q, y_=y_: e.activation(out=q, in_=y_, func=AF.Square), reads=[Ry], writes=[Rq])
            P.op("vector", lambda e, q=q: e.tensor_scalar(out=q, in0=q, scalar1=0.044715, scalar2=1.0, op0=ALU.mult, op1=ALU.add), reads=[Rq], writes=[Rq])
            P.op("gpsimd", lambda e, q=q, y_=y_: e.tensor_tensor(out=q, in0=q, in1=y_, op=ALU.mult), reads=[Rq, Ry], writes=[Rq])
            P.op("scalar", lambda e, q=q: e.activation(out=q, in_=q, func=AF.Sigmoid, scale=1.5957691216), reads=[Rq], writes=[Rq])
            P.op("vector", lambda e, q=q, y_=y_, kq=kq: e.tensor_tensor(out=zf[:, kq, :], in0=q, in1=y_, op=ALU.mult), reads=[Rq, Ry], writes=[Rzf[kq]])
            P.op("gpsimd", lambda e, kq=kq: e.tensor_copy(out=zb[:, kq, :], in_=zf[:, kq, :]), reads=[Rzf[kq]], writes=[Rzb[kq]])
        for nq in range(8):
            b = nq % 2
            pg = bank(C, b)
            for kc in range(8):
                P.op("tensor", lambda e, pg=pg, kc=kc, nq=nq: e.matmul(pg, lhsT=wg[:, kc, nq * 128:(nq + 1) * 128], rhs=zb[:, kc, :], start=(kc == 0), stop=(kc == 7)),
                     reads=[Rzb[kc]], writes=[Rpb[b]])
            m = mixT[nq % 2]; Rm = RmixT[nq % 2]
            P.op("scalar", lambda e, pg=pg, m=m, nq=nq: e.activation(out=m, in_=pg, func=AF.Sigmoid, bias=bcol[:, nq:nq + 1], scale=1.0), reads=[Rpb[b]], writes=[Rm])
            P.op("vector", lambda e, m=m, nq=nq: e.tensor_tensor(out=m, in0=m, in1=zf[:, nq, :], op=ALU.mult), reads=[Rm, Rzf[nq]], writes=[Rm])
            bt = 2 + nq % 4
            pt = bank(C, bt)
            for tb in range(4):
                P.op("tensor", lambda e, pt=pt, tb=tb, m=m: e.transpose(out=pt[:, tb * 128:(tb + 1) * 128], in_=m[:, tb * 128:(tb + 1) * 128], identity=C.ident_f),
                     reads=[Rm], writes=[Rpb[bt]])
            for tb in range(4):
                P.op("vector", lambda e, pt=pt, tb=tb, nq=nq: e.tensor_tensor(out=ht[tb][:, nq * 128:(nq + 1) * 128], in0=pt[:, tb * 128:(tb + 1) * 128],
                                                                            in1=ht[tb][:, nq * 128:(nq + 1) * 128], op=ALU.add),
                     reads=[Rpb[bt], Rht[tb]], writes=[Rht[tb]])
        for tb in range(4):
            blk = tt * 4 + tb
            P.op("gpsimd", lambda e, tb=tb, blk=blk: e.dma_start(out=h_dram[blk * 128:(blk + 1) * 128, :], in_=ht[tb]), reads=[Rht[tb]], writes=[Rh[blk]], dma=True)


C0 = 0.6065306597126334
NH = 16
GN_EPS = 64e-5


def xnT_ext_phase(C, h_dram, Rh, g_dram):
    P, A = C.P, C.A
    P.barrier()
    A.reset(C.base)
    xe = A.alloc([128, 8, T + 2], BF16)
    C.base2 = A.off
    gb = A.alloc([128, D], F32)
    Rgb = Res("gb")
    P.op("sync", lambda e: e.dma_start(out=gb, in_=bcast_row(g_dram, D)), writes=[Rgb], dma=True)
    P.op("vector", lambda e: e.memset(xe[:, :, 0:1], 0.0), writes=[Res()])
    ht = [A.alloc([128, D], F32) for _ in range(2)]
    Rht = [Res("ht") for _ in range(2)]
    xn = [A.alloc([128, D], BF16) for _ in range(2)]
    Rxn = [Res("xn") for _ in range(2)]
    junk = A.alloc([128, D], BF16)
    Rjunk = Res("junk")
    ss = [A.alloc([128, 2], F32) for _ in range(2)]
    Rss = [Res("ss") for _ in range(2)]
    Rpb = [Res("pb") for _ in range(2)]
    for blk in range(NB):
        s = blk % 2
        P.op("sync", lambda e, s=s, blk=blk: e.dma_start(out=ht[s], in_=h_dram[blk * 128:(blk + 1) * 128, :]),
             reads=[Rh[blk]], writes=[Rht[s]], dma=True)
        norm_rows(C, ht[s], Rht[s], gb, xn[s], Rxn[s], ss[s], Rss[s], junk, Rjunk)
        transpose_rows(C, xn[s], Rxn[s], 8, s, Rpb[s], xe[:, :, 1 + blk * 128:1 + (blk + 1) * 128], Res(), evac_eng="vector")
    return xe


def load_w_mu(C, w_dram, c0, ncols, dst_a, dst_b, mucol, omcol, stage, Rstage, nrows=D):
    P = C.P
    P.op("sync", lambda e: e.dma_start(out=stage, in_=w_dram[:, c0:c0 + ncols].rearrange("(kc p) j -> p kc j", p=128)),
         writes=[Rstage], dma=True)
    for kc in range(8):
        ACT(P, dst_a[:, kc, :], stage[:, kc, :], AF.Copy if False else AF.Identity, [Rstage], [Res()], scale=omcol[:, kc:kc + 1])
        ACT(P, dst_b[:, kc, :], stage[:, kc, :], AF.Identity, [Rstage], [Res()], scale=mucol[:, kc:kc + 1])


def rw_params(C, W):
    P, A = C.P, C.A
    pr = A.alloc([88, 128], F32)
    pc = A.alloc([128, 88], F32)
    om = A.alloc([128, 48], F32)
    Rp = Res("pr")
    P.op("sync", lambda e: e.dma_start(out=pr[0:48, :], in_=W["mu"].rearrange("n (kc p) -> (n kc) p", p=128)), writes=[Res()], dma=True)
    for i, nm in enumerate(("w0", "a0", "k_k", "k_a", "r_k")):
        P.op("sync", lambda e, i=i, nm=nm: e.dma_start(out=pr[48 + 8 * i:56 + 8 * i, :], in_=W[nm].rearrange("(kc p) -> kc p", p=128)), writes=[Res()], dma=True)
    P.barrier()
    pp = bank(C, 0)
    Rb = Res()
    P.op("tensor", lambda e: e.transpose(out=pp[:, 0:88], in_=pr, identity=C.ident_f[0:88, 0:88]), writes=[Rb])
    P.op("vector", lambda e: e.tensor_copy(out=pc, in_=pp[:, 0:88]), reads=[Rb], writes=[Rp])
    P.op("vector", lambda e: e.tensor_scalar(out=om, in0=pc[:, 0:48], scalar1=-1.0, scalar2=1.0, op0=ALU.mult, op1=ALU.add), reads=[Rp], writes=[Rp])
    P.barrier()
    return pc, om


def rw_r1a(C, xe, W, SC, pc, om):
    P, A = C.P, C.A
    P.barrier()
    m0 = A.off
    wva = A.alloc([128, 8, D], BF16)
    wvb = A.alloc([128, 8, D], BF16)
    g1a = A.alloc([128, 8, 160], BF16)
    g1b = A.alloc([128, 8, 160], BF16)
    g2a = A.alloc([128, D], BF16)
    g2b = A.alloc([32, D], BF16)
    m1 = A.off
    stage = A.alloc([128, 8, 512], F32)
    Rstage = Res()
    for j in range(0, D, 512):
        load_w_mu(C, W["w_rkv"][2], j, 512, wva[:, :, j:j + 512], wvb[:, :, j:j + 512], pc[:, 16:24], om[:, 16:24], stage, Rstage)
    load_w_mu(C, W["g1"], 0, 160, g1a, g1b, pc[:, 40:48], om[:, 40:48], stage[:, :, 0:160], Rstage)
    g2st = A.alloc([128, D], F32)
    Rg2 = Res()
    P.op("sync", lambda e: e.dma_start(out=g2st, in_=W["g2"][0:128, :]), writes=[Rg2], dma=True)
    P.op("vector", lambda e: e.tensor_copy(out=g2a, in_=g2st), reads=[Rg2], writes=[Res()])
    P.op("sync", lambda e: e.dma_start(out=g2st[0:32, :], in_=W["g2"][128:160, :]), reads=[Rg2], writes=[Rg2], dma=True)
    P.op("vector", lambda e: e.tensor_copy(out=g2b, in_=g2st[0:32, :]), reads=[Rg2], writes=[Res()])
    P.barrier()
    A.reset(m1)
    ot = [A.alloc([128, D], BF16) for _ in range(2)]
    Rot = [Res() for _ in range(2)]
    Rod = [Res() for _ in range(2)]
    gt = [A.alloc([128, D], BF16) for _ in range(2)]
    Rgt = [Res() for _ in range(2)]
    Rgd = [Res() for _ in range(2)]
    sga = [A.alloc([128, 512], BF16) for _ in range(2)]
    sgb = [A.alloc([32, 512], BF16) for _ in range(2)]
    Rsg = [Res() for _ in range(2)]
    Rpb = [Res() for _ in range(8)]
    k = 0
    for tp in range(T // 512):
        s = tp % 2
        xc = lambda kc, c0=tp * 512: xe[:, kc, 1 + c0:1 + c0 + 512]
        xp = lambda kc, c0=tp * 512: xe[:, kc, c0:c0 + 512]
        for (r0, r1, bnk, dst) in ((0, 128, 4, sga[s]), (128, 160, 5, sgb[s])):
            pb = bank(C, bnk)[0:r1 - r0, :]
            for kc in range(8):
                P.op("tensor", lambda e, pb=pb, kc=kc, r0=r0, r1=r1, xc=xc: e.matmul(pb, lhsT=g1a[:, kc, r0:r1], rhs=xc(kc), start=(kc == 0), stop=False), writes=[Rpb[bnk]])
            for kc in range(8):
                P.op("tensor", lambda e, pb=pb, kc=kc, r0=r0, r1=r1, xp=xp: e.matmul(pb, lhsT=g1b[:, kc, r0:r1], rhs=xp(kc), start=False, stop=(kc == 7)), writes=[Rpb[bnk]])
            P.op("scalar", lambda e, pb=pb, dst=dst: e.activation(out=dst, in_=pb, func=AF.Sigmoid), reads=[Rpb[bnk]], writes=[Rsg[s]])
        for tb in range(4):
            blk = tp * 4 + tb
            s2 = blk % 2
            bc = lambda kc, blk=blk: xe[:, kc, 1 + blk * 128:1 + (blk + 1) * 128]
            bp = lambda kc, blk=blk: xe[:, kc, blk * 128:(blk + 1) * 128]
            for j in range(0, D, 512):
                b = k % 4
                k += 1
                pb = bank(C, b)
                for kc in range(8):
                    P.op("tensor", lambda e, pb=pb, kc=kc, j=j, bc=bc: e.matmul(pb, lhsT=bc(kc), rhs=wva[:, kc, j:j + 512], start=(kc == 0), stop=False), writes=[Rpb[b]])
                for kc in range(8):
                    P.op("tensor", lambda e, pb=pb, kc=kc, j=j, bp=bp: e.matmul(pb, lhsT=bp(kc), rhs=wvb[:, kc, j:j + 512], start=False, stop=(kc == 7)), writes=[Rpb[b]])
                P.op("scalar", lambda e, pb=pb, s2=s2, j=j: e.copy(out=ot[s2][:, j:j + 512], in_=pb), reads=[Rpb[b]], writes=[Rot[s2]])
            P.op("gpsimd", lambda e, s2=s2, blk=blk: e.dma_start(out=SC["v"][blk * 128:(blk + 1) * 128, :], in_=ot[s2]), reads=[Rot[s2]], writes=[Rod[s2]], dma=True)
            for j in range(0, D, 512):
                b = 6 + (j // 512)
                pb = bank(C, b)
                P.op("tensor", lambda e, pb=pb, j=j, s=s, tb=tb: e.matmul(pb, lhsT=sga[s][:, tb * 128:(tb + 1) * 128], rhs=g2a[:, j:j + 512], start=True, stop=False),
                     reads=[Rsg[s]], writes=[Rpb[b]])
                P.op("tensor", lambda e, pb=pb, j=j, s=s, tb=tb: e.matmul(pb, lhsT=sgb[s][:, tb * 128:(tb + 1) * 128], rhs=g2b[:, j:j + 512], start=False, stop=True),
                     reads=[Rsg[s]], writes=[Rpb[b]])
                P.op("vector", lambda e, pb=pb, s2=s2, j=j: e.tensor_copy(out=gt[s2][:, j:j + 512], in_=pb), reads=[Rpb[b]], writes=[Rgt[s2]])
            P.op("gpsimd", lambda e, s2=s2, blk=blk: e.dma_start(out=SC["g"][blk * 128:(blk + 1) * 128, :], in_=gt[s2]), reads=[Rgt[s2]], writes=[Rgd[s2]], dma=True)
    P.barrier()
    A.reset(m0)


def rw_r1b(C, xe, W, SC, pc, om):
    P, A = C.P, C.A
    P.barrier()
    m0 = A.off
    m01 = A.alloc([128, 512], F32)
    blk1 = A.alloc([128, 128], F32)
    ind2 = A.alloc([128, 2], BF16)
    ind2f = A.alloc([128, 2], F32)
    rk_all = A.alloc([128, NB, NH], F32)
    Rrk = Res()
    P.op("vector", lambda e: e.memset(m01, 1.0), writes=[Res()])
    P.op("vector", lambda e: e.memset(m01.rearrange("p (a b) -> p a b", b=128)[:, :, 0:1], 0.0), writes=[Res()])
    P.op("sync", lambda e: e.dma_start(out=blk1, in_=W["blk1"]), writes=[Res()], dma=True)
    P.op("sync", lambda e: e.dma_start(out=ind2f, in_=W["ind2"]), writes=[Res()], dma=True)
    P.barrier()
    P.op("vector", lambda e: e.tensor_copy(out=ind2, in_=ind2f), writes=[Res()])
    w1a = A.alloc([128, 8, 64], BF16); w1b = A.alloc([128, 8, 64], BF16)
    a1a = A.alloc([128, 8, 64], BF16); a1b = A.alloc([128, 8, 64], BF16)
    w2 = A.alloc([64, D], BF16); a2 = A.alloc([64, D], BF16)
    wra = A.alloc([128, 8, 512], BF16); wrb = A.alloc([128, 8, 512], BF16)
    wka = A.alloc([128, 8, 512], BF16); wkb = A.alloc([128, 8, 512], BF16)
    mw = A.off
    stage = A.alloc([128, 8, 512], F32)
    Rstage = Res()
    load_w_mu(C, W["w1"], 0, 64, w1a, w1b, pc[:, 24:32], om[:, 24:32], stage[:, :, 0:64], Rstage)
    load_w_mu(C, W["a1"], 0, 64, a1a, a1b, pc[:, 32:40], om[:, 32:40], stage[:, :, 0:64], Rstage)
    st2 = A.alloc([64, D], F32)
    Rst2 = Res()
    for src, dst in ((W["w2"], w2), (W["a2"], a2)):
        P.op("sync", lambda e, src=src: e.dma_start(out=st2, in_=src), reads=[Rst2], writes=[Rst2], dma=True)
        P.op("vector", lambda e, dst=dst: e.tensor_copy(out=dst, in_=st2), reads=[Rst2], writes=[Rst2])
    for half in range(2):
        P.barrier()
        A.reset(mw)
        stage = A.alloc([128, 8, 512], F32)
        Rstage = Res()
        load_w_mu(C, W["w_rkv"][0], half * 512, 512, wra, wrb, pc[:, 0:8], om[:, 0:8], stage, Rstage)
        load_w_mu(C, W["w_rkv"][1], half * 512, 512, wka, wkb, pc[:, 8:16], om[:, 8:16], stage, Rstage)
        P.barrier()
        A.reset(mw)
        thT = A.alloc([64, 512], BF16); laT = A.alloc([64, 512], BF16)
        Rth, Rla = Res(), Res()
        NF = 11
        f2 = [[A.alloc([128, 512], F32) for _ in range(NF)] for _ in range(2)]
        Rf2 = [[Res() for _ in range(NF)] for _ in range(2)]
        ob = [[A.alloc([128, 512], BF16) for _ in range(7)] for _ in range(2)]
        Rob = [[Res() for _ in range(7)] for _ in range(2)]
        Rod = [[Res() for _ in range(7)] for _ in range(2)]
        gc4 = [A.alloc([128, 4], F32) for _ in range(2)]
        Rgc4 = [Res() for _ in range(2)]
        Rgcd = [Res() for _ in range(2)]
        tk = [[A.alloc([128, 4, 128], BF16) for _ in range(3)] for _ in range(2)]
        Rtk = [[Res() for _ in range(3)] for _ in range(2)]
        Rtkd = [[Res() for _ in range(3)] for _ in range(2)]
        Rpb = [Res() for _ in range(8)]
        it = 0
        pending_tail = [None]
        for tp in range(T // 512):
            c0 = tp * 512
            xc = lambda kc, c0=c0: xe[:, kc, 1 + c0:1 + c0 + 512]
            xp = lambda kc, c0=c0: xe[:, kc, c0:c0 + 512]
            cols = slice(c0, c0 + 512)
            for (wa, wb, dst, Rd, fn) in ((w1a, w1b, thT, Rth, AF.Tanh), (a1a, a1b, laT, Rla, AF.Copy)):
                pb = bank(C, 6)[0:64, :]
                for kc in range(8):
                    P.op("tensor", lambda e, pb=pb, kc=kc, wa=wa, xc=xc: e.matmul(pb, lhsT=wa[:, kc, :], rhs=xc(kc), start=(kc == 0), stop=False), writes=[Rpb[6]])
                for kc in range(8):
                    P.op("tensor", lambda e, pb=pb, kc=kc, wb=wb, xp=xp: e.matmul(pb, lhsT=wb[:, kc, :], rhs=xp(kc), start=False, stop=(kc == 7)), writes=[Rpb[6]])
                P.op("scalar", lambda e, pb=pb, dst=dst, fn=fn: e.activation(out=dst, in_=pb, func=fn), reads=[Rpb[6]], writes=[Rd])
            for pl in range(4):
                p = half * 4 + pl
                s = it % 2
                it += 1
                pcs = slice(pl * 128, (pl + 1) * 128)
                gcs = slice(p * 128, (p + 1) * 128)
                br, bk = s, 2 + s
                r_ps, k_ps, w_ps, a_ps, ss_ps = bank(C, br), bank(C, bk), bank(C, 4), bank(C, 5), bank(C, 6)
                for (ps_, wa, wb, bb) in ((r_ps, wra, wrb, br), (k_ps, wka, wkb, bk)):
                    for kc in range(8):
                        P.op("tensor", lambda e, ps_=ps_, kc=kc, wa=wa, xc=xc, pcs=pcs: e.matmul(ps_, lhsT=wa[:, kc, pcs], rhs=xc(kc), start=(kc == 0), stop=False), writes=[Rpb[bb]])
                    for kc in range(8):
                        P.op("tensor", lambda e, ps_=ps_, kc=kc, wb=wb, xp=xp, pcs=pcs: e.matmul(ps_, lhsT=wb[:, kc, pcs], rhs=xp(kc), start=False, stop=(kc == 7)), writes=[Rpb[bb]])
                P.op("tensor", lambda e, w_ps=w_ps, gcs=gcs: e.matmul(w_ps, lhsT=w2[:, gcs], rhs=thT, start=True, stop=True), reads=[Rth], writes=[Rpb[4]])
                P.op("tensor", lambda e, a_ps=a_ps, gcs=gcs: e.matmul(a_ps, lhsT=a2[:, gcs], rhs=laT, start=True, stop=True), reads=[Rla], writes=[Rpb[5]])
                if pending_tail[0] is not None:
                    pending_tail[0]()
                    pending_tail[0] = None
                sgw, cs, gam, ginv, gprev, gcg, av, kk, kk2, k2, bv = f2[s]
                Rsgw, Rcs, Rgam, Rginv, Rgprev, Rgcg, Rav, Rkk, Rkk2, Rk2, Rbv = Rf2[s]
                A1T, BtT, KtT, R1T, BdF, KdF, prod = ob[s]
                RA1T, RBtT, RKtT, RR1T, RBdF, RKdF, Rprod = Rob[s]
                col = lambda base, p=p: pc[:, base + p:base + p + 1]
                v3 = lambda ap: ap.rearrange("p (a b) -> p a b", b=128)
                ACT(P, sgw, w_ps, AF.Sigmoid, [Rpb[4]], [Rsgw], bias=col(48), scale=1.0)
                ACT(P, av, a_ps, AF.Sigmoid, [Rpb[5]], [Rav], bias=col(56), scale=1.0)
                SCAN(P, cs, m01, sgw, 0.0, [Rsgw], [Rcs])
                TS(P, "vector", kk, k_ps, col(64), None, ALU.mult, None, [Rpb[bk]], [Rkk])
                ACT(P, kk2, kk, AF.Square, [Rkk], [Rkk2])
                MM(P, ss_ps, blk1, kk2, True, True, [Rkk2], [Rpb[6]])
                TT(P, "gpsimd", gprev, cs, sgw, ALU.subtract, [Rcs, Rsgw], [Rgprev])
                TT(P, "vector", v3(gcg), v3(cs)[:, :, 127:128].to_broadcast([128, 4, 128]), v3(cs), ALU.subtract, [Rcs], [Rgcg])
                ACT(P, gam, cs, AF.Exp, [Rcs], [Rgam], scale=-C0)
                ACT(P, ginv, cs, AF.Exp, [Rcs], [Rginv], scale=C0)
                ACT(P, gprev, gprev, AF.Exp, [Rgprev], [Rgprev], scale=-C0)
                ACT(P, gcg, gcg, AF.Exp, [Rgcg], [Rgcg], scale=-C0)
                ACT(P, gc4[s], v3(cs)[:, :, 127], AF.Exp, [Rcs], [Rgc4[s]], scale=-C0)
                DMA(P, "sync", SC["gC"][gcs, tp * 4:(tp + 1) * 4], gc4[s], [Rgc4[s]], [Rgcd[s]])
                TS(P, "vector", kk2, ss_ps, 1e-24, None, ALU.max, None, [Rpb[6]], [Rkk2])
                ACT(P, kk2, kk2, AF.Ln, [Rkk2], [Rkk2])
                ACT(P, kk2, kk2, AF.Exp, [Rkk2], [Rkk2], scale=-0.5)
                TT(P, "vector", kk, kk, kk2, ALU.mult, [Rkk, Rkk2], [Rkk])
                TS(P, "vector", k2, av, -1.0, col(72), ALU.add, ALU.mult, [Rav], [Rk2])
                STT(P, k2, k2, 1.0, k_ps, ALU.add, ALU.mult, [Rk2, Rpb[bk]], [Rk2])
                TT(P, "gpsimd", bv, kk, av, ALU.mult, [Rkk, Rav], [Rbv])
                STT(P, A1T, kk, -1.0, gprev, ALU.mult, ALU.mult, [Rkk, Rgprev], [RA1T])
                TT(P, "gpsimd", BtT, bv, ginv, ALU.mult, [Rbv, Rginv], [RBtT])
                TT(P, "gpsimd", KtT, k2, ginv, ALU.mult, [Rk2, Rginv], [RKtT])
                TT(P, "vector", R1T, r_ps, gam, ALU.mult, [Rpb[br], Rgam], [RR1T])
                TT(P, "gpsimd", BdF, bv, gcg, ALU.mult, [Rbv, Rgcg], [RBdF])
                TT(P, "gpsimd", KdF, k2, gcg, ALU.mult, [Rk2, Rgcg], [RKdF])
                STT(P, prod, k2, col(80), r_ps, ALU.mult, ALU.mult, [Rk2, Rpb[br]], [Rprod])
                for i, nm in enumerate(("A1T", "BtT", "KtT", "R1T")):
                    DMA(P, "sync", SC[nm][gcs, cols], ob[s][i], [Rob[s][i]], [Rod[s][i]])
                def tail(s=s, gcs=gcs, tp=tp, p=p):
                    pt = bank(C, 7, BF16)
                    for i, src_i in enumerate((0, 4, 5)):
                        for tb in range(4):
                            TR(P, pt[:, tb * 128:(tb + 1) * 128], ob[s][src_i][:, tb * 128:(tb + 1) * 128], C.ident_b, [Rob[s][src_i]], [Rpb[7]])
                        CP(P, "vector", tk[s][i], pt[:, 0:512].rearrange("p (a b) -> p a b", b=128), [Rpb[7]], [Rtk[s][i]])
                        nm = ("A1", "Bd", "Kd")[i]
                        DMA(P, "gpsimd", SC[nm][tp * 512:(tp + 1) * 512, gcs].rearrange("(a t) c -> t a c", t=128), tk[s][i], [Rtk[s][i]], [Rtkd[s][i]])
                    prk = bank(C, 7)[:, 256:264]
                    for tb in range(4):
                        MM(P, prk[:, tb * 2:tb * 2 + 2], ob[s][6][:, tb * 128:(tb + 1) * 128], ind2, True, True, [Rob[s][6]], [Rpb[7]])
                    CP(P, "vector", rk_all[:, tp * 4:(tp + 1) * 4, 2 * p:2 * p + 2], prk.rearrange("p (a b) -> p a b", b=2), [Rpb[7]], [Rrk])
                pending_tail[0] = tail
        if pending_tail[0] is not None:
            pending_tail[0]()
            pending_tail[0] = None
    P.op("sync", lambda e: e.dma_start(out=SC["rk"].rearrange("(a t) h -> t a h", t=128), in_=rk_all), reads=[Rrk], writes=[Res()], dma=True)
    P.barrier()
    A.reset(m0)


def rw_r2(C, W, SC, og_dram, nchunks=NB):
    P, A = C.P, C.A
    P.barrier()
    A.reset(C.base)
    mk = A.alloc([128, 8, 128], F32)
    mkb = A.alloc([128, 3, 128], BF16)
    gCt = A.alloc([128, 8, 32], F32)
    lnw = A.alloc([128, D], F32)
    lnb = A.alloc([128, D], F32)
    DMA(P, "sync", mk, W["masks"].rearrange("m a b -> a m b"), [], [Res()])
    DMA(P, "sync", gCt, SC["gC"].rearrange("(p q) c -> q p c", q=128), [], [Res()])
    DMA(P, "sync", lnw, bcast_row(W["ln_w"], D), [], [Res()])
    DMA(P, "sync", lnb, bcast_row(W["ln_b"], D), [], [Res()])
    P.barrier()
    CP(P, "vector", mkb, mk[:, 5:8, :], [], [Res()])
    P.barrier()
    mb = lambda i: mk[:, i, :].unsqueeze(1).to_broadcast([128, 4, 128])
    mbb = lambda i: mkb[:, i, :].unsqueeze(1).to_broadcast([128, 4, 128])
    identb4 = C.ident_f.unsqueeze(1).to_broadcast([128, 4, 128])
    FTn = ("A1T", "BtT", "KtT", "R1T")
    TKn = ("A1", "Bd", "Kd", "v")
    FT = [[A.alloc([128, 8, 128], BF16) for _ in range(4)] for _ in range(2)]
    RFT = [[Res() for _ in range(4)] for _ in range(2)]
    TK = [[A.alloc([128, D], BF16) for _ in range(4)] for _ in range(2)]
    RTK = [[Res() for _ in range(4)] for _ in range(2)]
    gtk = [A.alloc([128, D], BF16) for _ in range(2)]
    Rgtk = [Res() for _ in range(2)]
    rkt = [A.alloc([128, NH], F32) for _ in range(2)]
    Rrkt = [Res() for _ in range(2)]
    NSLOT = 4
    GF = []
    GB = ["Qd", "Nd", "Xa", "XaT", "Xb", "XbT", "Qb", "Nb", "Qo0", "No0", "Qo1", "No1", "Qo2", "No2", "Zb", "Tmb", "Y1", "Y2", "MTb"]
    G = [dict([(nm, A.alloc([128, 4, 128], F32)) for nm in GF] + [(nm, A.alloc([128, 4, 128], BF16)) for nm in GB]) for _ in range(NSLOT)]
    RG = [{nm: Res(nm) for nm in GF + GB} for _ in range(NSLOT)]
    MVb = [A.alloc([128, 4, 64], BF16) for _ in range(NSLOT)]
    RMVb = [Res() for _ in range(NSLOT)]
    PbT = [A.alloc([128, NH, 128], BF16) for _ in range(2)]
    PkT = [A.alloc([128, NH, 128], BF16) for _ in range(2)]
    W2 = [A.alloc([128, NH, 64], F32) for _ in range(2)]
    AhT = [A.alloc([128, 8, 128], BF16) for _ in range(2)]
    RPbT = [[Res() for _ in range(4)] for _ in range(2)]
    RPkT = [[Res() for _ in range(4)] for _ in range(2)]
    RW2 = [[Res() for _ in range(4)] for _ in range(2)]
    RAhT = [[Res() for _ in range(4)] for _ in range(2)]
    Ub = A.alloc([128, NH, 64], BF16); RUb = Res()
    H = A.alloc([128, 8, 64], F32); RH = Res()
    Hb = A.alloc([128, 8, 64], BF16); RHb = Res()
    ysb = A.alloc([128, D], F32); Rysb = Res()
    ysq = A.alloc([128, D], F32); Rysq = Res()
    st = A.alloc([128, 6, NH], F32); Rst = Res()
    ogt = [A.alloc([128, D], BF16) for _ in range(2)]
    Rogt = [Res() for _ in range(2)]
    Rogd = [Res() for _ in range(2)]
    Rpb = [Res("pb") for _ in range(8)]
    bctr = [0]

    def nb():
        b = bctr[0] % 8
        bctr[0] += 1
        return b

    MEMSET(P, "vector", H, 0.0, [RH])
    MEMSET(P, "vector", Hb, 0.0, [RHb])
    v4 = lambda ap: ap.rearrange("p (a b) -> p a b", b=128)
    h3 = lambda ap: ap.rearrange("p (a b) -> p a b", b=64)
    pv = lambda ap, hh: ap.rearrange("p (a two) w -> p a two w", two=2)[:, :, hh, :]
    hgrp = lambda hd: 2 * ((hd // 2) // 4) + hd % 2

    def loads(n):
        s = n % 2
        for i, nm in enumerate(FTn):
            DMA(P, "sync", FT[s][i], SC[nm][:, n * 128:(n + 1) * 128].rearrange("(p q) t -> q p t", q=128), [], [RFT[s][i]])
        for i, nm in enumerate(TKn):
            DMA(P, "sync", TK[s][i], SC[nm][n * 128:(n + 1) * 128, :], [], [RTK[s][i]])
        DMA(P, "sync", gtk[s], SC["g"][n * 128:(n + 1) * 128, :], [], [Rgtk[s]])
        DMA(P, "sync", rkt[s], SC["rk"][n * 128:(n + 1) * 128, :], [], [Rrkt[s]])

    def stageA(n, g4, sl):
        s = n % 2
        g, R = G[sl], RG[sl]
        A1T, BtT, KtT, R1T = FT[s]
        A1k, Bdk, Kdk, Vk = TK[s]
        half8, hh = g4 // 2, g4 % 2
        p16 = lambda ap: ap.rearrange("p (a two) w -> p a two w", two=2)[:, 4 * half8:4 * half8 + 4, hh, :]
        Rj = slice(64 * hh, 64 * hh + 64)

        def mm4(lhs, rhs):
            b = nb()
            pb = bank(C, b)
            for hl in range(4):
                MM(P, pb[:, hl * 128:(hl + 1) * 128], g[lhs][:, hl, :], g[rhs][:, hl, :], True, True, [R[lhs], R[rhs]], [Rpb[b]])
            return b, v4(pb)

        bs = [nb() for _ in range(5)]
        pbs = [bank(C, b) for b in bs]
        specs = ((BtT, A1T, 1, 0), (A1T, BtT, 0, 1), (KtT, A1T, 2, 0), (BtT, R1T, 1, 3), (KtT, R1T, 2, 3))
        for mi, (la, ra, li, ri) in enumerate(specs):
            for hl in range(4):
                p = 4 * half8 + hl
                MM(P, pbs[mi][:, hl * 128:(hl + 1) * 128], la[Rj, p, :], ra[Rj, p, :], True, True, [RFT[s][li], RFT[s][ri]], [Rpb[bs[mi]]])
        TT(P, "vector", g["Qd"], v4(pbs[0]), mb(3), ALU.mult, [Rpb[bs[0]]], [R["Qd"]])
        TT(P, "vector", g["Nd"], v4(pbs[1]), mb(4), ALU.mult, [Rpb[bs[1]]], [R["Nd"]])
        TT(P, "vector", g["Qb"], v4(pbs[0]), mb(0), ALU.mult, [Rpb[bs[0]]], [R["Qb"]])
        TT(P, "vector", g["Nb"], v4(pbs[1]), mb(2), ALU.mult, [Rpb[bs[1]]], [R["Nb"]])
        TT(P, "vector", g["MTb"], v4(pbs[2]), mb(0), ALU.mult, [Rpb[bs[2]]], [R["MTb"]])
        TT(P, "vector", p16(PbT[s]), v4(pbs[3]), mb(1), ALU.mult, [Rpb[bs[3]]], [RPbT[s][g4]])
        TT(P, "vector", p16(PkT[s]), v4(pbs[4]), mb(1), ALU.mult, [Rpb[bs[4]]], [RPkT[s][g4]])
        yield
        xs_, xts_ = "Qd", "Nd"
        for k in range(3):
            xn_, xtn_ = ("Xa", "XaT") if k % 2 == 0 else ("Xb", "XbT")
            b1, p1 = mm4(xts_, xs_)
            b2, p2 = mm4(xs_, xts_)
            CP(P, "scalar", g[xn_], p1, [Rpb[b1]], [R[xn_]])
            CP(P, "scalar", g[xtn_], p2, [Rpb[b2]], [R[xtn_]])
            TT(P, "gpsimd", g[f"Qo{k}"], g["Qb"], mbb(k), ALU.mult, [R["Qb"]], [R[f"Qo{k}"]])
            TT(P, "gpsimd", g[f"No{k}"], g["Nb"], mbb(k), ALU.mult, [R["Nb"]], [R[f"No{k}"]])
            yield
            if k == 0:
                b3 = nb(); pb3 = bank(C, b3)
                b4 = nb(); pb4 = bank(C, b4)
                for hl in range(4):
                    o3 = pb3[:, hl * 128:(hl + 1) * 128]
                    MM(P, o3, g[xtn_][:, hl, :], g["Qd"][:, hl, :], True, False, [R[xtn_], R["Qd"]], [Rpb[b3]])
                    MM(P, o3, C.ident_b, g[xn_][:, hl, :], False, False, [R[xn_]], [Rpb[b3]])
                    MM(P, o3, C.ident_b, g["Qd"][:, hl, :], False, True, [R["Qd"]], [Rpb[b3]])
                for hl in range(4):
                    o4 = pb4[:, hl * 128:(hl + 1) * 128]
                    MM(P, o4, g[xn_][:, hl, :], g["Nd"][:, hl, :], True, False, [R[xn_], R["Nd"]], [Rpb[b4]])
                    MM(P, o4, C.ident_b, g[xtn_][:, hl, :], False, False, [R[xtn_]], [Rpb[b4]])
                    MM(P, o4, C.ident_b, g["Nd"][:, hl, :], False, True, [R["Nd"]], [Rpb[b4]])
                TT(P, "vector", g["Zb"], v4(pb3), identb4, ALU.add, [Rpb[b3]], [R["Zb"]])
                TT(P, "vector", g["Tmb"], v4(pb4), identb4, ALU.add, [Rpb[b4]], [R["Tmb"]])
            else:
                b3, p3 = mm4(xtn_, "Zb")
                b4, p4 = mm4(xn_, "Tmb")
                TT(P, "vector", g["Zb"], p3, g["Zb"], ALU.add, [Rpb[b3], R["Zb"]], [R["Zb"]])
                TT(P, "vector", g["Tmb"], p4, g["Tmb"], ALU.add, [Rpb[b4], R["Tmb"]], [R["Tmb"]])
            yield
            xs_, xts_ = xn_, xtn_
        for lv in range(3):
            last = (lv == 2)
            b1, p1 = mm4(f"No{lv}", "Zb")
            CP(P, "scalar", g["Y1"], p1, [Rpb[b1]], [R["Y1"]])
            if not last:
                b2, p2 = mm4(f"Qo{lv}", "Tmb")
                CP(P, "scalar", g["Y2"], p2, [Rpb[b2]], [R["Y2"]])
            yield
            b3, p3 = mm4("Tmb", "Y1")
            if not last:
                b4, p4 = mm4("Zb", "Y2")
            TT(P, "vector", g["Zb"], p3, g["Zb"], ALU.add, [Rpb[b3], R["Zb"]], [R["Zb"]])
            if not last:
                TT(P, "vector", g["Tmb"], p4, g["Tmb"], ALU.add, [Rpb[b4], R["Tmb"]], [R["Tmb"]])
            yield
        b = nb(); pb = bank(C, b)
        for hl in range(4):
            hd = 2 * (4 * half8 + hl) + hh
            MM(P, pb[:, hl * 64:(hl + 1) * 64], g["MTb"][:, hl, :], Vk[:, hd * 64:(hd + 1) * 64], True, True, [R["MTb"], RTK[s][3]], [Rpb[b]])
        CP(P, "scalar", MVb[sl], pb[:, 0:256].rearrange("p (a b) -> p a b", b=64), [Rpb[b]], [RMVb[sl]])
        b2_ = nb(); pb2 = bank(C, b2_)
        for hl in range(4):
            hd = 2 * (4 * half8 + hl) + hh
            MM(P, pb2[64 * hh:64 * hh + 64, hl * 128:(hl + 1) * 128], A1k[:, hd * 64:(hd + 1) * 64], g["Zb"][:, hl, :], True, True, [RTK[s][0], R["Zb"]], [Rpb[b2_]])
        CP(P, "vector", AhT[s][64 * hh:64 * hh + 64, 4 * half8:4 * half8 + 4, :], pb2[64 * hh:64 * hh + 64, :].rearrange("p (a b) -> p a b", b=128), [Rpb[b2_]], [RAhT[s][g4]])
        yield
        b = nb(); pb = bank(C, b)
        for hl in range(4):
            MM(P, pb[:, hl * 64:(hl + 1) * 64], g["Zb"][:, hl, :], MVb[sl][:, hl, :], True, True, [R["Zb"], RMVb[sl]], [Rpb[b]])
        CP(P, "scalar", p16(W2[s]), pb[:, 0:256].rearrange("p (a b) -> p a b", b=64), [Rpb[b]], [RW2[s][g4]])
        yield

    def stageB(n):
        s = n % 2
        A1T, BtT, KtT, R1T = FT[s]
        A1k, Bdk, Kdk, Vk = TK[s]
        bu = [nb(), nb()]
        for hd in range(NH):
            p, hh = hd // 2, hd % 2
            Rj = slice(64 * hh, 64 * hh + 64)
            b = bu[hh]
            MM(P, bank(C, b)[:, p * 64:(p + 1) * 64], AhT[s][Rj, p, :], Hb[Rj, p, :], True, True, [RAhT[s][hgrp(hd)], RHb], [Rpb[b]])
        for hh in range(2):
            TT(P, "vector", pv(Ub, hh), bank(C, bu[hh]).rearrange("p (a b) -> p a b", b=64), pv(W2[s], hh), ALU.add,
               [Rpb[bu[hh]]] + RW2[s], [RUb])
        yield
        by = [nb(), nb()]
        for hd in range(NH):
            p, hh = hd // 2, hd % 2
            Rj = slice(64 * hh, 64 * hh + 64)
            b = by[hh]
            o = bank(C, b)[:, p * 64:(p + 1) * 64]
            MM(P, o, R1T[Rj, p, :], Hb[Rj, p, :], True, False, [RFT[s][3], RHb], [Rpb[b]])
            MM(P, o, PbT[s][:, hd, :], Ub[:, hd, :], False, False, [RPbT[s][hgrp(hd)], RUb], [Rpb[b]])
            MM(P, o, PkT[s][:, hd, :], Vk[:, hd * 64:(hd + 1) * 64], False, True, [RPkT[s][hgrp(hd)], RTK[s][3]], [Rpb[b]])
        bh = nb()
        for hd in range(NH):
            p, hh = hd // 2, hd % 2
            o = bank(C, bh)[64 * hh:64 * hh + 64, p * 64:(p + 1) * 64]
            MM(P, o, Bdk[:, hd * 64:(hd + 1) * 64], Ub[:, hd, :], True, False, [RTK[s][1], RUb], [Rpb[bh]])
            MM(P, o, Kdk[:, hd * 64:(hd + 1) * 64], Vk[:, hd * 64:(hd + 1) * 64], False, True, [RTK[s][2], RTK[s][3]], [Rpb[bh]])
        TT(P, "vector", H, H, gCt[:, :, n:n + 1].to_broadcast([128, 8, 64]), ALU.mult, [RH], [RH])
        TT(P, "vector", H, bank(C, bh).rearrange("p (a b) -> p a b", b=64), H, ALU.add, [Rpb[bh], RH], [RH])
        CP(P, "scalar", Hb, H, [RH], [RHb])
        for hh in range(2):
            CP(P, "scalar", pv(h3(ysb), hh), bank(C, by[hh]).rearrange("p (a b) -> p a b", b=64), [Rpb[by[hh]]], [Rysb])
        yield
        ACT(P, ysq, ysb, AF.Square, [Rysb], [Rysq])
        yield
        s1, s2_, mean, var, rstd, m2 = (st[:, i, :] for i in range(6))
        P.op("vector", lambda e, s1=s1: e.tensor_reduce(out=s1, in_=h3(ysb), axis=AX.X, op=ALU.add), reads=[Rysb], writes=[Rst])
        P.op("vector", lambda e, s2_=s2_: e.tensor_reduce(out=s2_, in_=h3(ysq), axis=AX.X, op=ALU.add), reads=[Rysq], writes=[Rst])
        TS(P, "vector", mean, s1, 1.0 / 64, None, ALU.mult, None, [Rst], [Rst])
        TT(P, "vector", m2, mean, mean, ALU.mult, [Rst], [Rst])
        STT(P, var, s2_, 1.0 / 64, m2, ALU.mult, ALU.subtract, [Rst], [Rst])
        TS(P, "vector", var, var, GN_EPS, None, ALU.add, None, [Rst], [Rst])
        ACT(P, var, var, AF.Sqrt, [Rst], [Rst])
        P.op("vector", lambda e, rstd=rstd, var=var: e.reciprocal(out=rstd, in_=var), reads=[Rst], writes=[Rst])
        yield
        bc16 = lambda ap: ap.unsqueeze(2).to_broadcast([128, NH, 64])
        TT(P, "gpsimd", h3(ysb), h3(ysb), bc16(mean), ALU.subtract, [Rysb, Rst], [Rysb])
        TT(P, "gpsimd", h3(ysb), h3(ysb), bc16(rstd), ALU.mult, [Rysb, Rst], [Rysb])
        TT(P, "gpsimd", h3(ysq), h3(Vk), bc16(rkt[s]), ALU.mult, [RTK[s][3], Rrkt[s], Rysq], [Rysq])
        yield
        TT(P, "gpsimd", ysb, ysb, lnw, ALU.mult, [Rysb], [Rysb])
        TT(P, "vector", ysb, ysb, lnb, ALU.add, [Rysb], [Rysb])
        TT(P, "vector", ysb, ysb, ysq, ALU.add, [Rysb, Rysq], [Rysb])
        yield
        TT(P, "gpsimd", ogt[s], ysb, gtk[s], ALU.mult, [Rysb, Rgtk[s]], [Rogt[s]])
        DMA(P, "sync", og_dram[n * 128:(n + 1) * 128, :], ogt[s], [Rogt[s]], [Rogd[s]])
        yield

    def lockstep(gens):
        gens = list(gens)
        while gens:
            alive = []
            for gn in gens:
                try:
                    next(gn)
                    alive.append(gn)
                except StopIteration:
                    pass
            gens = alive

    loads(0)
    lockstep([stageA(0, g4, g4) for g4 in range(4)])
    for n in range(nchunks):
        if n + 1 < nchunks:
            loads(n + 1)
            lockstep([stageB(n)] + [stageA(n + 1, g4, g4) for g4 in range(4)])
        else:
            lockstep([stageB(n)])


def rwkv_mixer(C, h_dram, Rh, g_dram, W, SC, upto="all", prefetch=None):
    P, A = C.P, C.A
    xe = xnT_ext_phase(C, h_dram, Rh, g_dram)
    P.barrier()
    A.reset(C.base2)
    pc, om = rw_params(C, W)
    rw_r1a(C, xe, W, SC, pc, om)
    if upto == "r1a":
        return
    rw_r1b(C, xe, W, SC, pc, om)
    if upto == "r1b":
        return
    rw_r2(C, W, SC, SC["og"], nchunks=(int(upto[2:]) if upto.startswith("r2") and len(upto) > 2 else NB))
    if upto.startswith("r2"):
        return
    outproj_phase(C, SC["og"], D, W["w_out"], h_dram, Rh, prefetch=prefetch)


def _consts():
    i = np.arange(128)
    c = {}
    c["c_ident"] = np.eye(128, dtype=np.float32)
    c["c_maskT"] = np.triu(np.ones((128, 128), np.float32))
    lg = np.log(1.0 - 2.0 ** (-5.0 - np.arange(4, dtype=np.float64)))
    cc = np.arange(128, dtype=np.float64)
    tab = np.stack([np.stack([np.exp((cc + 1) * lg[h]), np.exp(-(cc + 1) * lg[h]) * 256 ** -0.5,
                              np.exp((127 - cc) * lg[h]) * 256 ** -0.5]) for h in range(4)])
    c["c_rettab"] = tab.reshape(4, 384).astype(np.float32)
    inv = (10000.0 ** (-np.linspace(0.0, 1.0, 128, dtype=np.float32))).astype(np.float32)
    c["c_invt"] = (inv.astype(np.float64) / (2 * np.pi)).astype(np.float32).reshape(128, 1)
    c["c_iota"] = np.tile(np.arange(513, dtype=np.float32)[None], (128, 1))
    c["c_msk2"] = np.stack([((i // 16) % 2 == 0), ((i // 16) % 2 == 1)], 1).astype(np.float32)
    c["c_blk1"] = (i[:, None] // 64 == i[None, :] // 64).astype(np.float32)
    c["c_ind2"] = np.stack([(i < 64), (i >= 64)], 1).astype(np.float32)
    a, b = i[:, None], i[None, :]
    mUs = (a < b); mUi = (a <= b); mLs = (a > b)
    D16 = (a // 16 == b // 16)
    O16 = (a // 32 == b // 32) & (a // 16 != b // 16)
    O32 = (a // 64 == b // 64) & (a // 32 != b // 32)
    O64 = (a // 64 != b // 64)
    c["c_masks"] = np.stack([mUs, mUi, mLs, mUs & D16, mLs & D16, O16, O32, O64]).astype(np.float32)
    return c


_RET_GL = [float(np.exp(128 * np.log(1.0 - 2.0 ** (-5.0 - h)))) for h in range(4)]

_IN_SHAPES = {
    "x": ([T, D], F32), "p": ([4, T, 256], F32), "positions": ([T], I32), "norm_g": ([4, 4, D], F32), "final_g": ([D], F32),
    "ffn_w_gu": ([4, 2, D, 2 * FF], F32), "ffn_w_d": ([4, 2, FF, D], F32), "ple_w_proj": ([4, 256, D], F32), "ple_w_gate": ([4, D, D], F32),
    "gla_w_in": ([1, D, 3088], F32), "gla_w_gate_up": ([1, 16, 512], F32), "gla_b_gate": ([1, 512], F32), "gla_norm_g": ([1, 256], F32),
    "gla_w_out": ([1, D, D], F32), "ret_w_in": ([1, D, 6144], F32), "ret_w_out": ([1, 2048, D], F32),
    "s5_lam_re": ([1, 64, 64], F32), "s5_lam_im": ([1, 64, 64], F32), "s5_log_dt": ([1, 64], F32), "s5_b_re": ([1, 64, 64, 16], F32),
    "s5_b_im": ([1, 64, 64, 16], F32), "s5_c_re": ([1, 64, 16, 64], F32), "s5_c_im": ([1, 64, 16, 64], F32), "s5_d": ([1, D], F32),
    "s5_w_glu": ([1, D, D], F32), "s5_b_glu": ([1, D], F32),
    "rw_mu": ([1, 6, D], F32), "rw_w_rkv": ([1, 3, D, D], F32), "rw_w0": ([1, D], F32), "rw_w1": ([1, D, 64], F32), "rw_w2": ([1, 64, D], F32),
    "rw_a0": ([1, D], F32), "rw_a1": ([1, D, 64], F32), "rw_a2": ([1, 64, D], F32), "rw_g1": ([1, D, 160], F32), "rw_g2": ([1, 160, D], F32),
    "rw_k_k": ([1, D], F32), "rw_k_a": ([1, D], F32), "rw_r_k": ([1, 16, 64], F32), "rw_ln_w": ([1, D], F32), "rw_ln_b": ([1, D], F32),
    "rw_w_out": ([1, D, D], F32),
    "c_ident": ([128, 128], F32), "c_maskT": ([128, 128], F32), "c_rettab": ([4, 384], F32), "c_invt": ([128, 1], F32),
    "c_iota": ([128, 513], F32), "c_msk2": ([128, 2], F32), "c_blk1": ([128, 128], F32), "c_ind2": ([128, 2], F32), "c_masks": ([8, 128, 128], F32),
}


def build_program(n_layers=4, with_final=True):
    nc = bass.Bass("TRN2", target_bir_lowering=False)
    I = {k: nc.dram_tensor(k, sh, d, kind="ExternalInput").ap() for k, (sh, d) in _IN_SHAPES.items()}
    out = nc.dram_tensor("out", [T, D], F32, kind="ExternalOutput").ap()
    sc = lambda name, shape, d=BF16: nc.dram_tensor("sc_" + name, shape, d, kind="Internal").ap()
    h = sc("h", [T, D], F32)
    with ExitStack() as es:
        C = make_ctx(nc, es)
        setup_consts(C, I["c_ident"])
        Rh = [Res("h") for _ in range(NB)]
        Rout = [Res("o") for _ in range(NB)]
        pre_a = False
        for i in range(n_layers):
            kind = i % 4
            ffn_phase(C, I["x"] if i == 0 else h, h, Rh, I["ffn_w_gu"][i, 0], I["ffn_w_d"][i, 0], I["norm_g"][i, 0], preloaded=pre_a)
            pf_b = (lambda st, Rst, i=i: ffn_prefetch_pieces(C, I["ffn_w_gu"][i, 1], I["ffn_w_d"][i, 1], I["norm_g"][i, 2], st, Rst))
            pre_b = False
            g1 = I["norm_g"][i, 1]
            if kind == 0:
                W = dict(w_in=I["gla_w_in"][0], w_gate_up=I["gla_w_gate_up"][0], b_gate=I["gla_b_gate"][0], norm_g=I["gla_norm_g"][0],
                         w_out=I["gla_w_out"][0], maskT=I["c_maskT"])
                SC = {k: sc(f"gla_{k}", [T, 1024]) for k in ("v", "sg", "og")}
                gla_mixer(C, h, Rh, g1, W, SC)
            elif kind == 1:
                W = dict(w_in=I["ret_w_in"][0], w_out=I["ret_w_out"][0], positions=I["positions"], invt=I["c_invt"], tab=I["c_rettab"],
                         maskT=I["c_maskT"], gl=_RET_GL)
                SC = {k: sc(f"ret_{k}", [T, 2048]) for k in ("v", "sg", "og")}
                ret_mixer(C, h, Rh, g1, W, SC)
            elif kind == 2:
                W = dict(lam_re=I["s5_lam_re"][0], lam_im=I["s5_lam_im"][0], log_dt=I["s5_log_dt"][0], b_re=I["s5_b_re"][0], b_im=I["s5_b_im"][0],
                         c_re=I["s5_c_re"][0], c_im=I["s5_c_im"][0], d=I["s5_d"][0], w_glu=I["s5_w_glu"][0], b_glu=I["s5_b_glu"][0],
                         iota=I["c_iota"], msk2=I["c_msk2"])
                SC = {"y": sc("s5_y", [D, T], F32)}
                s5_mixer(C, h, Rh, g1, W, SC)
            else:
                W = dict(mu=I["rw_mu"][0], w_rkv=[I["rw_w_rkv"][0, j] for j in range(3)], w0=I["rw_w0"][0], w1=I["rw_w1"][0], w2=I["rw_w2"][0],
                         a0=I["rw_a0"][0], a1=I["rw_a1"][0], a2=I["rw_a2"][0], g1=I["rw_g1"][0], g2=I["rw_g2"][0], k_k=I["rw_k_k"][0],
                         k_a=I["rw_k_a"][0], r_k=I["rw_r_k"][0].rearrange("h j -> (h j)"), ln_w=I["rw_ln_w"][0], ln_b=I["rw_ln_b"][0],
                         w_out=I["rw_w_out"][0], blk1=I["c_blk1"], ind2=I["c_ind2"], masks=I["c_masks"])
                SC = {k: sc("rw_" + k, [D, T]) for k in ("A1T", "BtT", "KtT", "R1T")}
                SC.update({k: sc("rw_" + k, [T, D]) for k in ("A1", "Bd", "Kd", "v", "g", "og")})
                SC["gC"] = sc("rw_gC", [D, 32], F32)
                SC["rk"] = sc("rw_rk", [T, 16], F32)
                rwkv_mixer(C, h, Rh, g1, W, SC)
            ffn_phase(C, h, h, Rh, I["ffn_w_gu"][i, 1], I["ffn_w_d"][i, 1], I["norm_g"][i, 2], preloaded=pre_b)
            pre_a = False
            pf_a = (lambda st, Rst, i=i: ffn_prefetch_pieces(C, I["ffn_w_gu"][i + 1, 0], I["ffn_w_d"][i + 1, 0], I["norm_g"][i + 1, 0], st, Rst)) if pre_a else None
            fin = (I["final_g"], out, Rout) if (with_final and i == n_layers - 1) else None
            ple_phase(C, h, Rh, I["ple_w_gate"][i], I["ple_w_proj"][i], I["norm_g"][i, 3], I["p"][i], prefetch=pf_a, final=fin)
        if with_final:
            pass
        else:
            A = C.A
            C.P.barrier()
            A.reset(C.base)
            tl = [A.alloc([128, D], F32) for _ in range(2)]
            Rt = [Res() for _ in range(2)]
            for blk in range(NB):
                DMA(C.P, "sync", tl[blk % 2], h[blk * 128:(blk + 1) * 128, :], [Rh[blk]], [Rt[blk % 2]])
                DMA(C.P, "sync", out[blk * 128:(blk + 1) * 128, :], tl[blk % 2], [Rt[blk % 2]], [Rout[blk]])
        C.P.emit(final_waits=Rout)
    return nc


_NC_CACHE = {}


def kernel(**inputs):
    if "nc" not in _NC_CACHE:
        _NC_CACHE["nc"] = build_program()
    nc = _NC_CACHE["nc"]
    consts = _consts()
    B = inputs["x"].shape[0]
    in_maps = []
    for b in range(B):
        m = {}
        for k in _IN_SHAPES:
            if k.startswith("c_"):
                m[k] = consts[k]
            elif k == "x":
                m[k] = np.ascontiguousarray(inputs["x"][b])
            elif k == "p":
                m[k] = np.ascontiguousarray(inputs["p"][:, b])
            elif k == "positions":
                m[k] = np.ascontiguousarray(inputs["positions"][b]).astype(np.int32)
            else:
                m[k] = np.asarray(inputs[k])
        in_maps.append(m)
    res = run_bass_kernel_spmd(nc, in_maps, core_ids=list(range(B)))
    return np.stack([res.results[b]["out"] for b in range(B)], axis=0).astype(np.float32)
```

```python
import numpy as np
from contextlib import ExitStack
import concourse.bass as bass
import concourse.mybir as mybir
from concourse.bass_utils import run_bass_kernel_spmd

F32 = mybir.dt.float32
BF16 = mybir.dt.bfloat16
I32 = mybir.dt.int32
AF = mybir.ActivationFunctionType
ALU = mybir.AluOpType
AX = mybir.AxisListType

T = 4096
D = 1024
FF = 2816
NB = T // 128
EPS = 1e-6
ENGS = ("sync", "scalar", "gpsimd", "tensor", "vector")


class Res:
    __slots__ = ("name", "w", "r", "dsem", "dcnt")

    def __init__(self, name=""):
        self.name = name
        self.w = None
        self.r = []
        self.dsem = None
        self.dcnt = 0


class Prog:
    def __init__(self, nc, same_engine_sync=True):
        self.nc = nc
        self.ops = {e: [] for e in ENGS}
        self.seen = {e: {} for e in ENGS}
        self.dres = []
        self.free_dsems = []
        self.ndsem = 0
        self.same_engine_sync = same_engine_sync
        self.last_ev = {e: None for e in ENGS}

    def _need(self, eng, ev, waits):
        if ev is None:
            return
        if ev[0] == 'E':
            _, peng, idx = ev
            if peng == eng and (eng == "tensor" or not self.same_engine_sync):
                return
            key = ('E', peng)
            val = idx
        else:
            _, sid, cnt = ev
            key = ('D', sid)
            val = cnt
        if self.seen[eng].get(key, -1) >= val:
            return
        self.seen[eng][key] = val
        waits.append(ev)

    def op(self, eng, fn, reads=(), writes=(), dma=False):
        waits = []
        for r in reads:
            self._need(eng, r.w, waits)
        for w in writes:
            self._need(eng, w.w, waits)
            for ev in w.r:
                self._need(eng, ev, waits)
        if dma:
            res = writes[0]
            if res.dsem is None:
                if self.free_dsems:
                    res.dsem, res.dcnt = self.free_dsems.pop()
                else:
                    res.dsem = self.ndsem
                    self.ndsem += 1
                    res.dcnt = 0
                self.dres.append(res)
            res.dcnt += 16
            ev = ('D', res.dsem, res.dcnt)
        else:
            ev = ('E', eng, len(self.ops[eng]))
            self.last_ev[eng] = ev
        self.ops[eng].append([fn, waits, ev, dma])
        for r in reads:
            r.r.append(ev)
        for w in writes:
            w.w = ev
            w.r = []
        return ev

    def barrier(self, keep=()):
        evs = [self.last_ev[e] for e in ENGS if self.last_ev[e] is not None]
        evs += [('D', r.dsem, r.dcnt) for r in self.dres]
        for e in ENGS:
            waits = []
            for ev in evs:
                self._need(e, ev, waits)
            if waits:
                self.ops[e].append([None, waits, None, False])
        newd = []
        for r in self.dres:
            if r in keep:
                newd.append(r)
            else:
                self.free_dsems.append((r.dsem, r.dcnt))
                r.dsem = None
                r.w = None
                r.r = []
        self.dres = newd

    def emit(self, final_waits=()):
        nc = self.nc
        fw = []
        for r in final_waits:
            self._need("sync", r.w, fw)
        self.ops["sync"].append([None, fw, None, False])
        needed = {e: set() for e in ENGS}
        for e in ENGS:
            for fn, waits, ev, dma in self.ops[e]:
                for w in waits:
                    if w[0] == 'E':
                        needed[w[1]].add(w[2])
        semval = {e: {} for e in ENGS}
        for e in ENGS:
            c = 0
            for i in sorted(needed[e]):
                c += 1
                semval[e][i] = c
            print(f"[prog] {e}: {len(self.ops[e])} ops, {c} signals", flush=True)
        print(f"[prog] dma sems: {self.ndsem}", flush=True)
        with ExitStack() as es:
            esem = {e: es.enter_context(nc.semaphore(f"es_{e}")) for e in ENGS}
            dsem = [es.enter_context(nc.semaphore(f"ds_{i}")) for i in range(self.ndsem)]
            block = es.enter_context(nc.Block())

            def mk(e):
                def body(engobj):
                    for i, (fn, waits, ev, dma) in enumerate(self.ops[e]):
                        for w in waits:
                            if w[0] == 'E':
                                engobj.wait_ge(esem[w[1]], semval[w[1]][w[2]])
                            else:
                                engobj.wait_ge(dsem[w[1]], w[2])
                        if fn is None:
                            continue
                        ins = fn(engobj)
                        if dma:
                            ins.then_inc(dsem[ev[1]], 16)
                        elif i in semval[e]:
                            ins.then_inc(esem[e], 1)
                return body

            for e in ENGS:
                getattr(block, e)(mk(e))


class Arena:
    def __init__(self, tensor, n32):
        self.t = tensor
        self.n32 = n32
        self.off = 0

    def reset(self, to=0):
        self.off = to

    def alloc_at(self, off32, shape, dt):
        save = self.off
        self.off = off32
        ap = self.alloc(shape, dt)
        end = self.off
        self.off = save
        return ap, end

    def alloc(self, shape, dt):
        nel = 1
        for s in shape[1:]:
            nel *= s
        n32 = nel if dt in (F32, I32) else (nel + 1) // 2
        n32 = (n32 + 7) // 8 * 8
        assert self.off + n32 <= self.n32, f"arena overflow {self.off}+{n32}>{self.n32}"
        ap = self.t[:, self.off:self.off + n32]
        self.off += n32
        if dt != F32:
            ap = ap.bitcast(dt)
        ap = ap[:, 0:nel]
        np_ = shape[0]
        if np_ < 128:
            ap = ap[0:np_, :]
        if len(shape) == 3:
            ap = ap.rearrange("p (a b) -> p a b", b=shape[2])
        elif len(shape) == 4:
            ap = ap.rearrange("p (a b c) -> p a b c", b=shape[2], c=shape[3])
        return ap


class Ctx:
    pass


ARENA_N32 = 206 * 256
FFN_TOP_N32 = 8 * 5632 // 2 + 22 * 1024 // 2 + 1024
FFN_TOP_OFF = ARENA_N32 - FFN_TOP_N32


def make_ctx(nc, es):
    C = Ctx()
    C.nc = nc
    C.P = Prog(nc)
    arena_t = es.enter_context(nc.sbuf_tensor("arena", [128, ARENA_N32], F32))
    C.A = Arena(arena_t, ARENA_N32)
    C.ps = es.enter_context(nc.psum_tensor("psum", [128, 4096], F32))
    C.ident_f = C.A.alloc([128, 128], F32)
    C.ident_b = C.A.alloc([128, 128], BF16)
    C.base = C.A.off
    C.Rconst = Res("const")
    return C


def bank(C, b, dt=F32):
    ap = C.ps[:, b * 512:(b + 1) * 512]
    if dt != F32:
        ap = ap.bitcast(dt)
    return ap


def setup_consts(C, ident_dram):
    P = C.P
    P.op("sync", lambda e: e.dma_start(out=C.ident_f, in_=ident_dram), writes=[C.Rconst], dma=True)
    P.op("vector", lambda e: e.tensor_copy(out=C.ident_b, in_=C.ident_f), reads=[C.Rconst], writes=[C.Rconst])


def bcast_row(dram_vec_ap, n):
    return dram_vec_ap.rearrange("(o n) -> o n", o=1).to_broadcast([128, n])


def norm_rows(C, ht, Rht, gb, xn, Rxn, ss, Rss, junk, Rjunk, eps=EPS, dfeat=D):
    P = C.P
    P.op("scalar", lambda e: e.activation(out=junk, in_=ht, func=AF.Square, accum_out=ss[:, 0:1]),
         reads=[Rht], writes=[Rjunk, Rss])
    P.op("vector", lambda e: e.tensor_scalar(out=ss[:, 1:2], in0=ss[:, 0:1], scalar1=1.0 / dfeat, scalar2=eps,
                                             op0=ALU.mult, op1=ALU.add), reads=[Rss], writes=[Rss])
    P.op("scalar", lambda e: e.activation(out=ss[:, 1:2], in_=ss[:, 1:2], func=AF.Sqrt), reads=[Rss], writes=[Rss])
    P.op("vector", lambda e: e.reciprocal(out=ss[:, 1:2], in_=ss[:, 1:2]), reads=[Rss], writes=[Rss])
    if gb is not None:
        P.op("vector", lambda e: e.scalar_tensor_tensor(out=xn, in0=ht, scalar=ss[:, 1:2], in1=gb,
                                                        op0=ALU.mult, op1=ALU.mult),
             reads=[Rht, Rss], writes=[Rxn])
    else:
        P.op("vector", lambda e: e.tensor_scalar(out=xn, in0=ht, scalar1=ss[:, 1:2], scalar2=None, op0=ALU.mult),
             reads=[Rht, Rss], writes=[Rxn])


def transpose_rows(C, xn, Rxn, nchunk, pbank, Rpb, dst, Rdst, evac_eng="scalar"):
    P = C.P
    pb = bank(C, pbank, BF16)
    for c in range(nchunk):
        P.op("tensor", lambda e, c=c: e.transpose(out=pb[:, c * 128:(c + 1) * 128], in_=xn[:, c * 128:(c + 1) * 128],
                                                  identity=C.ident_b),
             reads=[Rxn, C.Rconst], writes=[Rpb])
    src = pb[:, 0:nchunk * 128].rearrange("p (a b) -> p a b", b=128)
    if evac_eng == "scalar":
        P.op("scalar", lambda e: e.copy(out=dst, in_=src), reads=[Rpb], writes=[Rdst])
    else:
        P.op("vector", lambda e: e.tensor_copy(out=dst, in_=src), reads=[Rpb], writes=[Rdst])


def load_cast_weight(C, w_dram, rows, cols, dst, Rdst, stages, Rstages, col_chunk, cnt=[0]):
    P = C.P
    nk = rows // 128
    for kc in range(nk):
        for c0 in range(0, cols, col_chunk):
            cw = min(col_chunk, cols - c0)
            i = cnt[0] % len(stages)
            cnt[0] += 1
            st, Rst = stages[i], Rstages[i]
            q = "sync" if (cnt[0] % 2 == 0) else "gpsimd"
            P.op(q, lambda e, st=st, kc=kc, c0=c0, cw=cw: e.dma_start(out=st[:, 0:cw], in_=w_dram[kc * 128:(kc + 1) * 128, c0:c0 + cw]),
                 writes=[Rst], dma=True)
            ce = ("vector", "scalar")[cnt[0] % 2]
            if ce == "vector":
                P.op("vector", lambda e, st=st, kc=kc, c0=c0, cw=cw: e.tensor_copy(out=dst[:, kc, c0:c0 + cw], in_=st[:, 0:cw]),
                     reads=[Rst], writes=[Res()])
            else:
                P.op("scalar", lambda e, st=st, kc=kc, c0=c0, cw=cw: e.copy(out=dst[:, kc, c0:c0 + cw], in_=st[:, 0:cw]),
                     reads=[Rst], writes=[Res()])


def ffn_top(C):
    A = C.A
    wgu, e1 = A.alloc_at(FFN_TOP_OFF, [128, 8, 2 * FF], BF16)
    wd, e2 = A.alloc_at(e1, [128, FF // 128, D], BF16)
    gb, e3 = A.alloc_at(e2, [128, D], F32)
    assert e3 <= ARENA_N32
    return wgu, wd, gb


def ffn_prefetch_pieces(C, wgu_dram, wd_dram, g_dram, stages, Rstages):
    P = C.P
    wgu, wd, gb = ffn_top(C)
    pieces = []
    cnt = [0]

    def mk(src, dst, cw):
        def f():
            i = cnt[0] % len(stages)
            cnt[0] += 1
            st, Rst = stages[i], Rstages[i]
            DMA(P, "sync", st[:, 0:cw], src, [], [Rst])
            CP(P, ("vector", "scalar")[cnt[0] % 2], dst, st[:, 0:cw], [Rst], [Res()])
        return f

    pieces.append(lambda: DMA(P, "sync", gb, bcast_row(g_dram, D), [], [Res()]))
    CW = 704
    for kc in range(8):
        for c0 in range(0, 2 * FF, CW):
            pieces.append(mk(wgu_dram[kc * 128:(kc + 1) * 128, c0:c0 + CW], wgu[:, kc, c0:c0 + CW], CW))
    for fc in range(FF // 128):
        pieces.append(mk(wd_dram[fc * 128:(fc + 1) * 128, :], wd[:, fc, :], D))
    return pieces


def emit_pieces(pieces, i, n):
    if not pieces:
        return
    lo = (len(pieces) * i) // n
    hi = (len(pieces) * (i + 1)) // n
    for f in pieces[lo:hi]:
        f()


def ffn_phase(C, h_in, h_dram, Rh, wgu_dram, wd_dram, g_dram, preloaded=False):
    P, A = C.P, C.A
    P.barrier()
    A.reset(C.base)
    TT_ = 256
    TT = TT_
    wgu, wd, gb = ffn_top(C)
    if not preloaded:
        stages = [A.alloc([128, FF], F32) for _ in range(2)]
        Rst = [Res("st") for _ in stages]
        Rgb = Res("gb")
        P.op("sync", lambda e: e.dma_start(out=gb, in_=bcast_row(g_dram, D)), writes=[Rgb], dma=True)
        load_cast_weight(C, wgu_dram, D, 2 * FF, wgu, None, stages, Rst, FF)
        P.barrier()
        A.reset(C.base)
    NWS = 6
    wdst = [A.alloc([128, D], F32) for _ in range(NWS)]
    Rwdst = [Res("wdst") for _ in range(NWS)]
    Rwd = [Res("wd") for _ in range(FF // 128)]
    ht = [[A.alloc([128, D], F32) for _ in range(2)] for _ in range(2)]
    Rht = [[Res("ht") for _ in range(2)] for _ in range(2)]
    xn = [A.alloc([128, D], BF16) for _ in range(2)]
    Rxn = [Res("xn") for _ in range(2)]
    junk = A.alloc([128, D], BF16)
    Rjunk = Res("junk")
    ss = [A.alloc([128, 2], F32) for _ in range(2)]
    Rss = [Res("ss") for _ in range(2)]
    xnT = [A.alloc([128, 8, TT], BF16) for _ in range(2)]
    RxnT = [[Res("xnT") for _ in range(2)] for _ in range(2)]
    hT = A.alloc([128, FF // 128, TT], BF16)
    RhT = [Res("hT") for _ in range(FF // 128)]
    sg = [A.alloc([128, TT], F32) for _ in range(2)]
    Rsg = [Res("sg") for _ in range(2)]
    Rpb = [Res("pb") for _ in range(8)]
    assert A.off <= FFN_TOP_OFF, A.off
    nfc = FF // 128
    k = 0
    kk_ = [0]

    def head_load(tt):
        s = tt % 2
        for tb in range(2):
            blk = tt * 2 + tb
            P.op("sync", lambda e, s=s, tb=tb, blk=blk: e.dma_start(out=ht[s][tb], in_=h_in[blk * 128:(blk + 1) * 128, :]),
                 reads=[Rh[blk]], writes=[Rht[s][tb]], dma=True)

    def head_norm(tt, load=True):
        if load:
            head_load(tt)
        s = tt % 2
        for tb in range(2):
            k = (tt * 2 + tb) % 2
            norm_rows(C, ht[s][tb], Rht[s][tb], gb, xn[k], Rxn[k], ss[k], Rss[k], junk, Rjunk)

    def head_T(tt):
        s = tt % 2
        for tb in range(2):
            k = (tt * 2 + tb) % 2
            transpose_rows(C, xn[k], Rxn[k], 8, k, Rpb[k], xnT[s][:, :, tb * 128:(tb + 1) * 128], RxnT[s][tb], evac_eng="vector")

    head_norm(0)
    head_T(0)
    for tt in range(T // TT):
        s = tt % 2
        if tt == 0 and not preloaded:
            for fc0 in range(NWS):
                DMA(P, "sync", wdst[fc0], wd_dram[fc0 * 128:(fc0 + 1) * 128, :], [], [Rwdst[fc0]])
        for fc in range(nfc):
            if fc == 9 and tt + 1 < T // TT:
                head_load(tt + 1)
            if fc == 14 and tt + 1 < T // TT:
                head_norm(tt + 1, load=False)
            j = fc % 2
            pg = bank(C, 2 + 2 * j)[:, 0:TT]
            pu = bank(C, 3 + 2 * j)[:, 0:TT]
            for kc in range(8):
                P.op("tensor", lambda e, pg=pg, kc=kc, fc=fc, s=s: e.matmul(pg, lhsT=wgu[:, kc, fc * 128:(fc + 1) * 128], rhs=xnT[s][:, kc, :],
                                                                          start=(kc == 0), stop=(kc == 7)),
                     reads=[RxnT[s][0], RxnT[s][1]], writes=[Rpb[2 + 2 * j]])
            for kc in range(8):
                P.op("tensor", lambda e, pu=pu, kc=kc, fc=fc, s=s: e.matmul(pu, lhsT=wgu[:, kc, FF + fc * 128:FF + (fc + 1) * 128], rhs=xnT[s][:, kc, :],
                                                                          start=(kc == 0), stop=(kc == 7)),
                     reads=[RxnT[s][0], RxnT[s][1]], writes=[Rpb[3 + 2 * j]])
            P.op("scalar", lambda e, pg=pg, j=j: e.activation(out=sg[j], in_=pg, func=AF.Silu), reads=[Rpb[2 + 2 * j]], writes=[Rsg[j]])
            P.op("vector", lambda e, pu=pu, j=j, fc=fc: e.tensor_tensor(out=hT[:, fc, :], in0=pu, in1=sg[j], op=ALU.mult),
                 reads=[Rpb[3 + 2 * j], Rsg[j]], writes=[RhT[fc]])
            if tt == 0 and not preloaded:
                CP(P, ("gpsimd", "scalar", "vector")[fc % 3], wd[:, fc, :], wdst[fc % NWS], [Rwdst[fc % NWS]], [Rwd[fc]])
                if fc + NWS < nfc:
                    DMA(P, "sync", wdst[(fc + NWS) % NWS], wd_dram[(fc + NWS) * 128:(fc + NWS + 1) * 128, :], [], [Rwdst[(fc + NWS) % NWS]])
        if tt + 1 < T // TT:
            head_T(tt + 1)
        for tb in range(2):
            blk = tt * 2 + tb
            for dh in range(2):
                b = 6 + dh
                po = bank(C, b)
                for fc in range(nfc):
                    P.op("tensor", lambda e, po=po, fc=fc, tb=tb, dh=dh: e.matmul(po, lhsT=hT[:, fc, tb * 128:(tb + 1) * 128], rhs=wd[:, fc, dh * 512:(dh + 1) * 512],
                                                                                   start=(fc == 0), stop=(fc == nfc - 1)),
                         reads=[RhT[fc], Rwd[fc]], writes=[Rpb[b]])
                P.op("vector", lambda e, po=po, s=s, tb=tb, dh=dh: e.scalar_tensor_tensor(out=ht[s][tb][:, dh * 512:(dh + 1) * 512], in0=po, scalar=0.5,
                                                                                         in1=ht[s][tb][:, dh * 512:(dh + 1) * 512], op0=ALU.mult, op1=ALU.add),
                     reads=[Rpb[b], Rht[s][tb]], writes=[Rht[s][tb]])
            P.op("gpsimd", lambda e, s=s, tb=tb, blk=blk: e.dma_start(out=h_dram[blk * 128:(blk + 1) * 128, :], in_=ht[s][tb]),
                 reads=[Rht[s][tb]], writes=[Rh[blk]], dma=True)


def ple_phase(C, h_dram, Rh, wg_dram, wp_dram, g_dram, p_dram, prefetch=None, final=None):
    P, A = C.P, C.A
    P.barrier()
    A.reset(C.base)
    wg = A.alloc([128, 8, D], BF16)
    wp = A.alloc([128, 2, D], BF16)
    gb = A.alloc([128, D], F32)
    stages = [A.alloc([128, D], F32) for _ in range(2)]
    Rst = [Res("st") for _ in stages]
    Rgb = Res("gb")
    P.op("sync", lambda e: e.dma_start(out=gb, in_=bcast_row(g_dram, D)), writes=[Rgb], dma=True)
    load_cast_weight(C, wg_dram, D, D, wg, None, stages, Rst, D)
    load_cast_weight(C, wp_dram, 256, D, wp, None, stages, Rst, D)
    P.barrier()
    pieces = prefetch(stages, [Res(), Res()]) if prefetch else []
    if final is not None:
        gf_dram, out_dram, Rout = final
        gfb = A.alloc([128, D], F32)
        DMA(P, "sync", gfb, bcast_row(gf_dram, D), [], [Res()])
        xo = [A.alloc([128, D], F32) for _ in range(2)]
        Rxo = [Res("xo") for _ in range(2)]
        ssf = [A.alloc([128, 2], F32) for _ in range(2)]
        Rssf = [Res("ssf") for _ in range(2)]
    ht = [A.alloc([128, D], F32) for _ in range(2)]
    Rht = [Res("ht") for _ in range(2)]
    pt = [A.alloc([128, 256], F32) for _ in range(2)]
    Rpt = [Res("pt") for _ in range(2)]
    pbf = [A.alloc([128, 256], BF16) for _ in range(2)]
    Rpbf = [Res("pbf") for _ in range(2)]
    xn = [A.alloc([128, D], BF16) for _ in range(2)]
    Rxn = [Res("xn") for _ in range(2)]
    junk = A.alloc([128, D], BF16)
    Rjunk = Res("junk")
    ss = [A.alloc([128, 2], F32) for _ in range(2)]
    Rss = [Res("ss") for _ in range(2)]
    xnT = [A.alloc([128, 8, 128], BF16) for _ in range(2)]
    RxnT = [Res("xnT") for _ in range(2)]
    pT = [A.alloc([128, 2, 128], BF16) for _ in range(2)]
    RpT = [Res("pT") for _ in range(2)]
    sg = [A.alloc([128, 512], F32) for _ in range(2)]
    Rsg = [Res("sg") for _ in range(2)]
    Rpb = [Res("pb") for _ in range(8)]
    for blk in range(NB):
        s = blk % 2
        P.op("sync", lambda e, s=s, blk=blk: e.dma_start(out=ht[s], in_=h_dram[blk * 128:(blk + 1) * 128, :]),
             reads=[Rh[blk]], writes=[Rht[s]], dma=True)
        P.op("sync", lambda e, s=s, blk=blk: e.dma_start(out=pt[s], in_=p_dram[blk * 128:(blk + 1) * 128, :]),
             writes=[Rpt[s]], dma=True)
        norm_rows(C, ht[s], Rht[s], gb, xn[s], Rxn[s], ss[s], Rss[s], junk, Rjunk)
        transpose_rows(C, xn[s], Rxn[s], 8, s, Rpb[s], xnT[s], RxnT[s], evac_eng="vector")
        P.op("gpsimd", lambda e, s=s: e.tensor_copy(out=pbf[s], in_=pt[s]), reads=[Rpt[s]], writes=[Rpbf[s]])
        transpose_rows(C, pbf[s], Rpbf[s], 2, 2 + s, Rpb[2 + s], pT[s], RpT[s], evac_eng="scalar")
        for dh in range(2):
            pg = bank(C, 4 + dh)
            pp = bank(C, 6 + dh)
            for kc in range(8):
                P.op("tensor", lambda e, pg=pg, kc=kc, s=s, dh=dh: e.matmul(pg, lhsT=xnT[s][:, kc, :], rhs=wg[:, kc, dh * 512:(dh + 1) * 512],
                                                                          start=(kc == 0), stop=(kc == 7)),
                     reads=[RxnT[s]], writes=[Rpb[4 + dh]])
            for kc in range(2):
                P.op("tensor", lambda e, pp=pp, kc=kc, s=s, dh=dh: e.matmul(pp, lhsT=pT[s][:, kc, :], rhs=wp[:, kc, dh * 512:(dh + 1) * 512],
                                                                          start=(kc == 0), stop=(kc == 1)),
                     reads=[RpT[s]], writes=[Rpb[6 + dh]])
            P.op("scalar", lambda e, pg=pg, dh=dh: e.activation(out=sg[dh], in_=pg, func=AF.Sigmoid), reads=[Rpb[4 + dh]], writes=[Rsg[dh]])
            P.op("vector", lambda e, pp=pp, dh=dh: e.tensor_tensor(out=sg[dh], in0=pp, in1=sg[dh], op=ALU.mult),
                 reads=[Rpb[6 + dh], Rsg[dh]], writes=[Rsg[dh]])
            P.op("gpsimd", lambda e, s=s, dh=dh: e.tensor_tensor(out=ht[s][:, dh * 512:(dh + 1) * 512], in0=ht[s][:, dh * 512:(dh + 1) * 512],
                                                                  in1=sg[dh], op=ALU.add),
                 reads=[Rsg[dh], Rht[s]], writes=[Rht[s]])
        if final is None:
            P.op("gpsimd", lambda e, s=s, blk=blk: e.dma_start(out=h_dram[blk * 128:(blk + 1) * 128, :], in_=ht[s]),
                 reads=[Rht[s]], writes=[Rh[blk]], dma=True)
        else:
            norm_rows(C, ht[s], Rht[s], gfb, xo[s], Rxo[s], ssf[s], Rssf[s], junk, Rjunk)
            DMA(P, "gpsimd", out_dram[blk * 128:(blk + 1) * 128, :], xo[s], [Rxo[s]], [Rout[blk]])
        emit_pieces(pieces, blk, NB)
    assert (not prefetch) or A.off <= FFN_TOP_OFF, A.off


def final_phase(C, h_dram, Rh, g_dram, out_dram, Rout):
    P, A = C.P, C.A
    P.barrier()
    A.reset(C.base)
    gb = A.alloc([128, D], F32)
    Rgb = Res("gb")
    P.op("sync", lambda e: e.dma_start(out=gb, in_=bcast_row(g_dram, D)), writes=[Rgb], dma=True)
    P.barrier()
    ht = [A.alloc([128, D], F32) for _ in range(2)]
    Rht = [Res("ht") for _ in range(2)]
    xo = [A.alloc([128, D], F32) for _ in range(2)]
    Rxo = [Res("xo") for _ in range(2)]
    junk = A.alloc([128, D], BF16)
    Rjunk = Res("junk")
    ss = [A.alloc([128, 2], F32) for _ in range(2)]
    Rss = [Res("ss") for _ in range(2)]
    for blk in range(NB):
        s = blk % 2
        P.op("sync", lambda e, s=s, blk=blk: e.dma_start(out=ht[s], in_=h_dram[blk * 128:(blk + 1) * 128, :]),
             reads=[Rh[blk]], writes=[Rht[s]], dma=True)
        norm_rows(C, ht[s], Rht[s], gb, xo[s], Rxo[s], ss[s], Rss[s], junk, Rjunk)
        P.op("gpsimd", lambda e, s=s, blk=blk: e.dma_start(out=out_dram[blk * 128:(blk + 1) * 128, :], in_=xo[s]),
             reads=[Rxo[s]], writes=[Rout[blk]], dma=True)


def ACT(P, out, in_, func, R=(), Wr=(), bias=None, scale=None, accum_out=None):
    kw = {}
    if bias is not None:
        kw["bias"] = bias
    if scale is not None:
        kw["scale"] = scale
    if accum_out is not None:
        kw["accum_out"] = accum_out
    return P.op("scalar", lambda e: e.activation(out=out, in_=in_, func=func, **kw), reads=R, writes=Wr)


def TT(P, eng, out, in0, in1, op, R=(), Wr=()):
    return P.op(eng, lambda e: e.tensor_tensor(out=out, in0=in0, in1=in1, op=op), reads=R, writes=Wr)


def TS(P, eng, out, in0, s1, s2, op0, op1=None, R=(), Wr=()):
    if op1 is None:
        return P.op(eng, lambda e: e.tensor_scalar(out=out, in0=in0, scalar1=s1, scalar2=None, op0=op0), reads=R, writes=Wr)
    return P.op(eng, lambda e: e.tensor_scalar(out=out, in0=in0, scalar1=s1, scalar2=s2, op0=op0, op1=op1), reads=R, writes=Wr)


def STT(P, out, in0, scalar, in1, op0, op1, R=(), Wr=()):
    return P.op("vector", lambda e: e.scalar_tensor_tensor(out=out, in0=in0, scalar=scalar, in1=in1, op0=op0, op1=op1), reads=R, writes=Wr)


def MM(P, out, lhsT, rhs, start=True, stop=True, R=(), Wr=()):
    return P.op("tensor", lambda e: e.matmul(out, lhsT=lhsT, rhs=rhs, start=start, stop=stop), reads=R, writes=Wr)


def TR(P, out, in_, ident, R=(), Wr=()):
    return P.op("tensor", lambda e: e.transpose(out=out, in_=in_, identity=ident), reads=R, writes=Wr)


def CP(P, eng, out, in_, R=(), Wr=()):
    if eng == "scalar":
        return P.op("scalar", lambda e: e.copy(out=out, in_=in_), reads=R, writes=Wr)
    return P.op(eng, lambda e: e.tensor_copy(out=out, in_=in_), reads=R, writes=Wr)


def DMA(P, q, out, in_, R=(), Wr=(), **kw):
    return P.op(q, lambda e: e.dma_start(out=out, in_=in_, **kw), reads=R, writes=Wr, dma=True)


def SCAN(P, out, d0, d1, init, R=(), Wr=()):
    return P.op("vector", lambda e: e.tensor_tensor_scan(out=out, data0=d0, data1=d1, initial=init, op0=ALU.mult, op1=ALU.add), reads=R, writes=Wr)


def MEMSET(P, eng, ap, val, Wr=()):
    return P.op(eng, lambda e: e.memset(ap, val), writes=Wr)


def xnT_all_phase(C, h_dram, Rh, g_dram):
    P, A = C.P, C.A
    P.barrier()
    A.reset(C.base)
    xnT = A.alloc([128, 8, T], BF16)
    C.base2 = A.off
    gb = A.alloc([128, D], F32)
    Rgb = Res("gb")
    P.op("sync", lambda e: e.dma_start(out=gb, in_=bcast_row(g_dram, D)), writes=[Rgb], dma=True)
    ht = [A.alloc([128, D], F32) for _ in range(2)]
    Rht = [Res("ht") for _ in range(2)]
    xn = [A.alloc([128, D], BF16) for _ in range(2)]
    Rxn = [Res("xn") for _ in range(2)]
    junk = A.alloc([128, D], BF16)
    Rjunk = Res("junk")
    ss = [A.alloc([128, 2], F32) for _ in range(2)]
    Rss = [Res("ss") for _ in range(2)]
    Rpb = [Res("pb") for _ in range(2)]
    for blk in range(NB):
        s = blk % 2
        P.op("sync", lambda e, s=s, blk=blk: e.dma_start(out=ht[s], in_=h_dram[blk * 128:(blk + 1) * 128, :]),
             reads=[Rh[blk]], writes=[Rht[s]], dma=True)
        norm_rows(C, ht[s], Rht[s], gb, xn[s], Rxn[s], ss[s], Rss[s], junk, Rjunk)
        transpose_rows(C, xn[s], Rxn[s], 8, s, Rpb[s], xnT[:, :, blk * 128:(blk + 1) * 128], Res(), evac_eng="vector")
    return xnT


def load_w_cols(C, w_dram, c0, ncols, dst, stage, Rstage, eng="vector", deint=False):
    P = C.P
    P.op("sync", lambda e: e.dma_start(out=stage, in_=w_dram[:, c0:c0 + ncols].rearrange("(kc p) j -> p kc j", p=128)),
         writes=[Rstage], dma=True)
    if deint:
        src = stage.rearrange("p k (m two) -> p k two m", two=2)
        dstv = dst.rearrange("p k (two m) -> p k two m", two=2)
    else:
        src, dstv = stage, dst
    if eng == "vector":
        P.op("vector", lambda e: e.tensor_copy(out=dstv, in_=src), reads=[Rstage], writes=[Res()])
    else:
        P.op("scalar", lambda e: e.copy(out=dstv, in_=src), reads=[Rstage], writes=[Res()])


def proj_tok_phase(C, xnT, w_dram, c0, ncols, out_dram, func):
    P, A = C.P, C.A
    P.barrier()
    A.reset(C.base2)
    wb = A.alloc([128, 8, ncols], BF16)
    m0 = A.off
    stage = A.alloc([128, 8, 512], F32)
    Rstage = Res("st")
    for j in range(0, ncols, 512):
        load_w_cols(C, w_dram, c0 + j, 512, wb[:, :, j:j + 512], stage, Rstage, eng=("vector", "gpsimd")[(j // 512) % 2])
    P.barrier()
    A.reset(m0)
    ot = [A.alloc([128, ncols], BF16) for _ in range(2)]
    Rot = [Res("ot") for _ in range(2)]
    Rod = [Res("od") for _ in range(2)]
    Rpb = [Res("pb") for _ in range(8)]
    k = 0
    for blk in range(NB):
        s = blk % 2
        for j in range(0, ncols, 512):
            b = k % 4
            k += 1
            pb = bank(C, b)
            for kc in range(8):
                P.op("tensor", lambda e, pb=pb, kc=kc, blk=blk, j=j: e.matmul(pb, lhsT=xnT[:, kc, blk * 128:(blk + 1) * 128], rhs=wb[:, kc, j:j + 512],
                                                                          start=(kc == 0), stop=(kc == 7)), writes=[Rpb[b]])
            P.op("scalar", lambda e, pb=pb, s=s, j=j: e.activation(out=ot[s][:, j:j + 512], in_=pb, func=func), reads=[Rpb[b]], writes=[Rot[s]])
        P.op("gpsimd", lambda e, s=s, blk=blk: e.dma_start(out=out_dram[blk * 128:(blk + 1) * 128, :], in_=ot[s]), reads=[Rot[s]], writes=[Rod[s]], dma=True)


def outproj_phase(C, og_dram, nfeat, wo_dram, h_dram, Rh, prefetch=None):
    P, A = C.P, C.A
    P.barrier()
    A.reset(C.base)
    nkc = nfeat // 128
    wo = A.alloc([128, nkc, D], BF16)
    stages = [A.alloc([128, D], F32) for _ in range(2)]
    Rst = [Res("st") for _ in stages]
    load_cast_weight(C, wo_dram, nfeat, D, wo, None, stages, Rst, D)
    P.barrier()
    pieces = prefetch(stages, [Res(), Res()]) if prefetch else []
    og = [A.alloc([128, nfeat], BF16) for _ in range(2)]
    Rog = [Res("og") for _ in range(2)]
    ogT = [A.alloc([128, nkc, 128], BF16) for _ in range(2)]
    RogT = [Res("ogT") for _ in range(2)]
    ht = [A.alloc([128, D], F32) for _ in range(2)]
    Rht = [Res("ht") for _ in range(2)]
    Rpb = [Res("pb") for _ in range(8)]
    for blk in range(NB):
        s = blk % 2
        P.op("sync", lambda e, s=s, blk=blk: e.dma_start(out=og[s], in_=og_dram[blk * 128:(blk + 1) * 128, :]), writes=[Rog[s]], dma=True)
        P.op("sync", lambda e, s=s, blk=blk: e.dma_start(out=ht[s], in_=h_dram[blk * 128:(blk + 1) * 128, :]), reads=[Rh[blk]], writes=[Rht[s]], dma=True)
        for c8 in range(0, nkc, 8):
            n8 = min(8, nkc - c8)
            b = (blk * 2 + c8 // 8) % 2
            transpose_rows(C, og[s][:, c8 * 128:(c8 + n8) * 128], Rog[s], n8, b, Rpb[b], ogT[s][:, c8:c8 + n8, :], RogT[s],
                           evac_eng=("vector", "scalar")[(c8 // 8) % 2])
        for dh in range(2):
            b = 2 + (blk * 2 + dh) % 4
            pb = bank(C, b)
            for kc in range(nkc):
                P.op("tensor", lambda e, pb=pb, kc=kc, s=s, dh=dh: e.matmul(pb, lhsT=ogT[s][:, kc, :], rhs=wo[:, kc, dh * 512:(dh + 1) * 512],
                                                                          start=(kc == 0), stop=(kc == nkc - 1)), reads=[RogT[s]], writes=[Rpb[b]])
            P.op("vector", lambda e, pb=pb, s=s, dh=dh: e.tensor_tensor(out=ht[s][:, dh * 512:(dh + 1) * 512], in0=pb, in1=ht[s][:, dh * 512:(dh + 1) * 512], op=ALU.add),
                 reads=[Rpb[b], Rht[s]], writes=[Rht[s]])
        P.op("gpsimd", lambda e, s=s, blk=blk: e.dma_start(out=h_dram[blk * 128:(blk + 1) * 128, :], in_=ht[s]), reads=[Rht[s]], writes=[Rh[blk]], dma=True)
        emit_pieces(pieces, blk, NB)
    assert (not prefetch) or A.off <= FFN_TOP_OFF, A.off


def la_core(C, cfg, h, qe, ke, kdT, gl, v_dram, sg_dram, og_dram, maskT, gng, eps):
    P, A = C.P, C.A
    ndc, dv = cfg["ndc"], cfg["dv"]
    m0 = A.off
    S = [A.alloc([128, dv], F32) for _ in range(ndc)]
    Sb = [A.alloc([128, dv], BF16) for _ in range(ndc)]
    RS = [Res("S") for _ in range(ndc)]
    RSb = [Res("Sb") for _ in range(ndc)]
    vb = [A.alloc([128, dv], BF16) for _ in range(3)]
    Rvb = [Res("vb") for _ in range(3)]
    sgb = [A.alloc([128, dv], BF16) for _ in range(3)]
    Rsgb = [Res("sgb") for _ in range(3)]
    scm = [A.alloc([128, 128], BF16) for _ in range(2)]
    Rscm = [Res("scm") for _ in range(2)]
    st4 = [A.alloc([128, 8], F32) for _ in range(2)]
    Rst4 = [Res("st4") for _ in range(2)]
    junk = A.alloc([128, dv], BF16)
    Rjunk = Res("junk")
    tmp = [A.alloc([128, dv], F32) for _ in range(2)]
    Rtmp = [Res("tmp") for _ in range(2)]
    ogt = [A.alloc([128, dv], BF16) for _ in range(2)]
    Rogt = [Res("ogt") for _ in range(2)]
    Rogd = [Res("ogd") for _ in range(2)]
    Rpb = [Res("pb") for _ in range(8)]
    for dc in range(ndc):
        P.op("vector", lambda e, dc=dc: e.memset(S[dc], 0.0), writes=[RS[dc]])
        P.op("vector", lambda e, dc=dc: e.memset(Sb[dc], 0.0), writes=[RSb[dc]])
    post = _la_post_factory(C, cfg, h, og_dram, gng, eps, junk, Rjunk, st4, Rst4, tmp, Rtmp, ogt, Rogt, Rogd, sgb, Rsgb, Rpb)
    deferred = None
    for blk in range(NB):
        s2, s3 = blk % 2, blk % 3
        cs = slice(blk * 128, (blk + 1) * 128)
        for lb in ([0, 1] if blk == 0 else [blk + 1]):
            if lb < NB:
                DMA(P, "sync", vb[lb % 3], v_dram[lb * 128:(lb + 1) * 128, h * dv:(h + 1) * dv], [], [Rvb[lb % 3]])
                DMA(P, "sync", sgb[lb % 3], sg_dram[lb * 128:(lb + 1) * 128, h * dv:(h + 1) * dv], [], [Rsgb[lb % 3]])
        psc = bank(C, s2)[:, 0:128]
        for dc in range(ndc):
            P.op("tensor", lambda e, psc=psc, dc=dc, cs=cs: e.matmul(psc, lhsT=ke[dc][:, cs], rhs=qe[dc][:, cs], start=(dc == 0), stop=(dc == ndc - 1)),
                 writes=[Rpb[s2]])
        P.op("vector", lambda e, psc=psc, s2=s2: e.tensor_tensor(out=scm[s2], in0=psc, in1=maskT, op=ALU.mult), reads=[Rpb[s2]], writes=[Rscm[s2]])
        po = bank(C, 2 + s2)[:, 0:dv]
        P.op("tensor", lambda e, po=po, s2=s2, s3=s3: e.matmul(po, lhsT=scm[s2], rhs=vb[s3], start=True, stop=False),
             reads=[Rscm[s2], Rvb[s3]], writes=[Rpb[2 + s2]])
        for dc in range(ndc):
            P.op("tensor", lambda e, po=po, dc=dc, cs=cs: e.matmul(po, lhsT=qe[dc][:, cs], rhs=Sb[dc], start=False, stop=(dc == ndc - 1)),
                 reads=[RSb[dc]], writes=[Rpb[2 + s2]])
        for dc in range(ndc):
            b = 4 + (blk * ndc + dc) % 4
            pst = bank(C, b)[:, 0:dv]
            P.op("tensor", lambda e, pst=pst, dc=dc, blk=blk, s3=s3: e.matmul(pst, lhsT=kdT[:, blk, dc * 128:(dc + 1) * 128], rhs=vb[s3], start=True, stop=True),
                 reads=[Rvb[s3]], writes=[Rpb[b]])
            glv = gl if isinstance(gl, float) else gl[:, dc, blk:blk + 1]
            P.op("vector", lambda e, pst=pst, dc=dc, glv=glv: e.scalar_tensor_tensor(out=Sb[dc], in0=S[dc], scalar=glv, in1=pst, op0=ALU.mult, op1=ALU.add),
                 reads=[Rpb[b], RS[dc]], writes=[RSb[dc]])
            P.op("vector", lambda e, pst=pst, dc=dc, glv=glv: e.scalar_tensor_tensor(out=S[dc], in0=S[dc], scalar=glv, in1=pst, op0=ALU.mult, op1=ALU.add),
                 reads=[Rpb[b], RS[dc]], writes=[RS[dc]])
        if deferred is not None:
            deferred()
        deferred = (lambda blk=blk, s2=s2, s3=s3, po=po: post(blk, s2, s3, po))
    deferred()
    A.reset(m0)


def _la_post_factory(C, cfg, h, og_dram, gng, eps, junk, Rjunk, st4, Rst4, tmp, Rtmp, ogt, Rogt, Rogd, sgb, Rsgb, Rpb):
    P = C.P
    dv = cfg["dv"]

    def post(blk, s2, s3, po):
        if cfg["kind"] == "gla":
            P.op("scalar", lambda e, po=po, s2=s2: e.activation(out=junk, in_=po, func=AF.Square, accum_out=st4[s2][:, 0:1]),
                 reads=[Rpb[2 + s2]], writes=[Rjunk, Rst4[s2]])
            P.op("vector", lambda e, s2=s2: e.tensor_scalar(out=st4[s2][:, 1:2], in0=st4[s2][:, 0:1], scalar1=1.0 / dv, scalar2=eps, op0=ALU.mult, op1=ALU.add),
                 reads=[Rst4[s2]], writes=[Rst4[s2]])
            P.op("scalar", lambda e, s2=s2: e.activation(out=st4[s2][:, 1:2], in_=st4[s2][:, 1:2], func=AF.Sqrt), reads=[Rst4[s2]], writes=[Rst4[s2]])
            P.op("vector", lambda e, s2=s2: e.reciprocal(out=st4[s2][:, 1:2], in_=st4[s2][:, 1:2]), reads=[Rst4[s2]], writes=[Rst4[s2]])
            P.op("vector", lambda e, po=po, s2=s2: e.scalar_tensor_tensor(out=tmp[s2], in0=po, scalar=st4[s2][:, 1:2], in1=gng, op0=ALU.mult, op1=ALU.mult),
                 reads=[Rpb[2 + s2], Rst4[s2]], writes=[Rtmp[s2]])
        else:
            P.op("scalar", lambda e, po=po, s2=s2: e.activation(out=junk, in_=po, func=AF.Square, accum_out=st4[s2][:, 0:1]),
                 reads=[Rpb[2 + s2]], writes=[Rjunk, Rst4[s2]])
            P.op("scalar", lambda e, po=po, s2=s2: e.activation(out=junk, in_=po, func=AF.Identity, accum_out=st4[s2][:, 2:3]),
                 reads=[Rpb[2 + s2]], writes=[Rjunk, Rst4[s2]])
            P.op("vector", lambda e, s2=s2: e.tensor_scalar(out=st4[s2][:, 3:4], in0=st4[s2][:, 2:3], scalar1=1.0 / dv, scalar2=None, op0=ALU.mult),
                 reads=[Rst4[s2]], writes=[Rst4[s2]])
            P.op("vector", lambda e, s2=s2: e.tensor_tensor(out=st4[s2][:, 4:5], in0=st4[s2][:, 3:4], in1=st4[s2][:, 3:4], op=ALU.mult),
                 reads=[Rst4[s2]], writes=[Rst4[s2]])
            P.op("vector", lambda e, s2=s2: e.scalar_tensor_tensor(out=st4[s2][:, 1:2], in0=st4[s2][:, 0:1], scalar=1.0 / dv, in1=st4[s2][:, 4:5], op0=ALU.mult, op1=ALU.subtract),
                 reads=[Rst4[s2]], writes=[Rst4[s2]])
            P.op("vector", lambda e, s2=s2: e.tensor_scalar(out=st4[s2][:, 1:2], in0=st4[s2][:, 1:2], scalar1=eps, scalar2=None, op0=ALU.add),
                 reads=[Rst4[s2]], writes=[Rst4[s2]])
            P.op("scalar", lambda e, s2=s2: e.activation(out=st4[s2][:, 1:2], in_=st4[s2][:, 1:2], func=AF.Sqrt), reads=[Rst4[s2]], writes=[Rst4[s2]])
            P.op("vector", lambda e, s2=s2: e.reciprocal(out=st4[s2][:, 1:2], in_=st4[s2][:, 1:2]), reads=[Rst4[s2]], writes=[Rst4[s2]])
            P.op("vector", lambda e, po=po, s2=s2: e.tensor_scalar(out=tmp[s2], in0=po, scalar1=st4[s2][:, 3:4], scalar2=st4[s2][:, 1:2], op0=ALU.subtract, op1=ALU.mult),
                 reads=[Rpb[2 + s2], Rst4[s2]], writes=[Rtmp[s2]])
        P.op("gpsimd", lambda e, s2=s2, s3=s3: e.tensor_tensor(out=ogt[s2], in0=tmp[s2], in1=sgb[s3], op=ALU.mult),
             reads=[Rtmp[s2], Rsgb[s3]], writes=[Rogt[s2]])
        P.op("sync", lambda e, s2=s2, blk=blk: e.dma_start(out=og_dram[blk * 128:(blk + 1) * 128, h * dv:(h + 1) * dv], in_=ogt[s2]),
             reads=[Rogt[s2]], writes=[Rogd[s2]], dma=True)
    return post


def gla_mixer(C, h_dram, Rh, g_dram, W, SC, prefetch=None):
    P, A = C.P, C.A
    H, DK, DV = 4, 128, 256
    cfg = dict(ndc=1, dv=DV, kind="gla")
    xnT = xnT_all_phase(C, h_dram, Rh, g_dram)
    proj_tok_phase(C, xnT, W["w_in"], 2 * H * DK, H * DV, SC["v"], AF.Copy)
    proj_tok_phase(C, xnT, W["w_in"], 2 * H * DK + H * DV, H * DV, SC["sg"], AF.Silu)
    P.barrier()
    A.reset(C.base2)
    maskT = A.alloc([128, 128], F32)
    gng = A.alloc([128, DV], F32)
    m01 = A.alloc([128, 1024], F32)
    zw = A.alloc([128, 8, 16], BF16)
    zwst = A.alloc([128, 8, 16], F32)
    wgu = A.alloc([16, 512], BF16)
    wgust = A.alloc([16, 512], F32)
    zT = A.alloc([16, T], BF16)
    Rc = [Res("c") for _ in range(8)]
    P.op("sync", lambda e: e.dma_start(out=maskT, in_=W["maskT"]), writes=[Rc[0]], dma=True)
    P.op("sync", lambda e: e.dma_start(out=gng, in_=bcast_row(W["norm_g"], DV)), writes=[Rc[1]], dma=True)
    P.op("vector", lambda e: e.memset(m01, 1.0), writes=[Rc[2]])
    P.op("vector", lambda e: e.memset(m01.rearrange("p (a b) -> p a b", b=128)[:, :, 0:1], 0.0), writes=[Rc[2]])
    load_w_cols(C, W["w_in"], 2 * H * DK + 2 * H * DV, 16, zw, zwst, Rc[3])
    P.op("sync", lambda e: e.dma_start(out=wgust, in_=W["w_gate_up"]), writes=[Rc[4]], dma=True)
    P.op("vector", lambda e: e.tensor_copy(out=wgu, in_=wgust), reads=[Rc[4]], writes=[Rc[4]])
    P.barrier()
    Rpb = [Res("pb") for _ in range(8)]
    for j in range(T // 512):
        b = j % 2
        pz = bank(C, b)[0:16, :]
        for kc in range(8):
            P.op("tensor", lambda e, pz=pz, kc=kc, j=j: e.matmul(pz, lhsT=zw[:, kc, :], rhs=xnT[:, kc, j * 512:(j + 1) * 512], start=(kc == 0), stop=(kc == 7)),
                 writes=[Rpb[b]])
        P.op("vector", lambda e, pz=pz, j=j: e.tensor_copy(out=zT[:, j * 512:(j + 1) * 512], in_=pz), reads=[Rpb[b]], writes=[Res()])
    mh = A.off
    for h in range(H):
        P.barrier()
        A.reset(mh)
        wqk = A.alloc([128, 8, 256], BF16)
        wst = A.alloc([128, 8, 128], F32)
        Rwst = Res("wst")
        nb = A.alloc([128, 1], F32)
        Rnb = Res("nb")
        load_w_cols(C, W["w_in"], h * DK, 128, wqk[:, :, 0:128], wst, Rwst)
        load_w_cols(C, W["w_in"], H * DK + h * DK, 128, wqk[:, :, 128:256], wst, Rwst)
        P.op("sync", lambda e, h=h: e.dma_start(out=nb, in_=W["b_gate"][h * 128:(h + 1) * 128].rearrange("(p o) -> p o", o=1)), writes=[Rnb], dma=True)
        P.op("vector", lambda e: e.tensor_scalar(out=nb, in0=nb, scalar1=-1.0, scalar2=None, op0=ALU.mult), reads=[Rnb], writes=[Rnb])
        qe = [A.alloc([128, T], BF16)]
        ke = [A.alloc([128, T], BF16)]
        kdT = A.alloc([128, NB, 128], BF16)
        gl = A.alloc([128, 1, NB], F32)
        P.barrier()
        e1 = [A.alloc([128, 1024], F32) for _ in range(2)]
        Re1 = [Res("e1") for _ in range(2)]
        cum = [A.alloc([128, 1024], F32) for _ in range(2)]
        Rcum = [Res("cum") for _ in range(2)]
        ed = A.alloc([128, 1024], F32)
        Red = Res("ed")
        eu = A.alloc([128, 1024], F32)
        Reu = Res("eu")
        kdw = A.alloc([128, 1024], BF16)
        Rkdw = Res("kdw")
        Rgl = Res("gl")
        for tp in range(T // 1024):
            s = tp % 2
            pqs, pks = [], []
            for half in range(2):
                cols = slice(tp * 1024 + half * 512, tp * 1024 + (half + 1) * 512)
                bq, bk, bl = half, 2 + half, 4 + half
                pq, pk, pl = bank(C, bq), bank(C, bk), bank(C, bl)
                pqs.append((pq, bq)); pks.append((pk, bk))
                for kc in range(8):
                    P.op("tensor", lambda e, pq=pq, kc=kc, cols=cols: e.matmul(pq, lhsT=wqk[:, kc, 0:128], rhs=xnT[:, kc, cols], start=(kc == 0), stop=(kc == 7)),
                         writes=[Rpb[bq]])
                for kc in range(8):
                    P.op("tensor", lambda e, pk=pk, kc=kc, cols=cols: e.matmul(pk, lhsT=wqk[:, kc, 128:256], rhs=xnT[:, kc, cols], start=(kc == 0), stop=(kc == 7)),
                         writes=[Rpb[bk]])
                P.op("tensor", lambda e, pl=pl, cols=cols, h=h: e.matmul(pl, lhsT=wgu[:, h * 128:(h + 1) * 128], rhs=zT[:, cols], start=True, stop=True),
                     writes=[Rpb[bl]])
                P.op("scalar", lambda e, pl=pl, s=s, half=half: e.activation(out=e1[s][:, half * 512:(half + 1) * 512], in_=pl, func=AF.Exp, bias=nb, scale=-1.0),
                     reads=[Rpb[bl], Rnb], writes=[Re1[s]])
            P.op("scalar", lambda e, s=s: e.activation(out=e1[s], in_=e1[s], func=AF.Ln, bias=1.0, scale=1.0), reads=[Re1[s]], writes=[Re1[s]])
            P.op("vector", lambda e, s=s: e.tensor_tensor_scan(out=cum[s], data0=m01, data1=e1[s], initial=0.0, op0=ALU.mult, op1=ALU.add),
                 reads=[Re1[s]], writes=[Rcum[s]])
            P.op("scalar", lambda e, s=s: e.activation(out=ed, in_=cum[s], func=AF.Exp, scale=-1.0 / 16), reads=[Rcum[s]], writes=[Red])
            for half in range(2):
                pq, bq = pqs[half]
                P.op("vector", lambda e, pq=pq, half=half, tp=tp: e.scalar_tensor_tensor(out=qe[0][:, tp * 1024 + half * 512:tp * 1024 + (half + 1) * 512], in0=pq,
                                                                                       scalar=float(DK) ** -0.5, in1=ed[:, half * 512:(half + 1) * 512], op0=ALU.mult, op1=ALU.mult),
                     reads=[Rpb[bq], Red], writes=[Res()])
            P.op("scalar", lambda e, s=s: e.activation(out=eu, in_=cum[s], func=AF.Exp, scale=1.0 / 16), reads=[Rcum[s]], writes=[Reu])
            for half in range(2):
                pk, bk = pks[half]
                P.op("vector", lambda e, pk=pk, half=half: e.tensor_tensor(out=eu[:, half * 512:(half + 1) * 512], in0=pk, in1=eu[:, half * 512:(half + 1) * 512], op=ALU.mult),
                     reads=[Rpb[bk], Reu], writes=[Reu])
            P.op("gpsimd", lambda e, tp=tp: e.tensor_copy(out=ke[0][:, tp * 1024:(tp + 1) * 1024], in_=eu), reads=[Reu], writes=[Res()])
            P.op("scalar", lambda e, s=s, tp=tp: e.activation(out=gl[:, 0, tp * 8:(tp + 1) * 8], in_=cum[s].rearrange("p (a b) -> p a b", b=128)[:, :, 127],
                                                              func=AF.Exp, scale=-1.0 / 16), reads=[Rcum[s]], writes=[Rgl])
            P.op("vector", lambda e, tp=tp: e.tensor_tensor(out=kdw.rearrange("p (a b) -> p a b", b=128), in0=eu.rearrange("p (a b) -> p a b", b=128),
                                                            in1=gl[:, 0, tp * 8:(tp + 1) * 8].unsqueeze(2).to_broadcast([128, 8, 128]), op=ALU.mult),
                 reads=[Reu, Rgl], writes=[Rkdw])
            bt = 6 + tp % 2
            pt = bank(C, bt, BF16)
            for j in range(8):
                P.op("tensor", lambda e, pt=pt, j=j: e.transpose(out=pt[:, j * 128:(j + 1) * 128], in_=kdw[:, j * 128:(j + 1) * 128], identity=C.ident_b),
                     reads=[Rkdw], writes=[Rpb[bt]])
            P.op("vector", lambda e, pt=pt, tp=tp: e.tensor_copy(out=kdT[:, tp * 8:(tp + 1) * 8, :], in_=pt.rearrange("p (a b) -> p a b", b=128)),
                 reads=[Rpb[bt]], writes=[Res()])
        P.barrier()
        la_core(C, cfg, h, qe, ke, kdT, gl, SC["v"], SC["sg"], SC["og"], maskT, gng, EPS)
    outproj_phase(C, SC["og"], H * DV, W["w_out"], h_dram, Rh, prefetch=prefetch)


def sincos_tables(C, pos_dram, invt, cosT, sinT, work_mark):
    P, A = C.P, C.A
    A.reset(work_mark)
    PI2 = 6.28318
    PI1 = 3.14159
    pi_ = [A.alloc([128, 1024], I32) for _ in range(2)]
    a = [A.alloc([128, 1024], F32) for _ in range(2)]
    ki = [A.alloc([128, 1024], I32) for _ in range(2)]
    R1 = [Res() for _ in range(2)]; R2 = [Res() for _ in range(2)]; R3 = [Res() for _ in range(2)]
    for tp in range(T // 1024):
        s = tp % 2
        cs = slice(tp * 1024, (tp + 1) * 1024)
        P.op("sync", lambda e, s=s, cs=cs: e.dma_start(out=pi_[s], in_=pos_dram[cs].rearrange("(o n) -> o n", o=1).to_broadcast([128, 1024])),
             writes=[R1[s]], dma=True)
        P.op("vector", lambda e, s=s: e.tensor_scalar(out=a[s], in0=pi_[s], scalar1=invt, scalar2=None, op0=ALU.mult), reads=[R1[s]], writes=[R2[s]])
        P.op("vector", lambda e, s=s: e.tensor_copy(out=ki[s], in_=a[s]), reads=[R2[s]], writes=[R3[s]])
        P.op("vector", lambda e, s=s: e.tensor_tensor(out=a[s], in0=a[s], in1=ki[s], op=ALU.subtract), reads=[R2[s], R3[s]], writes=[R2[s]])
        P.op("vector", lambda e, s=s: e.scalar_tensor_tensor(out=a[s], in0=a[s], scalar=0.5, in1=a[s], op0=ALU.is_gt, op1=ALU.subtract),
             reads=[R2[s]], writes=[R2[s]])
        P.op("scalar", lambda e, s=s, cs=cs: e.activation(out=sinT[:, cs], in_=a[s], func=AF.Sin, scale=-PI2), reads=[R2[s]], writes=[Res()])
        P.op("scalar", lambda e, s=s, cs=cs: e.activation(out=cosT[:, cs], in_=a[s], func=AF.Sin, scale=-PI1), reads=[R2[s]], writes=[R3[s]])
        P.op("scalar", lambda e, s=s, cs=cs: e.activation(out=cosT[:, cs], in_=cosT[:, cs], func=AF.Square), reads=[R3[s]], writes=[R3[s]])
        P.op("vector", lambda e, s=s, cs=cs: e.tensor_scalar(out=cosT[:, cs], in0=cosT[:, cs], scalar1=-2.0, scalar2=1.0, op0=ALU.mult, op1=ALU.add),
             reads=[R3[s]], writes=[R3[s]])
    P.barrier()
    A.reset(work_mark)


def ret_mixer(C, h_dram, Rh, g_dram, W, SC, prefetch=None):
    P, A = C.P, C.A
    H, DV = 4, 512
    cfg = dict(ndc=2, dv=DV, kind="ret")
    xnT = xnT_all_phase(C, h_dram, Rh, g_dram)
    proj_tok_phase(C, xnT, W["w_in"], 2048, H * DV, SC["v"], AF.Copy)
    proj_tok_phase(C, xnT, W["w_in"], 4096, H * DV, SC["sg"], AF.Silu)
    P.barrier()
    A.reset(C.base2)
    maskT = A.alloc([128, 128], F32)
    invt = A.alloc([128, 1], F32)
    cosT = A.alloc([128, T], F32)
    sinT = A.alloc([128, T], F32)
    Rc = [Res("c") for _ in range(4)]
    P.op("sync", lambda e: e.dma_start(out=maskT, in_=W["maskT"]), writes=[Rc[0]], dma=True)
    P.op("sync", lambda e: e.dma_start(out=invt, in_=W["invt"]), writes=[Rc[1]], dma=True)
    P.barrier()
    mh = A.off
    sincos_tables(C, W["positions"], invt, cosT, sinT, mh)
    Rpb = [Res("pb") for _ in range(8)]
    for h in range(H):
        P.barrier()
        A.reset(mh)
        qe = [A.alloc([128, T], BF16) for _ in range(2)]
        ke = [A.alloc([128, T], BF16) for _ in range(2)]
        kdT = A.alloc([128, NB, 256], BF16)
        tab = A.alloc([128, 3, 128], F32)
        mcore = A.off
        wq = A.alloc([128, 8, 256], BF16)
        wk = A.alloc([128, 8, 256], BF16)
        wst = A.alloc([128, 8, 256], F32)
        Rwst = Res("wst")
        Rtab = Res("tab")
        load_w_cols(C, W["w_in"], h * 256, 256, wq, wst, Rwst, deint=True)
        load_w_cols(C, W["w_in"], 1024 + h * 256, 256, wk, wst, Rwst, deint=True)
        P.op("sync", lambda e, h=h: e.dma_start(out=tab.rearrange("p a b -> p (a b)"), in_=bcast_row(W["tab"][h], 384)), writes=[Rtab], dma=True)
        P.barrier()
        tw = [[A.alloc([128, 512], F32) for _ in range(6)] for _ in range(2)]
        Rtw = [[Res() for _ in range(6)] for _ in range(2)]
        kdw = [A.alloc([128, 512], BF16) for _ in range(2)]
        Rkdw = [Res() for _ in range(2)]
        for j in range(T // 512):
            cols = slice(j * 512, (j + 1) * 512)
            b0 = (j % 2) * 4
            for qk, wmat in ((0, wq), (1, wk)):
                for eo in range(2):
                    b = b0 + qk * 2 + eo
                    pb = bank(C, b)
                    for kc in range(8):
                        P.op("tensor", lambda e, pb=pb, kc=kc, wmat=wmat, eo=eo, cols=cols: e.matmul(pb, lhsT=wmat[:, kc, eo * 128:(eo + 1) * 128], rhs=xnT[:, kc, cols],
                                                                                                 start=(kc == 0), stop=(kc == 7)), writes=[Rpb[b]])
            for qk in range(2):
                pe_, po_ = bank(C, b0 + qk * 2), bank(C, b0 + qk * 2 + 1)
                Re_, Ro_ = Rpb[b0 + qk * 2], Rpb[b0 + qk * 2 + 1]
                t = tw[qk]; Rt = Rtw[qk]
                P.op("vector", lambda e, pe_=pe_, t=t, cols=cols: e.tensor_tensor(out=t[0], in0=pe_, in1=cosT[:, cols], op=ALU.mult), reads=[Re_], writes=[Rt[0]])
                P.op("vector", lambda e, po_=po_, t=t, cols=cols: e.tensor_tensor(out=t[1], in0=po_, in1=sinT[:, cols], op=ALU.mult), reads=[Ro_], writes=[Rt[1]])
                P.op("vector", lambda e, po_=po_, t=t, cols=cols: e.tensor_tensor(out=t[2], in0=po_, in1=cosT[:, cols], op=ALU.mult), reads=[Ro_], writes=[Rt[2]])
                P.op("vector", lambda e, pe_=pe_, t=t, cols=cols: e.tensor_tensor(out=t[3], in0=pe_, in1=sinT[:, cols], op=ALU.mult), reads=[Re_], writes=[Rt[3]])
                P.op("vector", lambda e, t=t: e.tensor_tensor(out=t[4], in0=t[0], in1=t[1], op=ALU.subtract), reads=[Rt[0], Rt[1]], writes=[Rt[4]])
                P.op("vector", lambda e, t=t: e.tensor_tensor(out=t[5], in0=t[2], in1=t[3], op=ALU.add), reads=[Rt[2], Rt[3]], writes=[Rt[5]])
                v3 = lambda ap: ap.rearrange("p (a b) -> p a b", b=128)
                tb_ = lambda i: tab[:, i, :].unsqueeze(1).to_broadcast([128, 4, 128])
                if qk == 0:
                    for dc in range(2):
                        P.op("gpsimd", lambda e, t=t, dc=dc, cols=cols: e.tensor_tensor(out=v3(qe[dc][:, cols]), in0=v3(t[4 + dc]), in1=tb_(0), op=ALU.mult),
                             reads=[Rt[4 + dc], Rtab], writes=[Res()])
                else:
                    for dc in range(2):
                        P.op("gpsimd", lambda e, t=t, dc=dc, cols=cols: e.tensor_tensor(out=v3(ke[dc][:, cols]), in0=v3(t[4 + dc]), in1=tb_(1), op=ALU.mult),
                             reads=[Rt[4 + dc], Rtab], writes=[Res()])
                        P.op("gpsimd", lambda e, t=t, dc=dc: e.tensor_tensor(out=v3(kdw[dc]), in0=v3(t[4 + dc]), in1=tb_(2), op=ALU.mult),
                             reads=[Rt[4 + dc], Rtab], writes=[Rkdw[dc]])
            pt = bank(C, b0, BF16)
            for blk in range(4):
                for dc in range(2):
                    P.op("tensor", lambda e, pt=pt, blk=blk, dc=dc: e.transpose(out=pt[:, blk * 256 + dc * 128: blk * 256 + (dc + 1) * 128],
                                                                            in_=kdw[dc][:, blk * 128:(blk + 1) * 128], identity=C.ident_b),
                         reads=[Rkdw[dc]], writes=[Rpb[b0]])
            P.op("scalar", lambda e, pt=pt, j=j: e.copy(out=kdT[:, j * 4:(j + 1) * 4, :], in_=pt.rearrange("p (a b) -> p a b", b=256)),
                 reads=[Rpb[b0]], writes=[Res()])
        P.barrier()
        A.reset(mcore)
        la_core(C, cfg, h, qe, ke, kdT, W["gl"][h], SC["v"], SC["sg"], SC["og"], maskT, None, 1e-5)
    outproj_phase(C, SC["og"], H * DV, W["w_out"], h_dram, Rh, prefetch=prefetch)


PI2 = 6.28318
PI1 = 3.14159
LP = 512


def range_reduce_sincos(P, a, ki, R, sin_out, cos_out):
    P.op("vector", lambda e: e.tensor_copy(out=ki, in_=a), reads=[R], writes=[R])
    P.op("vector", lambda e: e.tensor_tensor(out=a, in0=a, in1=ki, op=ALU.subtract), reads=[R], writes=[R])
    P.op("vector", lambda e: e.scalar_tensor_tensor(out=a, in0=a, scalar=0.5, in1=a, op0=ALU.is_gt, op1=ALU.subtract), reads=[R], writes=[R])
    P.op("scalar", lambda e: e.activation(out=sin_out, in_=a, func=AF.Sin, scale=-PI2), reads=[R], writes=[R])
    P.op("scalar", lambda e: e.activation(out=cos_out, in_=a, func=AF.Sin, scale=-PI1), reads=[R], writes=[R])
    P.op("scalar", lambda e: e.activation(out=cos_out, in_=cos_out, func=AF.Square), reads=[R], writes=[R])
    P.op("vector", lambda e: e.tensor_scalar(out=cos_out, in0=cos_out, scalar1=-2.0, scalar2=1.0, op0=ALU.mult, op1=ALU.add), reads=[R], writes=[R])


def s5_mixer(C, h_dram, Rh, g_dram, W, SC):
    P, A = C.P, C.A
    xnT = xnT_all_phase(C, h_dram, Rh, g_dram)
    P.barrier()
    A.reset(C.base2)
    iota = A.alloc([128, LP + 1], F32)
    msk2 = A.alloc([128, 2], F32)
    BdT = [A.alloc([128, 32, 128], BF16) for _ in range(2)]
    CdT = [A.alloc([128, 8, 128], BF16) for _ in range(3)]
    mag = A.alloc([128, 32], F32)
    trn = A.alloc([128, 32], F32)
    mperm = A.off
    X32 = A.alloc([32, 3, 128], F32)
    ldt = A.alloc([32, 2], F32)
    par = A.alloc([128, 3, 32], F32)
    Rp = Res("par")
    Rpb = [Res("pb") for _ in range(8)]
    P.op("sync", lambda e: e.dma_start(out=iota, in_=W["iota"]), writes=[Res()], dma=True)
    P.op("sync", lambda e: e.dma_start(out=msk2, in_=W["msk2"]), writes=[Res()], dma=True)
    P.op("sync", lambda e: e.dma_start(out=X32[:, 0, :], in_=W["lam_re"].rearrange("(k two) p -> k (two p)", two=2)), writes=[Res()], dma=True)
    P.op("sync", lambda e: e.dma_start(out=X32[:, 1, :], in_=W["lam_im"].rearrange("(k two) p -> k (two p)", two=2)), writes=[Res()], dma=True)
    P.op("sync", lambda e: e.dma_start(out=ldt, in_=W["log_dt"].rearrange("(k two) -> k two", two=2)), writes=[Res()], dma=True)
    P.barrier()
    P.op("vector", lambda e: e.tensor_copy(out=X32[:, 2, :].rearrange("k (two p) -> k two p", two=2), in_=ldt.unsqueeze(2).to_broadcast([32, 2, 64])),
         writes=[Rp])
    pp = bank(C, 0)
    for j in range(3):
        P.op("tensor", lambda e, j=j: e.transpose(out=pp[:, j * 32:(j + 1) * 32], in_=X32[:, j, :], identity=C.ident_f[0:32, 0:32]), reads=[Rp], writes=[Rpb[0]])
    P.op("vector", lambda e: e.tensor_copy(out=par.rearrange("p a b -> p (a b)"), in_=pp[:, 0:96]), reads=[Rpb[0]], writes=[Rp])
    sm = [A.alloc([128, 32], F32) for _ in range(12)]
    smi = A.alloc([128, 32], I32)
    lre, lim = par[:, 0, :], par[:, 1, :]
    dt_, xr_, th_, sn, cs, lbr, lbi, den, um, fre, fim, tmp_ = sm
    V = lambda fn: P.op("vector", fn, reads=[Rp], writes=[Rp])
    S_ = lambda fn: P.op("scalar", fn, reads=[Rp], writes=[Rp])
    S_(lambda e: e.activation(out=dt_, in_=par[:, 2, :], func=AF.Exp))
    V(lambda e: e.tensor_tensor(out=xr_, in0=lre, in1=dt_, op=ALU.mult))
    V(lambda e: e.tensor_tensor(out=th_, in0=lim, in1=dt_, op=ALU.mult))
    S_(lambda e: e.activation(out=mag, in_=xr_, func=AF.Exp))
    V(lambda e: e.tensor_scalar(out=trn, in0=th_, scalar1=1.0 / (2 * np.pi), scalar2=None, op0=ALU.mult))
    V(lambda e: e.tensor_copy(out=th_, in_=trn))
    range_reduce_sincos(P, th_, smi, Rp, sn, cs)
    V(lambda e: e.tensor_tensor(out=lbr, in0=mag, in1=cs, op=ALU.mult))
    V(lambda e: e.tensor_tensor(out=lbi, in0=mag, in1=sn, op=ALU.mult))
    V(lambda e: e.tensor_tensor(out=den, in0=lre, in1=lre, op=ALU.mult))
    V(lambda e: e.tensor_tensor(out=tmp_, in0=lim, in1=lim, op=ALU.mult))
    V(lambda e: e.tensor_tensor(out=den, in0=den, in1=tmp_, op=ALU.add))
    V(lambda e: e.reciprocal(out=den, in_=den))
    V(lambda e: e.tensor_scalar(out=um, in0=lbr, scalar1=-1.0, scalar2=None, op0=ALU.add))
    V(lambda e: e.tensor_tensor(out=fre, in0=um, in1=lre, op=ALU.mult))
    V(lambda e: e.tensor_tensor(out=tmp_, in0=lbi, in1=lim, op=ALU.mult))
    V(lambda e: e.tensor_tensor(out=fre, in0=fre, in1=tmp_, op=ALU.add))
    V(lambda e: e.tensor_tensor(out=fre, in0=fre, in1=den, op=ALU.mult))
    V(lambda e: e.tensor_tensor(out=fim, in0=lbi, in1=lre, op=ALU.mult))
    V(lambda e: e.tensor_tensor(out=tmp_, in0=um, in1=lim, op=ALU.mult))
    V(lambda e: e.tensor_tensor(out=fim, in0=fim, in1=tmp_, op=ALU.subtract))
    V(lambda e: e.tensor_tensor(out=fim, in0=fim, in1=den, op=ALU.mult))
    bre = A.alloc([128, 32, 16], F32)
    bim = A.alloc([128, 32, 16], F32)
    bbr = A.alloc([128, 32, 16], F32)
    bbi = A.alloc([128, 32, 16], F32)
    t16 = A.alloc([128, 32, 16], F32)
    Rb = Res("b")
    P.op("sync", lambda e: e.dma_start(out=bre, in_=W["b_re"].rearrange("(k two) p c -> (two p) k c", two=2)), writes=[Res()], dma=True)
    P.op("sync", lambda e: e.dma_start(out=bim, in_=W["b_im"].rearrange("(k two) p c -> (two p) k c", two=2)), writes=[Res()], dma=True)
    P.barrier()
    fb = lambda f: f.unsqueeze(2).to_broadcast([128, 32, 16])
    V(lambda e: e.tensor_tensor(out=bbr, in0=bre, in1=fb(fre), op=ALU.mult))
    V(lambda e: e.tensor_tensor(out=t16, in0=bim, in1=fb(fim), op=ALU.mult))
    V(lambda e: e.tensor_tensor(out=bbr, in0=bbr, in1=t16, op=ALU.subtract))
    V(lambda e: e.tensor_tensor(out=bbi, in0=bim, in1=fb(fre), op=ALU.mult))
    V(lambda e: e.tensor_tensor(out=t16, in0=bre, in1=fb(fim), op=ALU.mult))
    V(lambda e: e.tensor_tensor(out=bbi, in0=bbi, in1=t16, op=ALU.add))
    Mall = [A.alloc([128, 8, 128], F32) for _ in range(2)]
    for ri, bb in enumerate((bbr, bbi)):
        V(lambda e, ri=ri: e.memset(Mall[ri], 0.0))
        for two in range(2):
            ps_ = slice(two * 64, (two + 1) * 64)
            dstv = Mall[ri][ps_].rearrange("p kq (kr tw c) -> p kq kr tw c", kr=4, tw=2)[:, :, :, two, :]
            srcv = bb[ps_].rearrange("p (kq kr) c -> p kq kr c", kr=4)
            V(lambda e, dstv=dstv, srcv=srcv: e.tensor_copy(out=dstv, in_=srcv))
    Ct = [A.alloc([128, 8, 64], F32) for _ in range(2)]
    Cx = [A.alloc([128, 8, 128], F32) for _ in range(2)]
    P.op("sync", lambda e: e.dma_start(out=Ct[0], in_=W["c_re"].rearrange("(kq kr two) i p -> (kr two i) kq p", kr=4, two=2)), writes=[Res()], dma=True)
    P.op("sync", lambda e: e.dma_start(out=Ct[1], in_=W["c_im"].rearrange("(kq kr two) i p -> (kr two i) kq p", kr=4, two=2)), writes=[Res()], dma=True)
    P.barrier()
    for ri in range(2):
        for two in range(2):
            dstv = Cx[ri].rearrange("p kq (two q) -> p kq two q", two=2)[:, :, two, :]
            if ri == 0:
                V(lambda e, dstv=dstv, two=two: e.tensor_scalar(out=dstv, in0=Ct[0], scalar1=msk2[:, two:two + 1], scalar2=None, op0=ALU.mult))
            else:
                V(lambda e, dstv=dstv, two=two: e.tensor_scalar(out=dstv, in0=Ct[1], scalar1=msk2[:, two:two + 1], scalar2=-1.0, op0=ALU.mult, op1=ALU.mult))
    Mz = [A.alloc([128, 32, 128], F32) for _ in range(2)]
    for ri in range(2):
        V(lambda e, ri=ri: e.memset(Mz[ri], 0.0))
        for kr in range(4):
            dstv = Mz[ri].rearrange("p (kq kr) x -> p kq kr x", kr=4)[:, :, kr, kr * 32:(kr + 1) * 32]
            srcv = Mall[ri][:, :, kr * 32:(kr + 1) * 32]
            V(lambda e, dstv=dstv, srcv=srcv: e.tensor_copy(out=dstv, in_=srcv))
    n = 0
    for src, dst, ng in ((Mz[0], BdT[0], 8), (Mz[1], BdT[1], 8), (Cx[0], CdT[0], 2), (Cx[1], CdT[1], 2)):
        for half in range(ng):
            b = 1 + n % 4
            n += 1
            pb = bank(C, b)
            for q in range(4):
                kq = half * 4 + q
                P.op("tensor", lambda e, pb=pb, q=q, kq=kq, src=src: e.transpose(out=pb[:, q * 128:(q + 1) * 128], in_=src[:, kq, :], identity=C.ident_f),
                     reads=[Rp], writes=[Rpb[b]])
            P.op("vector", lambda e, pb=pb, dst=dst, half=half: e.tensor_copy(out=dst[:, half * 4:(half + 1) * 4, :], in_=pb.rearrange("p (a b) -> p a b", b=128)),
                 reads=[Rpb[b]], writes=[Res()])
    P.barrier()
    TS(P, "vector", CdT[2], CdT[0], -1.0, None, ALU.mult, None, [], [Res()])
    P.barrier()
    A.reset(mperm)
    cT = [A.alloc([128, LP + 1], F32) for _ in range(2)]
    sT = [A.alloc([128, LP + 1], F32) for _ in range(2)]
    rfull = [A.alloc([128, LP], F32) for _ in range(2)]
    aT = A.alloc([128, LP + 1], F32)
    kiT = A.alloc([128, LP + 1], I32)
    Rtab = [Res("tab") for _ in range(2)]
    Rsc = Res("tabscratch")
    NS = 3
    tq = [[A.alloc([128, LP], F32) for _ in range(4)] for _ in range(NS)]
    Rtq = [[Res() for _ in range(4)] for _ in range(NS)]
    zq = [[A.alloc([128, LP], F32) for _ in range(2)] for _ in range(NS)]
    Rzq = [[Res() for _ in range(2)] for _ in range(NS)]
    wq = [[A.alloc([128, LP], F32) for _ in range(2)] for _ in range(NS)]
    Rwq = [[Res() for _ in range(2)] for _ in range(NS)]
    uq = [[A.alloc([128, LP], BF16) for _ in range(4)] for _ in range(2)]
    Ruq = [[Res() for _ in range(4)] for _ in range(2)]
    xb = [[A.alloc([128, LP], BF16) for _ in range(2)] for _ in range(2)]
    Rxb = [[Res() for _ in range(2)] for _ in range(2)]
    ini = [A.alloc([128, 4], F32) for _ in range(NS)]
    Rini = [Res() for _ in range(NS)]
    ysb = [A.alloc([128, LP], F32) for _ in range(2)]
    Rysb = [Res() for _ in range(2)]
    Ryd = [Res() for _ in range(2)]
    npc = T // LP
    NIT = 32 * npc

    def tables(k):
        tb = k % 2
        TS(P, "vector", aT, iota, trn[:, k:k + 1], None, ALU.mult, None, [Rsc], [Rsc])
        CP(P, "vector", kiT, aT, [Rsc], [Rsc])
        TT(P, "vector", aT, aT, kiT, ALU.subtract, [Rsc], [Rsc])
        STT(P, aT, aT, 0.5, aT, ALU.is_gt, ALU.subtract, [Rsc], [Rsc])
        ACT(P, sT[tb], aT, AF.Sin, [Rsc], [Rtab[tb]], scale=-PI2)
        ACT(P, cT[tb], aT, AF.Sin, [Rsc], [Rtab[tb]], scale=-PI1)
        ACT(P, cT[tb], cT[tb], AF.Square, [Rtab[tb]], [Rtab[tb]])
        TS(P, "vector", cT[tb], cT[tb], -2.0, 1.0, ALU.mult, ALU.add, [Rtab[tb]], [Rtab[tb]])
        TS(P, "vector", rfull[tb], iota[:, 0:LP], 0.0, mag[:, k:k + 1], ALU.mult, ALU.add, [], [Rtab[tb]])

    def stT(it):
        k, j = it // npc, it % npc
        if j == 0:
            tables(k)
        tb, s3, s2 = k % 2, it % NS, it % 2
        kq = k // 4
        cols = slice(j * LP, (j + 1) * LP)
        br_, bi_ = 2 * s2, 2 * s2 + 1
        pbre, pbim = bank(C, br_), bank(C, bi_)
        MM(P, pbre, BdT[0][:, k, :], xnT[:, kq, cols], True, True, [], [Rpb[br_]])
        MM(P, pbim, BdT[1][:, k, :], xnT[:, kq, cols], True, True, [], [Rpb[bi_]])
        c_, s_ = cT[tb][:, 0:LP], sT[tb][:, 0:LP]
        t = tq[s3]; Rt = Rtq[s3]
        pend = ini_ops(it - 1) if it >= 1 else []
        pend = pend + [None] * (4 - len(pend))
        TT(P, "vector", t[0], pbre, c_, ALU.mult, [Rpb[br_], Rtab[tb]], [Rt[0]])
        if pend[0]: pend[0]()
        TT(P, "vector", t[1], pbim, s_, ALU.mult, [Rpb[bi_], Rtab[tb]], [Rt[1]])
        if pend[1]: pend[1]()
        TT(P, "vector", t[2], pbim, c_, ALU.mult, [Rpb[bi_], Rtab[tb]], [Rt[2]])
        if pend[2]: pend[2]()
        TT(P, "vector", t[3], pbre, s_, ALU.mult, [Rpb[br_], Rtab[tb]], [Rt[3]])
        if pend[3]: pend[3]()

    def ini_ops(it):
        if it < 0 or it >= NIT:
            return []
        k, j = it // npc, it % npc
        if j == 0:
            return []
        tb, s3 = k % 2, it % NS
        p3 = (it - 1) % NS
        wrl, wil = wq[p3][0][:, LP - 1:LP], wq[p3][1][:, LP - 1:LP]
        cL, sL = cT[tb][:, LP:LP + 1], sT[tb][:, LP:LP + 1]
        iv = ini[s3]
        return [
            lambda: TS(P, "vector", iv[:, 0:1], wil, sL, None, ALU.mult, None, [Rwq[p3][1], Rtab[tb]], [Rini[s3]]),
            lambda: TS(P, "vector", iv[:, 2:3], wil, cL, None, ALU.mult, None, [Rwq[p3][1], Rtab[tb]], [Rini[s3]]),
            lambda: STT(P, iv[:, 1:2], wrl, cL, iv[:, 0:1], ALU.mult, ALU.subtract, [Rwq[p3][0], Rini[s3]], [Rini[s3]]),
            lambda: STT(P, iv[:, 3:4], wrl, sL, iv[:, 2:3], ALU.mult, ALU.add, [Rwq[p3][0], Rini[s3]], [Rini[s3]]),
        ]

    def stZ(it):
        s3 = it % NS
        t = tq[s3]; Rt = Rtq[s3]
        TT(P, "gpsimd", zq[s3][0], t[0], t[1], ALU.add, [Rt[0], Rt[1]], [Rzq[s3][0]])
        TT(P, "gpsimd", zq[s3][1], t[2], t[3], ALU.subtract, [Rt[2], Rt[3]], [Rzq[s3][1]])

    def stS(it):
        k, j = it // npc, it % npc
        tb, s3 = k % 2, it % NS
        if j == 0:
            init_r, init_i, rd = 0.0, 0.0, []
        else:
            iv = ini[s3]
            init_r, init_i, rd = iv[:, 1:2], iv[:, 3:4], [Rini[s3]]
        SCAN(P, wq[s3][0], rfull[tb], zq[s3][0], init_r, [Rzq[s3][0], Rtab[tb]] + rd, [Rwq[s3][0]])
        SCAN(P, wq[s3][1], rfull[tb], zq[s3][1], init_i, [Rzq[s3][1], Rtab[tb]] + rd, [Rwq[s3][1]])

    def stU(it):
        k = it // npc
        tb, s3, s2 = k % 2, it % NS, it % 2
        c_, s_ = cT[tb][:, 0:LP], sT[tb][:, 0:LP]
        wr, wi = wq[s3]
        u = uq[s2]; Ru = Ruq[s2]
        TT(P, "gpsimd", u[0], wr, c_, ALU.mult, [Rwq[s3][0], Rtab[tb]], [Ru[0]])
        TT(P, "gpsimd", u[1], wi, s_, ALU.mult, [Rwq[s3][1], Rtab[tb]], [Ru[1]])
        TT(P, "gpsimd", u[2], wr, s_, ALU.mult, [Rwq[s3][0], Rtab[tb]], [Ru[2]])
        TT(P, "vector", u[3], wi, c_, ALU.mult, [Rwq[s3][1], Rtab[tb]], [Ru[3]])

    def stX(it):
        k, j = it // npc, it % npc
        s2 = it % 2
        kq, kr = k // 4, k % 4
        Rs = slice(32 * kr, 32 * kr + 32)
        cols = slice(j * LP, (j + 1) * LP)
        u = uq[s2]; Ru = Ruq[s2]
        by = 4 + s2
        py = bank(C, by)
        MM(P, py, CdT[0][:, kq, :], u[0], True, False, [Ru[0]], [Rpb[by]])
        MM(P, py, CdT[2][:, kq, :], u[1], False, False, [Ru[1]], [Rpb[by]])
        MM(P, py, CdT[1][:, kq, :], u[2], False, False, [Ru[2]], [Rpb[by]])
        MM(P, py, CdT[1][:, kq, :], u[3], False, True, [Ru[3]], [Rpb[by]])
        CP(P, "scalar", ysb[s2], py, [Rpb[by]], [Rysb[s2]])
        DMA(P, "sync", SC["y"][32 * k:32 * k + 32, cols], ysb[s2][Rs, :], [Rysb[s2]], [Ryd[s2]])

    for tick in range(NIT + 2):
        if tick < NIT:
            stT(tick)
            stZ(tick)
        if tick == NIT:
            for f in ini_ops(tick - 1):
                f()
        if 0 <= tick - 1 < NIT:
            stS(tick - 1)
            stU(tick - 1)
        if 0 <= tick - 2 < NIT:
            stX(tick - 2)
    P.barrier()
    A.reset(C.base2)
    wg = A.alloc([128, 8, D], BF16)
    stage = A.alloc([128, 8, 512], F32)
    Rstage = Res()
    for j in range(0, D, 512):
        load_w_cols(C, W["w_glu"], j, 512, wg[:, :, j:j + 512], stage, Rstage)
    dcol = A.alloc([128, 8], F32)
    bcol = A.alloc([128, 8], F32)
    P.op("sync", lambda e: e.dma_start(out=dcol, in_=W["d"].rearrange("(kq p) -> p kq", p=128), allow_slow_non_contiguous=True), writes=[Res()], dma=True)
    P.op("sync", lambda e: e.dma_start(out=bcol, in_=W["b_glu"].rearrange("(kq p) -> p kq", p=128), allow_slow_non_contiguous=True), writes=[Res()], dma=True)
    P.barrier()
    yt = [A.alloc([128, 8, LP], F32) for _ in range(2)]
    Ryt = [[Res() for _ in range(8)] for _ in range(2)]
    zf = A.alloc([128, 8, LP], F32)
    Rzf = [Res() for _ in range(8)]
    zb = A.alloc([128, 8, LP], BF16)
    Rzb = [Res() for _ in range(8)]
    q1 = [A.alloc([128, LP], F32) for _ in range(2)]
    Rq1 = [Res() for _ in range(2)]
    mixT = [A.alloc([128, LP], F32) for _ in range(2)]
    RmixT = [Res() for _ in range(2)]
    ht = [A.alloc([128, D], F32) for _ in range(4)]
    Rht = [Res() for _ in range(4)]
    for tt in range(T // LP):
        s = tt % 2
        cols = slice(tt * LP, (tt + 1) * LP)
        for kq in range(8):
            P.op("sync", lambda e, s=s, kq=kq, cols=cols: e.dma_start(out=yt[s][:, kq, :], in_=SC["y"][kq * 128:(kq + 1) * 128, cols]), writes=[Ryt[s][kq]], dma=True)
        for tb in range(4):
            blk = tt * 4 + tb
            P.op("sync", lambda e, tb=tb, blk=blk: e.dma_start(out=ht[tb], in_=h_dram[blk * 128:(blk + 1) * 128, :]), reads=[Rh[blk]], writes=[Rht[tb]], dma=True)
        for kq in range(8):
            y_ = yt[s][:, kq, :]
            Ry = Ryt[s][kq]
            q = q1[kq % 2]; Rq = Rq1[kq % 2]
            P.op("vector", lambda e, y_=y_, kq=kq, cols=cols: e.scalar_tensor_tensor(out=y_, in0=xnT[:, kq, cols], scalar=dcol[:, kq:kq + 1], in1=y_, op0=ALU.mult, op1=ALU.add),
                 reads=[Ry], writes=[Ry])
            P.op("scalar", lambda e, q=q, y_=y_: e.activation(out=q, in_=y_, func=AF.Square), reads=[Ry], writes=[Rq])
            P.op("vector", lambda e, q=q: e.tensor_scalar(out=q, in0=q, scalar1=0.044715, scalar2=1.0, op0=ALU.mult, op1=ALU.add), reads=[Rq], writes=[Rq])
            P.op("gpsimd", lambda e, q=q, y_=y_: e.tensor_tensor(out=q, in0=q, in1=y_, op=ALU.mult), reads=[Rq, Ry], writes=[Rq])
            P.op("scalar", lambda e, q=q: e.activation(out=q, in_=q, func=AF.Sigmoid, scale=1.5957691216), reads=[Rq], writes=[Rq])
            P.op("vector", lambda e, q=q, y_=y_, kq=kq: e.tensor_tensor(out=zf[:, kq, :], in0=q, in1=y_, op=ALU.mult), reads=[Rq, Ry], writes=[Rzf[kq]])
            P.op("gpsimd", lambda e, kq=kq: e.tensor_copy(out=zb[:, kq, :], in_=zf[:, kq, :]), reads=[Rzf[kq]], writes=[Rzb[kq]])
        for nq in range(8):
            b = nq % 2
            pg = bank(C, b)
            for kc in range(8):
                P.op("tensor", lambda e, pg=pg, kc=kc, nq=nq: e.matmul(pg, lhsT=wg[:, kc, nq * 128:(nq + 1) * 128], rhs=zb[:, kc, :], start=(kc == 0), stop=(kc == 7)),
                     reads=[Rzb[kc]], writes=[Rpb[b]])
            m = mixT[nq % 2]; Rm = RmixT[nq % 2]
            P.op("scalar", lambda e, pg=pg, m=m, nq=nq: e.activation(out=m, in_=pg, func=AF.Sigmoid, bias=bcol[:, nq:nq + 1], scale=1.0), reads=[Rpb[b]], writes=[Rm])
            P.op("vector", lambda e, m=m, nq=nq: e.tensor_tensor(out=m, in0=m, in1=zf[:, nq, :], op=ALU.mult), reads=[Rm, Rzf[nq]], writes=[Rm])
            bt = 2 + nq % 4
            pt = bank(C, bt)
            for tb in range(4):
                P.op("tensor", lambda e, pt=pt, tb=tb, m=m: e.transpose(out=pt[:, tb * 128:(tb + 1) * 128], in_=m[:, tb * 128:(tb + 1) * 128], identity=C.ident_f),
                     reads=[Rm], writes=[Rpb[bt]])
            for tb in range(4):
                P.op("vector", lambda e, pt=pt, tb=tb, nq=nq: e.tensor_tensor(out=ht[tb][:, nq * 128:(nq + 1) * 128], in0=pt[:, tb * 128:(tb + 1) * 128],
                                                                            in1=ht[tb][:, nq * 128:(nq + 1) * 128], op=ALU.add),
                     reads=[Rpb[bt], Rht[tb]], writes=[Rht[tb]])
        for tb in range(4):
            blk = tt * 4 + tb
            P.op("gpsimd", lambda e, tb=tb, blk=blk: e.dma_start(out=h_dram[blk * 128:(blk + 1) * 128, :], in_=ht[tb]), reads=[Rht[tb]], writes=[Rh[blk]], dma=True)


C0 = 0.6065306597126334
NH = 16
GN_EPS = 64e-5


def xnT_ext_phase(C, h_dram, Rh, g_dram):
    P, A = C.P, C.A
    P.barrier()
    A.reset(C.base)
    xe = A.alloc([128, 8, T + 2], BF16)
    C.base2 = A.off
    gb = A.alloc([128, D], F32)
    Rgb = Res("gb")
    P.op("sync", lambda e: e.dma_start(out=gb, in_=bcast_row(g_dram, D)), writes=[Rgb], dma=True)
    P.op("vector", lambda e: e.memset(xe[:, :, 0:1], 0.0), writes=[Res()])
    ht = [A.alloc([128, D], F32) for _ in range(2)]
    Rht = [Res("ht") for _ in range(2)]
    xn = [A.alloc([128, D], BF16) for _ in range(2)]
    Rxn = [Res("xn") for _ in range(2)]
    junk = A.alloc([128, D], BF16)
    Rjunk = Res("junk")
    ss = [A.alloc([128, 2], F32) for _ in range(2)]
    Rss = [Res("ss") for _ in range(2)]
    Rpb = [Res("pb") for _ in range(2)]
    for blk in range(NB):
        s = blk % 2
        P.op("sync", lambda e, s=s, blk=blk: e.dma_start(out=ht[s], in_=h_dram[blk * 128:(blk + 1) * 128, :]),
             reads=[Rh[blk]], writes=[Rht[s]], dma=True)
        norm_rows(C, ht[s], Rht[s], gb, xn[s], Rxn[s], ss[s], Rss[s], junk, Rjunk)
        transpose_rows(C, xn[s], Rxn[s], 8, s, Rpb[s], xe[:, :, 1 + blk * 128:1 + (blk + 1) * 128], Res(), evac_eng="vector")
    return xe


def load_w_mu(C, w_dram, c0, ncols, dst_a, dst_b, mucol, omcol, stage, Rstage, nrows=D):
    P = C.P
    P.op("sync", lambda e: e.dma_start(out=stage, in_=w_dram[:, c0:c0 + ncols].rearrange("(kc p) j -> p kc j", p=128)),
         writes=[Rstage], dma=True)
    for kc in range(8):
        ACT(P, dst_a[:, kc, :], stage[:, kc, :], AF.Copy if False else AF.Identity, [Rstage], [Res()], scale=omcol[:, kc:kc + 1])
        ACT(P, dst_b[:, kc, :], stage[:, kc, :], AF.Identity, [Rstage], [Res()], scale=mucol[:, kc:kc + 1])


def rw_params(C, W):
    P, A = C.P, C.A
    pr = A.alloc([88, 128], F32)
    pc = A.alloc([128, 88], F32)
    om = A.alloc([128, 48], F32)
    Rp = Res("pr")
    P.op("sync", lambda e: e.dma_start(out=pr[0:48, :], in_=W["mu"].rearrange("n (kc p) -> (n kc) p", p=128)), writes=[Res()], dma=True)
    for i, nm in enumerate(("w0", "a0", "k_k", "k_a", "r_k")):
        P.op("sync", lambda e, i=i, nm=nm: e.dma_start(out=pr[48 + 8 * i:56 + 8 * i, :], in_=W[nm].rearrange("(kc p) -> kc p", p=128)), writes=[Res()], dma=True)
    P.barrier()
    pp = bank(C, 0)
    Rb = Res()
    P.op("tensor", lambda e: e.transpose(out=pp[:, 0:88], in_=pr, identity=C.ident_f[0:88, 0:88]), writes=[Rb])
    P.op("vector", lambda e: e.tensor_copy(out=pc, in_=pp[:, 0:88]), reads=[Rb], writes=[Rp])
    P.op("vector", lambda e: e.tensor_scalar(out=om, in0=pc[:, 0:48], scalar1=-1.0, scalar2=1.0, op0=ALU.mult, op1=ALU.add), reads=[Rp], writes=[Rp])
    P.barrier()
    return pc, om


def rw_r1a(C, xe, W, SC, pc, om):
    P, A = C.P, C.A
    P.barrier()
    m0 = A.off
    wva = A.alloc([128, 8, D], BF16)
    wvb = A.alloc([128, 8, D], BF16)
    g1a = A.alloc([128, 8, 160], BF16)
    g1b = A.alloc([128, 8, 160], BF16)
    g2a = A.alloc([128, D], BF16)
    g2b = A.alloc([32, D], BF16)
    m1 = A.off
    stage = A.alloc([128, 8, 512], F32)
    Rstage = Res()
    for j in range(0, D, 512):
        load_w_mu(C, W["w_rkv"][2], j, 512, wva[:, :, j:j + 512], wvb[:, :, j:j + 512], pc[:, 16:24], om[:, 16:24], stage, Rstage)
    load_w_mu(C, W["g1"], 0, 160, g1a, g1b, pc[:, 40:48], om[:, 40:48], stage[:, :, 0:160], Rstage)
    g2st = A.alloc([128, D], F32)
    Rg2 = Res()
    P.op("sync", lambda e: e.dma_start(out=g2st, in_=W["g2"][0:128, :]), writes=[Rg2], dma=True)
    P.op("vector", lambda e: e.tensor_copy(out=g2a, in_=g2st), reads=[Rg2], writes=[Res()])
    P.op("sync", lambda e: e.dma_start(out=g2st[0:32, :], in_=W["g2"][128:160, :]), reads=[Rg2], writes=[Rg2], dma=True)
    P.op("vector", lambda e: e.tensor_copy(out=g2b, in_=g2st[0:32, :]), reads=[Rg2], writes=[Res()])
    P.barrier()
    A.reset(m1)
    ot = [A.alloc([128, D], BF16) for _ in range(2)]
    Rot = [Res() for _ in range(2)]
    Rod = [Res() for _ in range(2)]
    gt = [A.alloc([128, D], BF16) for _ in range(2)]
    Rgt = [Res() for _ in range(2)]
    Rgd = [Res() for _ in range(2)]
    sga = [A.alloc([128, 512], BF16) for _ in range(2)]
    sgb = [A.alloc([32, 512], BF16) for _ in range(2)]
    Rsg = [Res() for _ in range(2)]
    Rpb = [Res() for _ in range(8)]
    k = 0
    for tp in range(T // 512):
        s = tp % 2
        xc = lambda kc, c0=tp * 512: xe[:, kc, 1 + c0:1 + c0 + 512]
        xp = lambda kc, c0=tp * 512: xe[:, kc, c0:c0 + 512]
        for (r0, r1, bnk, dst) in ((0, 128, 4, sga[s]), (128, 160, 5, sgb[s])):
            pb = bank(C, bnk)[0:r1 - r0, :]
            for kc in range(8):
                P.op("tensor", lambda e, pb=pb, kc=kc, r0=r0, r1=r1, xc=xc: e.matmul(pb, lhsT=g1a[:, kc, r0:r1], rhs=xc(kc), start=(kc == 0), stop=False), writes=[Rpb[bnk]])
            for kc in range(8):
                P.op("tensor", lambda e, pb=pb, kc=kc, r0=r0, r1=r1, xp=xp: e.matmul(pb, lhsT=g1b[:, kc, r0:r1], rhs=xp(kc), start=False, stop=(kc == 7)), writes=[Rpb[bnk]])
            P.op("scalar", lambda e, pb=pb, dst=dst: e.activation(out=dst, in_=pb, func=AF.Sigmoid), reads=[Rpb[bnk]], writes=[Rsg[s]])
        for tb in range(4):
            blk = tp * 4 + tb
            s2 = blk % 2
            bc = lambda kc, blk=blk: xe[:, kc, 1 + blk * 128:1 + (blk + 1) * 128]
            bp = lambda kc, blk=blk: xe[:, kc, blk * 128:(blk + 1) * 128]
            for j in range(0, D, 512):
                b = k % 4
                k += 1
                pb = bank(C, b)
                for kc in range(8):
                    P.op("tensor", lambda e, pb=pb, kc=kc, j=j, bc=bc: e.matmul(pb, lhsT=bc(kc), rhs=wva[:, kc, j:j + 512], start=(kc == 0), stop=False), writes=[Rpb[b]])
                for kc in range(8):
                    P.op("tensor", lambda e, pb=pb, kc=kc, j=j, bp=bp: e.matmul(pb, lhsT=bp(kc), rhs=wvb[:, kc, j:j + 512], start=False, stop=(kc == 7)), writes=[Rpb[b]])
                P.op("scalar", lambda e, pb=pb, s2=s2, j=j: e.copy(out=ot[s2][:, j:j + 512], in_=pb), reads=[Rpb[b]], writes=[Rot[s2]])
            P.op("gpsimd", lambda e, s2=s2, blk=blk: e.dma_start(out=SC["v"][blk * 128:(blk + 1) * 128, :], in_=ot[s2]), reads=[Rot[s2]], writes=[Rod[s2]], dma=True)
            for j in range(0, D, 512):
                b = 6 + (j // 512)
                pb = bank(C, b)
                P.op("tensor", lambda e, pb=pb, j=j, s=s, tb=tb: e.matmul(pb, lhsT=sga[s][:, tb * 128:(tb + 1) * 128], rhs=g2a[:, j:j + 512], start=True, stop=False),
                     reads=[Rsg[s]], writes=[Rpb[b]])
                P.op("tensor", lambda e, pb=pb, j=j, s=s, tb=tb: e.matmul(pb, lhsT=sgb[s][:, tb * 128:(tb + 1) * 128], rhs=g2b[:, j:j + 512], start=False, stop=True),
                     reads=[Rsg[s]], writes=[Rpb[b]])
                P.op("vector", lambda e, pb=pb, s2=s2, j=j: e.tensor_copy(out=gt[s2][:, j:j + 512], in_=pb), reads=[Rpb[b]], writes=[Rgt[s2]])
            P.op("gpsimd", lambda e, s2=s2, blk=blk: e.dma_start(out=SC["g"][blk * 128:(blk + 1) * 128, :], in_=gt[s2]), reads=[Rgt[s2]], writes=[Rgd[s2]], dma=True)
    P.barrier()
    A.reset(m0)


def rw_r1b(C, xe, W, SC, pc, om):
    P, A = C.P, C.A
    P.barrier()
    m0 = A.off
    m01 = A.alloc([128, 512], F32)
    blk1 = A.alloc([128, 128], F32)
    ind2 = A.alloc([128, 2], BF16)
    ind2f = A.alloc([128, 2], F32)
    rk_all = A.alloc([128, NB, NH], F32)
    Rrk = Res()
    P.op("vector", lambda e: e.memset(m01, 1.0), writes=[Res()])
    P.op("vector", lambda e: e.memset(m01.rearrange("p (a b) -> p a b", b=128)[:, :, 0:1], 0.0), writes=[Res()])
    P.op("sync", lambda e: e.dma_start(out=blk1, in_=W["blk1"]), writes=[Res()], dma=True)
    P.op("sync", lambda e: e.dma_start(out=ind2f, in_=W["ind2"]), writes=[Res()], dma=True)
    P.barrier()
    P.op("vector", lambda e: e.tensor_copy(out=ind2, in_=ind2f), writes=[Res()])
    w1a = A.alloc([128, 8, 64], BF16); w1b = A.alloc([128, 8, 64], BF16)
    a1a = A.alloc([128, 8, 64], BF16); a1b = A.alloc([128, 8, 64], BF16)
    w2 = A.alloc([64, D], BF16); a2 = A.alloc([64, D], BF16)
    wra = A.alloc([128, 8, 512], BF16); wrb = A.alloc([128, 8, 512], BF16)
    wka = A.alloc([128, 8, 512], BF16); wkb = A.alloc([128, 8, 512], BF16)
    mw = A.off
    stage = A.alloc([128, 8, 512], F32)
    Rstage = Res()
    load_w_mu(C, W["w1"], 0, 64, w1a, w1b, pc[:, 24:32], om[:, 24:32], stage[:, :, 0:64], Rstage)
    load_w_mu(C, W["a1"], 0, 64, a1a, a1b, pc[:, 32:40], om[:, 32:40], stage[:, :, 0:64], Rstage)
    st2 = A.alloc([64, D], F32)
    Rst2 = Res()
    for src, dst in ((W["w2"], w2), (W["a2"], a2)):
        P.op("sync", lambda e, src=src: e.dma_start(out=st2, in_=src), reads=[Rst2], writes=[Rst2], dma=True)
        P.op("vector", lambda e, dst=dst: e.tensor_copy(out=dst, in_=st2), reads=[Rst2], writes=[Rst2])
    for half in range(2):
        P.barrier()
        A.reset(mw)
        stage = A.alloc([128, 8, 512], F32)
        Rstage = Res()
        load_w_mu(C, W["w_rkv"][0], half * 512, 512, wra, wrb, pc[:, 0:8], om[:, 0:8], stage, Rstage)
        load_w_mu(C, W["w_rkv"][1], half * 512, 512, wka, wkb, pc[:, 8:16], om[:, 8:16], stage, Rstage)
        P.barrier()
        A.reset(mw)
        thT = A.alloc([64, 512], BF16); laT = A.alloc([64, 512], BF16)
        Rth, Rla = Res(), Res()
        NF = 11
        f2 = [[A.alloc([128, 512], F32) for _ in range(NF)] for _ in range(2)]
        Rf2 = [[Res() for _ in range(NF)] for _ in range(2)]
        ob = [[A.alloc([128, 512], BF16) for _ in range(7)] for _ in range(2)]
        Rob = [[Res() for _ in range(7)] for _ in range(2)]
        Rod = [[Res() for _ in range(7)] for _ in range(2)]
        gc4 = [A.alloc([128, 4], F32) for _ in range(2)]
        Rgc4 = [Res() for _ in range(2)]
        Rgcd = [Res() for _ in range(2)]
        tk = [[A.alloc([128, 4, 128], BF16) for _ in range(3)] for _ in range(2)]
        Rtk = [[Res() for _ in range(3)] for _ in range(2)]
        Rtkd = [[Res() for _ in range(3)] for _ in range(2)]
        Rpb = [Res() for _ in range(8)]
        it = 0
        pending_tail = [None]
        for tp in range(T // 512):
            c0 = tp * 512
            xc = lambda kc, c0=c0: xe[:, kc, 1 + c0:1 + c0 + 512]
            xp = lambda kc, c0=c0: xe[:, kc, c0:c0 + 512]
            cols = slice(c0, c0 + 512)
            for (wa, wb, dst, Rd, fn) in ((w1a, w1b, thT, Rth, AF.Tanh), (a1a, a1b, laT, Rla, AF.Copy)):
                pb = bank(C, 6)[0:64, :]
                for kc in range(8):
                    P.op("tensor", lambda e, pb=pb, kc=kc, wa=wa, xc=xc: e.matmul(pb, lhsT=wa[:, kc, :], rhs=xc(kc), start=(kc == 0), stop=False), writes=[Rpb[6]])
                for kc in range(8):
                    P.op("tensor", lambda e, pb=pb, kc=kc, wb=wb, xp=xp: e.matmul(pb, lhsT=wb[:, kc, :], rhs=xp(kc), start=False, stop=(kc == 7)), writes=[Rpb[6]])
                P.op("scalar", lambda e, pb=pb, dst=dst, fn=fn: e.activation(out=dst, in_=pb, func=fn), reads=[Rpb[6]], writes=[Rd])
            for pl in range(4):
                p = half * 4 + pl
                s = it % 2
                it += 1
                pcs = slice(pl * 128, (pl + 1) * 128)
                gcs = slice(p * 128, (p + 1) * 128)
                br, bk = s, 2 + s
                r_ps, k_ps, w_ps, a_ps, ss_ps = bank(C, br), bank(C, bk), bank(C, 4), bank(C, 5), bank(C, 6)
                for (ps_, wa, wb, bb) in ((r_ps, wra, wrb, br), (k_ps, wka, wkb, bk)):
                    for kc in range(8):
                        P.op("tensor", lambda e, ps_=ps_, kc=kc, wa=wa, xc=xc, pcs=pcs: e.matmul(ps_, lhsT=wa[:, kc, pcs], rhs=xc(kc), start=(kc == 0), stop=False), writes=[Rpb[bb]])
                    for kc in range(8):
                        P.op("tensor", lambda e, ps_=ps_, kc=kc, wb=wb, xp=xp, pcs=pcs: e.matmul(ps_, lhsT=wb[:, kc, pcs], rhs=xp(kc), start=False, stop=(kc == 7)), writes=[Rpb[bb]])
                P.op("tensor", lambda e, w_ps=w_ps, gcs=gcs: e.matmul(w_ps, lhsT=w2[:, gcs], rhs=thT, start=True, stop=True), reads=[Rth], writes=[Rpb[4]])
                P.op("tensor", lambda e, a_ps=a_ps, gcs=gcs: e.matmul(a_ps, lhsT=a2[:, gcs], rhs=laT, start=True, stop=True), reads=[Rla], writes=[Rpb[5]])
                if pending_tail[0] is not None:
                    pending_tail[0]()
                    pending_tail[0] = None
                sgw, cs, gam, ginv, gprev, gcg, av, kk, kk2, k2, bv = f2[s]
                Rsgw, Rcs, Rgam, Rginv, Rgprev, Rgcg, Rav, Rkk, Rkk2, Rk2, Rbv = Rf2[s]
                A1T, BtT, KtT, R1T, BdF, KdF, prod = ob[s]
                RA1T, RBtT, RKtT, RR1T, RBdF, RKdF, Rprod = Rob[s]
                col = lambda base, p=p: pc[:, base + p:base + p + 1]
                v3 = lambda ap: ap.rearrange("p (a b) -> p a b", b=128)
                ACT(P, sgw, w_ps, AF.Sigmoid, [Rpb[4]], [Rsgw], bias=col(48), scale=1.0)
                ACT(P, av, a_ps, AF.Sigmoid, [Rpb[5]], [Rav], bias=col(56), scale=1.0)
                SCAN(P, cs, m01, sgw, 0.0, [Rsgw], [Rcs])
                TS(P, "vector", kk, k_ps, col(64), None, ALU.mult, None, [Rpb[bk]], [Rkk])
                ACT(P, kk2, kk, AF.Square, [Rkk], [Rkk2])
                MM(P, ss_ps, blk1, kk2, True, True, [Rkk2], [Rpb[6]])
                TT(P, "gpsimd", gprev, cs, sgw, ALU.subtract, [Rcs, Rsgw], [Rgprev])
                TT(P, "vector", v3(gcg), v3(cs)[:, :, 127:128].to_broadcast([128, 4, 128]), v3(cs), ALU.subtract, [Rcs], [Rgcg])
                ACT(P, gam, cs, AF.Exp, [Rcs], [Rgam], scale=-C0)
                ACT(P, ginv, cs, AF.Exp, [Rcs], [Rginv], scale=C0)
                ACT(P, gprev, gprev, AF.Exp, [Rgprev], [Rgprev], scale=-C0)
                ACT(P, gcg, gcg, AF.Exp, [Rgcg], [Rgcg], scale=-C0)
                ACT(P, gc4[s], v3(cs)[:, :, 127], AF.Exp, [Rcs], [Rgc4[s]], scale=-C0)
                DMA(P, "sync", SC["gC"][gcs, tp * 4:(tp + 1) * 4], gc4[s], [Rgc4[s]], [Rgcd[s]])
                TS(P, "vector", kk2, ss_ps, 1e-24, None, ALU.max, None, [Rpb[6]], [Rkk2])
                ACT(P, kk2, kk2, AF.Ln, [Rkk2], [Rkk2])
                ACT(P, kk2, kk2, AF.Exp, [Rkk2], [Rkk2], scale=-0.5)
                TT(P, "vector", kk, kk, kk2, ALU.mult, [Rkk, Rkk2], [Rkk])
                TS(P, "vector", k2, av, -1.0, col(72), ALU.add, ALU.mult, [Rav], [Rk2])
                STT(P, k2, k2, 1.0, k_ps, ALU.add, ALU.mult, [Rk2, Rpb[bk]], [Rk2])
                TT(P, "gpsimd", bv, kk, av, ALU.mult, [Rkk, Rav], [Rbv])
                STT(P, A1T, kk, -1.0, gprev, ALU.mult, ALU.mult, [Rkk, Rgprev], [RA1T])
                TT(P, "gpsimd", BtT, bv, ginv, ALU.mult, [Rbv, Rginv], [RBtT])
                TT(P, "gpsimd", KtT, k2, ginv, ALU.mult, [Rk2, Rginv], [RKtT])
                TT(P, "vector", R1T, r_ps, gam, ALU.mult, [Rpb[br], Rgam], [RR1T])
                TT(P, "gpsimd", BdF, bv, gcg, ALU.mult, [Rbv, Rgcg], [RBdF])
                TT(P, "gpsimd", KdF, k2, gcg, ALU.mult, [Rk2, Rgcg], [RKdF])
                STT(P, prod, k2, col(80), r_ps, ALU.mult, ALU.mult, [Rk2, Rpb[br]], [Rprod])
                for i, nm in enumerate(("A1T", "BtT", "KtT", "R1T")):
                    DMA(P, "sync", SC[nm][gcs, cols], ob[s][i], [Rob[s][i]], [Rod[s][i]])
                def tail(s=s, gcs=gcs, tp=tp, p=p):
                    pt = bank(C, 7, BF16)
                    for i, src_i in enumerate((0, 4, 5)):
                        for tb in range(4):
                            TR(P, pt[:, tb * 128:(tb + 1) * 128], ob[s][src_i][:, tb * 128:(tb + 1) * 128], C.ident_b, [Rob[s][src_i]], [Rpb[7]])
                        CP(P, "vector", tk[s][i], pt[:, 0:512].rearrange("p (a b) -> p a b", b=128), [Rpb[7]], [Rtk[s][i]])
                        nm = ("A1", "Bd", "Kd")[i]
                        DMA(P, "gpsimd", SC[nm][tp * 512:(tp + 1) * 512, gcs].rearrange("(a t) c -> t a c", t=128), tk[s][i], [Rtk[s][i]], [Rtkd[s][i]])
                    prk = bank(C, 7)[:, 256:264]
                    for tb in range(4):
                        MM(P, prk[:, tb * 2:tb * 2 + 2], ob[s][6][:, tb * 128:(tb + 1) * 128], ind2, True, True, [Rob[s][6]], [Rpb[7]])
                    CP(P, "vector", rk_all[:, tp * 4:(tp + 1) * 4, 2 * p:2 * p + 2], prk.rearrange("p (a b) -> p a b", b=2), [Rpb[7]], [Rrk])
                pending_tail[0] = tail
        if pending_tail[0] is not None:
            pending_tail[0]()
            pending_tail[0] = None
    P.op("sync", lambda e: e.dma_start(out=SC["rk"].rearrange("(a t) h -> t a h", t=128), in_=rk_all), reads=[Rrk], writes=[Res()], dma=True)
    P.barrier()
    A.reset(m0)


def rw_r2(C, W, SC, og_dram, nchunks=NB):
    P, A = C.P, C.A
    P.barrier()
    A.reset(C.base)
    mk = A.alloc([128, 8, 128], F32)
    mkb = A.alloc([128, 3, 128], BF16)
    gCt = A.alloc([128, 8, 32], F32)
    lnw = A.alloc([128, D], F32)
    lnb = A.alloc([128, D], F32)
    DMA(P, "sync", mk, W["masks"].rearrange("m a b -> a m b"), [], [Res()])
    DMA(P, "sync", gCt, SC["gC"].rearrange("(p q) c -> q p c", q=128), [], [Res()])
    DMA(P, "sync", lnw, bcast_row(W["ln_w"], D), [], [Res()])
    DMA(P, "sync", lnb, bcast_row(W["ln_b"], D), [], [Res()])
    P.barrier()
    CP(P, "vector", mkb, mk[:, 5:8, :], [], [Res()])
    P.barrier()
    mb = lambda i: mk[:, i, :].unsqueeze(1).to_broadcast([128, 4, 128])
    mbb = lambda i: mkb[:, i, :].unsqueeze(1).to_broadcast([128, 4, 128])
    identb4 = C.ident_f.unsqueeze(1).to_broadcast([128, 4, 128])
    FTn = ("A1T", "BtT", "KtT", "R1T")
    TKn = ("A1", "Bd", "Kd", "v")
    FT = [[A.alloc([128, 8, 128], BF16) for _ in range(4)] for _ in range(2)]
    RFT = [[Res() for _ in range(4)] for _ in range(2)]
    TK = [[A.alloc([128, D], BF16) for _ in range(4)] for _ in range(2)]
    RTK = [[Res() for _ in range(4)] for _ in range(2)]
    gtk = [A.alloc([128, D], BF16) for _ in range(2)]
    Rgtk = [Res() for _ in range(2)]
    rkt = [A.alloc([128, NH], F32) for _ in range(2)]
    Rrkt = [Res() for _ in range(2)]
    NSLOT = 4
    GF = []
    GB = ["Qd", "Nd", "Xa", "XaT", "Xb", "XbT", "Qb", "Nb", "Qo0", "No0", "Qo1", "No1", "Qo2", "No2", "Zb", "Tmb", "Y1", "Y2", "MTb"]
    G = [dict([(nm, A.alloc([128, 4, 128], F32)) for nm in GF] + [(nm, A.alloc([128, 4, 128], BF16)) for nm in GB]) for _ in range(NSLOT)]
    RG = [{nm: Res(nm) for nm in GF + GB} for _ in range(NSLOT)]
    MVb = [A.alloc([128, 4, 64], BF16) for _ in range(NSLOT)]
    RMVb = [Res() for _ in range(NSLOT)]
    PbT = [A.alloc([128, NH, 128], BF16) for _ in range(2)]
    PkT = [A.alloc([128, NH, 128], BF16) for _ in range(2)]
    W2 = [A.alloc([128, NH, 64], F32) for _ in range(2)]
    AhT = [A.alloc([128, 8, 128], BF16) for _ in range(2)]
    RPbT = [[Res() for _ in range(4)] for _ in range(2)]
    RPkT = [[Res() for _ in range(4)] for _ in range(2)]
    RW2 = [[Res() for _ in range(4)] for _ in range(2)]
    RAhT = [[Res() for _ in range(4)] for _ in range(2)]
    Ub = A.alloc([128, NH, 64], BF16); RUb = Res()
    H = A.alloc([128, 8, 64], F32); RH = Res()
    Hb = A.alloc([128, 8, 64], BF16); RHb = Res()
    ysb = A.alloc([128, D], F32); Rysb = Res()
    ysq = A.alloc([128, D], F32); Rysq = Res()
    st = A.alloc([128, 6, NH], F32); Rst = Res()
    ogt = [A.alloc([128, D], BF16) for _ in range(2)]
    Rogt = [Res() for _ in range(2)]
    Rogd = [Res() for _ in range(2)]
    Rpb = [Res("pb") for _ in range(8)]
    bctr = [0]

    def nb():
        b = bctr[0] % 8
        bctr[0] += 1
        return b

    MEMSET(P, "vector", H, 0.0, [RH])
    MEMSET(P, "vector", Hb, 0.0, [RHb])
    v4 = lambda ap: ap.rearrange("p (a b) -> p a b", b=128)
    h3 = lambda ap: ap.rearrange("p (a b) -> p a b", b=64)
    pv = lambda ap, hh: ap.rearrange("p (a two) w -> p a two w", two=2)[:, :, hh, :]
    hgrp = lambda hd: 2 * ((hd // 2) // 4) + hd % 2

    def loads(n):
        s = n % 2
        for i, nm in enumerate(FTn):
            DMA(P, "sync", FT[s][i], SC[nm][:, n * 128:(n + 1) * 128].rearrange("(p q) t -> q p t", q=128), [], [RFT[s][i]])
        for i, nm in enumerate(TKn):
            DMA(P, "sync", TK[s][i], SC[nm][n * 128:(n + 1) * 128, :], [], [RTK[s][i]])
        DMA(P, "sync", gtk[s], SC["g"][n * 128:(n + 1) * 128, :], [], [Rgtk[s]])
        DMA(P, "sync", rkt[s], SC["rk"][n * 128:(n + 1) * 128, :], [], [Rrkt[s]])

    def stageA(n, g4, sl):
        s = n % 2
        g, R = G[sl], RG[sl]
        A1T, BtT, KtT, R1T = FT[s]
        A1k, Bdk, Kdk, Vk = TK[s]
        half8, hh = g4 // 2, g4 % 2
        p16 = lambda ap: ap.rearrange("p (a two) w -> p a two w", two=2)[:, 4 * half8:4 * half8 + 4, hh, :]
        Rj = slice(64 * hh, 64 * hh + 64)

        def mm4(lhs, rhs):
            b = nb()
            pb = bank(C, b)
            for hl in range(4):
                MM(P, pb[:, hl * 128:(hl + 1) * 128], g[lhs][:, hl, :], g[rhs][:, hl, :], True, True, [R[lhs], R[rhs]], [Rpb[b]])
            return b, v4(pb)

        bs = [nb() for _ in range(5)]
        pbs = [bank(C, b) for b in bs]
        specs = ((BtT, A1T, 1, 0), (A1T, BtT, 0, 1), (KtT, A1T, 2, 0), (BtT, R1T, 1, 3), (KtT, R1T, 2, 3))
        for mi, (la, ra, li, ri) in enumerate(specs):
            for hl in range(4):
                p = 4 * half8 + hl
                MM(P, pbs[mi][:, hl * 128:(hl + 1) * 128], la[Rj, p, :], ra[Rj, p, :], True, True, [RFT[s][li], RFT[s][ri]], [Rpb[bs[mi]]])
        TT(P, "vector", g["Qd"], v4(pbs[0]), mb(3), ALU.mult, [Rpb[bs[0]]], [R["Qd"]])
        TT(P, "vector", g["Nd"], v4(pbs[1]), mb(4), ALU.mult, [Rpb[bs[1]]], [R["Nd"]])
        TT(P, "vector", g["Qb"], v4(pbs[0]), mb(0), ALU.mult, [Rpb[bs[0]]], [R["Qb"]])
        TT(P, "vector", g["Nb"], v4(pbs[1]), mb(2), ALU.mult, [Rpb[bs[1]]], [R["Nb"]])
        TT(P, "vector", g["MTb"], v4(pbs[2]), mb(0), ALU.mult, [Rpb[bs[2]]], [R["MTb"]])
        TT(P, "vector", p16(PbT[s]), v4(pbs[3]), mb(1), ALU.mult, [Rpb[bs[3]]], [RPbT[s][g4]])
        TT(P, "vector", p16(PkT[s]), v4(pbs[4]), mb(1), ALU.mult, [Rpb[bs[4]]], [RPkT[s][g4]])
        yield
        xs_, xts_ = "Qd", "Nd"
        for k in range(3):
            xn_, xtn_ = ("Xa", "XaT") if k % 2 == 0 else ("Xb", "XbT")
            b1, p1 = mm4(xts_, xs_)
            b2, p2 = mm4(xs_, xts_)
            CP(P, "scalar", g[xn_], p1, [Rpb[b1]], [R[xn_]])
            CP(P, "scalar", g[xtn_], p2, [Rpb[b2]], [R[xtn_]])
            TT(P, "gpsimd", g[f"Qo{k}"], g["Qb"], mbb(k), ALU.mult, [R["Qb"]], [R[f"Qo{k}"]])
            TT(P, "gpsimd", g[f"No{k}"], g["Nb"], mbb(k), ALU.mult, [R["Nb"]], [R[f"No{k}"]])
            yield
            if k == 0:
                b3 = nb(); pb3 = bank(C, b3)
                b4 = nb(); pb4 = bank(C, b4)
                for hl in range(4):
                    o3 = pb3[:, hl * 128:(hl + 1) * 128]
                    MM(P, o3, g[xtn_][:, hl, :], g["Qd"][:, hl, :], True, False, [R[xtn_], R["Qd"]], [Rpb[b3]])
                    MM(P, o3, C.ident_b, g[xn_][:, hl, :], False, False, [R[xn_]], [Rpb[b3]])
                    MM(P, o3, C.ident_b, g["Qd"][:, hl, :], False, True, [R["Qd"]], [Rpb[b3]])
                for hl in range(4):
                    o4 = pb4[:, hl * 128:(hl + 1) * 128]
                    MM(P, o4, g[xn_][:, hl, :], g["Nd"][:, hl, :], True, False, [R[xn_], R["Nd"]], [Rpb[b4]])
                    MM(P, o4, C.ident_b, g[xtn_][:, hl, :], False, False, [R[xtn_]], [Rpb[b4]])
                    MM(P, o4, C.ident_b, g["Nd"][:, hl, :], False, True, [R["Nd"]], [Rpb[b4]])
                TT(P, "vector", g["Zb"], v4(pb3), identb4, ALU.add, [Rpb[b3]], [R["Zb"]])
                TT(P, "vector", g["Tmb"], v4(pb4), identb4, ALU.add, [Rpb[b4]], [R["Tmb"]])
            else:
                b3, p3 = mm4(xtn_, "Zb")
                b4, p4 = mm4(xn_, "Tmb")
                TT(P, "vector", g["Zb"], p3, g["Zb"], ALU.add, [Rpb[b3], R["Zb"]], [R["Zb"]])
                TT(P, "vector", g["Tmb"], p4, g["Tmb"], ALU.add, [Rpb[b4], R["Tmb"]], [R["Tmb"]])
            yield
            xs_, xts_ = xn_, xtn_
        for lv in range(3):
            last = (lv == 2)
            b1, p1 = mm4(f"No{lv}", "Zb")
            CP(P, "scalar", g["Y1"], p1, [Rpb[b1]], [R["Y1"]])
            if not last:
                b2, p2 = mm4(f"Qo{lv}", "Tmb")
                CP(P, "scalar", g["Y2"], p2, [Rpb[b2]], [R["Y2"]])
            yield
            b3, p3 = mm4("Tmb", "Y1")
            if not last:
                b4, p4 = mm4("Zb", "Y2")
            TT(P, "vector", g["Zb"], p3, g["Zb"], ALU.add, [Rpb[b3], R["Zb"]], [R["Zb"]])
            if not last:
                TT(P, "vector", g["Tmb"], p4, g["Tmb"], ALU.add, [Rpb[b4], R["Tmb"]], [R["Tmb"]])
            yield
        b = nb(); pb = bank(C, b)
        for hl in range(4):
            hd = 2 * (4 * half8 + hl) + hh
            MM(P, pb[:, hl * 64:(hl + 1) * 64], g["MTb"][:, hl, :], Vk[:, hd * 64:(hd + 1) * 64], True, True, [R["MTb"], RTK[s][3]], [Rpb[b]])
        CP(P, "scalar", MVb[sl], pb[:, 0:256].rearrange("p (a b) -> p a b", b=64), [Rpb[b]], [RMVb[sl]])
        b2_ = nb(); pb2 = bank(C, b2_)
        for hl in range(4):
            hd = 2 * (4 * half8 + hl) + hh
            MM(P, pb2[64 * hh:64 * hh + 64, hl * 128:(hl + 1) * 128], A1k[:, hd * 64:(hd + 1) * 64], g["Zb"][:, hl, :], True, True, [RTK[s][0], R["Zb"]], [Rpb[b2_]])
        CP(P, "vector", AhT[s][64 * hh:64 * hh + 64, 4 * half8:4 * half8 + 4, :], pb2[64 * hh:64 * hh + 64, :].rearrange("p (a b) -> p a b", b=128), [Rpb[b2_]], [RAhT[s][g4]])
        yield
        b = nb(); pb = bank(C, b)
        for hl in range(4):
            MM(P, pb[:, hl * 64:(hl + 1) * 64], g["Zb"][:, hl, :], MVb[sl][:, hl, :], True, True, [R["Zb"], RMVb[sl]], [Rpb[b]])
        CP(P, "scalar", p16(W2[s]), pb[:, 0:256].rearrange("p (a b) -> p a b", b=64), [Rpb[b]], [RW2[s][g4]])
        yield

    def stageB(n):
        s = n % 2
        A1T, BtT, KtT, R1T = FT[s]
        A1k, Bdk, Kdk, Vk = TK[s]
        bu = [nb(), nb()]
        for hd in range(NH):
            p, hh = hd // 2, hd % 2
            Rj = slice(64 * hh, 64 * hh + 64)
            b = bu[hh]
            MM(P, bank(C, b)[:, p * 64:(p + 1) * 64], AhT[s][Rj, p, :], Hb[Rj, p, :], True, True, [RAhT[s][hgrp(hd)], RHb], [Rpb[b]])
        for hh in range(2):
            TT(P, "vector", pv(Ub, hh), bank(C, bu[hh]).rearrange("p (a b) -> p a b", b=64), pv(W2[s], hh), ALU.add,
               [Rpb[bu[hh]]] + RW2[s], [RUb])
        yield
        by = [nb(), nb()]
        for hd in range(NH):
            p, hh = hd // 2, hd % 2
            Rj = slice(64 * hh, 64 * hh + 64)
            b = by[hh]
            o = bank(C, b)[:, p * 64:(p + 1) * 64]
            MM(P, o, R1T[Rj, p, :], Hb[Rj, p, :], True, False, [RFT[s][3], RHb], [Rpb[b]])
            MM(P, o, PbT[s][:, hd, :], Ub[:, hd, :], False, False, [RPbT[s][hgrp(hd)], RUb], [Rpb[b]])
            MM(P, o, PkT[s][:, hd, :], Vk[:, hd * 64:(hd + 1) * 64], False, True, [RPkT[s][hgrp(hd)], RTK[s][3]], [Rpb[b]])
        bh = nb()
        for hd in range(NH):
            p, hh = hd // 2, hd % 2
            o = bank(C, bh)[64 * hh:64 * hh + 64, p * 64:(p + 1) * 64]
            MM(P, o, Bdk[:, hd * 64:(hd + 1) * 64], Ub[:, hd, :], True, False, [RTK[s][1], RUb], [Rpb[bh]])
            MM(P, o, Kdk[:, hd * 64:(hd + 1) * 64], Vk[:, hd * 64:(hd + 1) * 64], False, True, [RTK[s][2], RTK[s][3]], [Rpb[bh]])
        TT(P, "vector", H, H, gCt[:, :, n:n + 1].to_broadcast([128, 8, 64]), ALU.mult, [RH], [RH])
        TT(P, "vector", H, bank(C, bh).rearrange("p (a b) -> p a b", b=64), H, ALU.add, [Rpb[bh], RH], [RH])
        CP(P, "scalar", Hb, H, [RH], [RHb])
        for hh in range(2):
            CP(P, "scalar", pv(h3(ysb), hh), bank(C, by[hh]).rearrange("p (a b) -> p a b", b=64), [Rpb[by[hh]]], [Rysb])
        yield
        ACT(P, ysq, ysb, AF.Square, [Rysb], [Rysq])
        yield
        s1, s2_, mean, var, rstd, m2 = (st[:, i, :] for i in range(6))
        P.op("vector", lambda e, s1=s1: e.tensor_reduce(out=s1, in_=h3(ysb), axis=AX.X, op=ALU.add), reads=[Rysb], writes=[Rst])
        P.op("vector", lambda e, s2_=s2_: e.tensor_reduce(out=s2_, in_=h3(ysq), axis=AX.X, op=ALU.add), reads=[Rysq], writes=[Rst])
        TS(P, "vector", mean, s1, 1.0 / 64, None, ALU.mult, None, [Rst], [Rst])
        TT(P, "vector", m2, mean, mean, ALU.mult, [Rst], [Rst])
        STT(P, var, s2_, 1.0 / 64, m2, ALU.mult, ALU.subtract, [Rst], [Rst])
        TS(P, "vector", var, var, GN_EPS, None, ALU.add, None, [Rst], [Rst])
        ACT(P, var, var, AF.Sqrt, [Rst], [Rst])
        P.op("vector", lambda e, rstd=rstd, var=var: e.reciprocal(out=rstd, in_=var), reads=[Rst], writes=[Rst])
        yield
        bc16 = lambda ap: ap.unsqueeze(2).to_broadcast([128, NH, 64])
        TT(P, "gpsimd", h3(ysb), h3(ysb), bc16(mean), ALU.subtract, [Rysb, Rst], [Rysb])
        TT(P, "gpsimd", h3(ysb), h3(ysb), bc16(rstd), ALU.mult, [Rysb, Rst], [Rysb])
        TT(P, "gpsimd", h3(ysq), h3(Vk), bc16(rkt[s]), ALU.mult, [RTK[s][3], Rrkt[s], Rysq], [Rysq])
        yield
        TT(P, "gpsimd", ysb, ysb, lnw, ALU.mult, [Rysb], [Rysb])
        TT(P, "vector", ysb, ysb, lnb, ALU.add, [Rysb], [Rysb])
        TT(P, "vector", ysb, ysb, ysq, ALU.add, [Rysb, Rysq], [Rysb])
        yield
        TT(P, "gpsimd", ogt[s], ysb, gtk[s], ALU.mult, [Rysb, Rgtk[s]], [Rogt[s]])
        DMA(P, "sync", og_dram[n * 128:(n + 1) * 128, :], ogt[s], [Rogt[s]], [Rogd[s]])
        yield

    def lockstep(gens):
        gens = list(gens)
        while gens:
            alive = []
            for gn in gens:
                try:
                    next(gn)
                    alive.append(gn)
                except StopIteration:
                    pass
            gens = alive

    loads(0)
    lockstep([stageA(0, g4, g4) for g4 in range(4)])
    for n in range(nchunks):
        if n + 1 < nchunks:
            loads(n + 1)
            lockstep([stageB(n)] + [stageA(n + 1, g4, g4) for g4 in range(4)])
        else:
            lockstep([stageB(n)])


def rwkv_mixer(C, h_dram, Rh, g_dram, W, SC, upto="all", prefetch=None):
    P, A = C.P, C.A
    xe = xnT_ext_phase(C, h_dram, Rh, g_dram)
    P.barrier()
    A.reset(C.base2)
    pc, om = rw_params(C, W)
    rw_r1a(C, xe, W, SC, pc, om)
    if upto == "r1a":
        return
    rw_r1b(C, xe, W, SC, pc, om)
    if upto == "r1b":
        return
    rw_r2(C, W, SC, SC["og"], nchunks=(int(upto[2:]) if upto.startswith("r2") and len(upto) > 2 else NB))
    if upto.startswith("r2"):
        return
    outproj_phase(C, SC["og"], D, W["w_out"], h_dram, Rh, prefetch=prefetch)


def _consts():
    i = np.arange(128)
    c = {}
    c["c_ident"] = np.eye(128, dtype=np.float32)
    c["c_maskT"] = np.triu(np.ones((128, 128), np.float32))
    lg = np.log(1.0 - 2.0 ** (-5.0 - np.arange(4, dtype=np.float64)))
    cc = np.arange(128, dtype=np.float64)
    tab = np.stack([np.stack([np.exp((cc + 1) * lg[h]), np.exp(-(cc + 1) * lg[h]) * 256 ** -0.5,
                              np.exp((127 - cc) * lg[h]) * 256 ** -0.5]) for h in range(4)])
    c["c_rettab"] = tab.reshape(4, 384).astype(np.float32)
    inv = (10000.0 ** (-np.linspace(0.0, 1.0, 128, dtype=np.float32))).astype(np.float32)
    c["c_invt"] = (inv.astype(np.float64) / (2 * np.pi)).astype(np.float32).reshape(128, 1)
    c["c_iota"] = np.tile(np.arange(513, dtype=np.float32)[None], (128, 1))
    c["c_msk2"] = np.stack([((i // 16) % 2 == 0), ((i // 16) % 2 == 1)], 1).astype(np.float32)
    c["c_blk1"] = (i[:, None] // 64 == i[None, :] // 64).astype(np.float32)
    c["c_ind2"] = np.stack([(i < 64), (i >= 64)], 1).astype(np.float32)
    a, b = i[:, None], i[None, :]
    mUs = (a < b); mUi = (a <= b); mLs = (a > b)
    D16 = (a // 16 == b // 16)
    O16 = (a // 32 == b // 32) & (a // 16 != b // 16)
    O32 = (a // 64 == b // 64) & (a // 32 != b // 32)
    O64 = (a // 64 != b // 64)
    c["c_masks"] = np.stack([mUs, mUi, mLs, mUs & D16, mLs & D16, O16, O32, O64]).astype(np.float32)
    return c


_RET_GL = [float(np.exp(128 * np.log(1.0 - 2.0 ** (-5.0 - h)))) for h in range(4)]

_IN_SHAPES = {
    "x": ([T, D], F32), "p": ([4, T, 256], F32), "positions": ([T], I32), "norm_g": ([4, 4, D], F32), "final_g": ([D], F32),
    "ffn_w_gu": ([4, 2, D, 2 * FF], F32), "ffn_w_d": ([4, 2, FF, D], F32), "ple_w_proj": ([4, 256, D], F32), "ple_w_gate": ([4, D, D], F32),
    "gla_w_in": ([1, D, 3088], F32), "gla_w_gate_up": ([1, 16, 512], F32), "gla_b_gate": ([1, 512], F32), "gla_norm_g": ([1, 256], F32),
    "gla_w_out": ([1, D, D], F32), "ret_w_in": ([1, D, 6144], F32), "ret_w_out": ([1, 2048, D], F32),
    "s5_lam_re": ([1, 64, 64], F32), "s5_lam_im": ([1, 64, 64], F32), "s5_log_dt": ([1, 64], F32), "s5_b_re": ([1, 64, 64, 16], F32),
    "s5_b_im": ([1, 64, 64, 16], F32), "s5_c_re": ([1, 64, 16, 64], F32), "s5_c_im": ([1, 64, 16, 64], F32), "s5_d": ([1, D], F32),
    "s5_w_glu": ([1, D, D], F32), "s5_b_glu": ([1, D], F32),
    "rw_mu": ([1, 6, D], F32), "rw_w_rkv": ([1, 3, D, D], F32), "rw_w0": ([1, D], F32), "rw_w1": ([1, D, 64], F32), "rw_w2": ([1, 64, D], F32),
    "rw_a0": ([1, D], F32), "rw_a1": ([1, D, 64], F32), "rw_a2": ([1, 64, D], F32), "rw_g1": ([1, D, 160], F32), "rw_g2": ([1, 160, D], F32),
    "rw_k_k": ([1, D], F32), "rw_k_a": ([1, D], F32), "rw_r_k": ([1, 16, 64], F32), "rw_ln_w": ([1, D], F32), "rw_ln_b": ([1, D], F32),
    "rw_w_out": ([1, D, D], F32),
    "c_ident": ([128, 128], F32), "c_maskT": ([128, 128], F32), "c_rettab": ([4, 384], F32), "c_invt": ([128, 1], F32),
    "c_iota": ([128, 513], F32), "c_msk2": ([128, 2], F32), "c_blk1": ([128, 128], F32), "c_ind2": ([128, 2], F32), "c_masks": ([8, 128, 128], F32),
}


def build_program(n_layers=4, with_final=True):
    nc = bass.Bass("TRN2", target_bir_lowering=False)
    I = {k: nc.dram_tensor(k, sh, d, kind="ExternalInput").ap() for k, (sh, d) in _IN_SHAPES.items()}
    out = nc.dram_tensor("out", [T, D], F32, kind="ExternalOutput").ap()
    sc = lambda name, shape, d=BF16: nc.dram_tensor("sc_" + name, shape, d, kind="Internal").ap()
    h = sc("h", [T, D], F32)
    with ExitStack() as es:
        C = make_ctx(nc, es)
        setup_consts(C, I["c_ident"])
        Rh = [Res("h") for _ in range(NB)]
        Rout = [Res("o") for _ in range(NB)]
        pre_a = False
        for i in range(n_layers):
            kind = i % 4
            ffn_phase(C, I["x"] if i == 0 else h, h, Rh, I["ffn_w_gu"][i, 0], I["ffn_w_d"][i, 0], I["norm_g"][i, 0], preloaded=pre_a)
            pf_b = (lambda st, Rst, i=i: ffn_prefetch_pieces(C, I["ffn_w_gu"][i, 1], I["ffn_w_d"][i, 1], I["norm_g"][i, 2], st, Rst))
            pre_b = False
            g1 = I["norm_g"][i, 1]
            if kind == 0:
                W = dict(w_in=I["gla_w_in"][0], w_gate_up=I["gla_w_gate_up"][0], b_gate=I["gla_b_gate"][0], norm_g=I["gla_norm_g"][0],
                         w_out=I["gla_w_out"][0], maskT=I["c_maskT"])
                SC = {k: sc(f"gla_{k}", [T, 1024]) for k in ("v", "sg", "og")}
                gla_mixer(C, h, Rh, g1, W, SC)
            elif kind == 1:
                W = dict(w_in=I["ret_w_in"][0], w_out=I["ret_w_out"][0], positions=I["positions"], invt=I["c_invt"], tab=I["c_rettab"],
                         maskT=I["c_maskT"], gl=_RET_GL)
                SC = {k: sc(f"ret_{k}", [T, 2048]) for k in ("v", "sg", "og")}
                ret_mixer(C, h, Rh, g1, W, SC)
            elif kind == 2:
                W = dict(lam_re=I["s5_lam_re"][0], lam_im=I["s5_lam_im"][0], log_dt=I["s5_log_dt"][0], b_re=I["s5_b_re"][0], b_im=I["s5_b_im"][0],
                         c_re=I["s5_c_re"][0], c_im=I["s5_c_im"][0], d=I["s5_d"][0], w_glu=I["s5_w_glu"][0], b_glu=I["s5_b_glu"][0],
                         iota=I["c_iota"], msk2=I["c_msk2"])
                SC = {"y": sc("s5_y", [D, T], F32)}
                s5_mixer(C, h, Rh, g1, W, SC)
            else:
                W = dict(mu=I["rw_mu"][0], w_rkv=[I["rw_w_rkv"][0, j] for j in range(3)], w0=I["rw_w0"][0], w1=I["rw_w1"][0], w2=I["rw_w2"][0],
                         a0=I["rw_a0"][0], a1=I["rw_a1"][0], a2=I["rw_a2"][0], g1=I["rw_g1"][0], g2=I["rw_g2"][0], k_k=I["rw_k_k"][0],
                         k_a=I["rw_k_a"][0], r_k=I["rw_r_k"][0].rearrange("h j -> (h j)"), ln_w=I["rw_ln_w"][0], ln_b=I["rw_ln_b"][0],
                         w_out=I["rw_w_out"][0], blk1=I["c_blk1"], ind2=I["c_ind2"], masks=I["c_masks"])
                SC = {k: sc("rw_" + k, [D, T]) for k in ("A1T", "BtT", "KtT", "R1T")}
                SC.update({k: sc("rw_" + k, [T, D]) for k in ("A1", "Bd", "Kd", "v", "g", "og")})
                SC["gC"] = sc("rw_gC", [D, 32], F32)
                SC["rk"] = sc("rw_rk", [T, 16], F32)
                rwkv_mixer(C, h, Rh, g1, W, SC)
            ffn_phase(C, h, h, Rh, I["ffn_w_gu"][i, 1], I["ffn_w_d"][i, 1], I["norm_g"][i, 2], preloaded=pre_b)
            pre_a = False
            pf_a = (lambda st, Rst, i=i: ffn_prefetch_pieces(C, I["ffn_w_gu"][i + 1, 0], I["ffn_w_d"][i + 1, 0], I["norm_g"][i + 1, 0], st, Rst)) if pre_a else None
            fin = (I["final_g"], out, Rout) if (with_final and i == n_layers - 1) else None
            ple_phase(C, h, Rh, I["ple_w_gate"][i], I["ple_w_proj"][i], I["norm_g"][i, 3], I["p"][i], prefetch=pf_a, final=fin)
        if with_final:
            pass
        else:
            A = C.A
            C.P.barrier()
            A.reset(C.base)
            tl = [A.alloc([128, D], F32) for _ in range(2)]
            Rt = [Res() for _ in range(2)]
            for blk in range(NB):
                DMA(C.P, "sync", tl[blk % 2], h[blk * 128:(blk + 1) * 128, :], [Rh[blk]], [Rt[blk % 2]])
                DMA(C.P, "sync", out[blk * 128:(blk + 1) * 128, :], tl[blk % 2], [Rt[blk % 2]], [Rout[blk]])
        C.P.emit(final_waits=Rout)
    return nc


_NC_CACHE = {}


def kernel(**inputs):
    if "nc" not in _NC_CACHE:
        _NC_CACHE["nc"] = build_program()
    nc = _NC_CACHE["nc"]
    consts = _consts()
    B = inputs["x"].shape[0]
    in_maps = []
    for b in range(B):
        m = {}
        for k in _IN_SHAPES:
            if k.startswith("c_"):
                m[k] = consts[k]
            elif k == "x":
                m[k] = np.ascontiguousarray(inputs["x"][b])
            elif k == "p":
                m[k] = np.ascontiguousarray(inputs["p"][:, b])
            elif k == "positions":
                m[k] = np.ascontiguousarray(inputs["positions"][b]).astype(np.int32)
            else:
                m[k] = np.asarray(inputs[k])
        in_maps.append(m)
    res = run_bass_kernel_spmd(nc, in_maps, core_ids=list(range(B)))
    return np.stack([res.results[b]["out"] for b in range(B)], axis=0).astype(np.float32)
```
